# Optimizing a Trainium2 kernel written in Bass

```python
import math
import jax, jax.numpy as jnp
from jax import lax
import numpy as np

D_MODEL = 1024
BATCH = 2
SEQ = 8192
DEPTH = 1

CHUNK = 64
N_META = 16
N_HEADS = 8
HEAD_DIM = 64
V_HEAD_DIM = 2 * HEAD_DIM
ATT_QK = N_HEADS * 2 * HEAD_DIM
ATT_V = N_HEADS * V_HEAD_DIM
CONV_CH = D_MODEL
CONV_K = 31
D_FF = 2816
FFN_CONV_K = 3
QB = 128
NEG_INF = -1e30
IN_COLS = 2 * ATT_QK + ATT_V + 2 * CONV_CH + 2 * D_MODEL

kernel_name = "hybrid_diffattn_conformer_conv_stream_block"


def _rms(x, w, eps=1e-6):
    xf = x.astype(jnp.float32)
    y = xf * lax.rsqrt(jnp.mean(xf * xf, axis=-1, keepdims=True) + eps)
    return (y * w.astype(jnp.float32)).astype(x.dtype)


def _layernorm(x, w, b, eps=1e-5):
    xf = x.astype(jnp.float32)
    mu = jnp.mean(xf, axis=-1, keepdims=True)
    xc = xf - mu
    y = xc * lax.rsqrt(jnp.mean(xc * xc, axis=-1, keepdims=True) + eps)
    return (y * w.astype(jnp.float32) + b.astype(jnp.float32)).astype(x.dtype)


def _causal_dwconv(x, w, b):
    k, c = w.shape
    y = lax.conv_general_dilated(
        x, w[:, None, :].astype(x.dtype), window_strides=(1,), padding=[(k - 1, 0)],
        dimension_numbers=('NWC', 'WIO', 'NWC'), feature_group_count=c)
    return y + b.astype(x.dtype)


def _diff_attention(q, k, v, lam, cid):
    b, lp = q.shape[0], q.shape[1]
    nblk = lp // QB
    qf = q.astype(jnp.float32) * (HEAD_DIM ** -0.5)
    kf = k.astype(jnp.float32)
    vf = v.astype(jnp.float32)

    def block(i):
        start = i * QB
        qb = lax.dynamic_slice_in_dim(qf, start, QB, axis=1)
        qc = lax.dynamic_slice_in_dim(cid, start, QB, axis=0)
        s = jnp.einsum('bqhmd,bkhmd->bhmqk', qb, kf)
        mask = cid[None, :] <= qc[:, None]
        p = jax.nn.softmax(jnp.where(mask, s, NEG_INF), axis=-1)
        a = p[:, :, 0] - lam * p[:, :, 1]
        return jnp.einsum('bhqk,bkhe->bqhe', a, vf)

    o = lax.map(block, jnp.arange(nblk))
    return jnp.moveaxis(o, 0, 1).reshape(b, lp, N_HEADS, V_HEAD_DIM)


def setup_inputs(seed: int = 0) -> dict:
    key = jax.random.key(seed)
    ks = jax.random.split(key, 24)
    f32 = jnp.float32
    nrm = lambda k, shape, scale: (jax.random.normal(k, shape, f32) * scale)
    gain = lambda k, shape: 1.0 + 0.05 * jax.random.normal(k, shape, f32)
    L = DEPTH
    return {
        "x": nrm(ks[0], (BATCH, SEQ, D_MODEL), 1.0),
        "meta_tokens": nrm(ks[1], (N_META, D_MODEL), 0.5),
        "norm_mix_w": gain(ks[2], (L, D_MODEL)),
        "w_in": nrm(ks[3], (L, D_MODEL, IN_COLS), D_MODEL ** -0.5),
        "lambda_q1": nrm(ks[4], (L, HEAD_DIM), 0.1),
        "lambda_k1": nrm(ks[5], (L, HEAD_DIM), 0.1),
        "lambda_q2": nrm(ks[6], (L, HEAD_DIM), 0.1),
        "lambda_k2": nrm(ks[7], (L, HEAD_DIM), 0.1),
        "subln_w": gain(ks[8], (L, V_HEAD_DIM)),
        "conv_dw_w": nrm(ks[9], (L, CONV_K, CONV_CH), CONV_K ** -0.5),
        "conv_dw_b": nrm(ks[10], (L, CONV_CH), 0.02),
        "conv_ln_w": gain(ks[11], (L, CONV_CH)),
        "conv_ln_b": nrm(ks[12], (L, CONV_CH), 0.02),
        "w_conv_out": nrm(ks[13], (L, CONV_CH, D_MODEL), CONV_CH ** -0.5),
        "w_out": nrm(ks[14], (L, D_MODEL, D_MODEL), D_MODEL ** -0.5),
        "norm_ffn_w": gain(ks[15], (L, D_MODEL)),
        "w_up": nrm(ks[16], (L, D_MODEL, 2 * D_FF), D_MODEL ** -0.5),
        "ffn_dw_w": nrm(ks[17], (L, FFN_CONV_K, 2 * D_FF), FFN_CONV_K ** -0.5),
        "ffn_dw_b": nrm(ks[18], (L, 2 * D_FF), 0.02),
        "w_down": nrm(ks[19], (L, D_FF, D_MODEL), D_FF ** -0.5),
        "norm_final_w": gain(ks[20], (D_MODEL,)),
    }


def reference(x, meta_tokens, norm_mix_w, w_in, lambda_q1, lambda_k1, lambda_q2, lambda_k2,
              subln_w, conv_dw_w, conv_dw_b, conv_ln_w, conv_ln_b, w_conv_out, w_out,
              norm_ffn_w, w_up, ffn_dw_w, ffn_dw_b, w_down, norm_final_w):
    b, s, d = x.shape
    total = s + N_META
    lp = ((total + QB - 1) // QB) * QB
    meta = jnp.broadcast_to(meta_tokens[None].astype(x.dtype), (b, N_META, d))
    h = jnp.concatenate([meta, x], axis=1)
    h = jnp.pad(h, ((0, 0), (0, lp - total), (0, 0)))
    pos = jnp.arange(lp)
    cid = jnp.where(pos < N_META, 0, (pos - N_META) // CHUNK + 1)

    splits = [ATT_QK, 2 * ATT_QK, 2 * ATT_QK + ATT_V,
              2 * ATT_QK + ATT_V + 2 * CONV_CH, 2 * ATT_QK + ATT_V + 2 * CONV_CH + D_MODEL]
    for l in range(DEPTH):
        lambda_init = 0.8 - 0.6 * math.exp(-0.3 * l)
        n = _rms(h, norm_mix_w[l])
        z = n @ w_in[l]
        q, k, v, glu, g_att, g_conv = jnp.split(z, splits, axis=-1)
        lam = (jnp.exp(jnp.sum(lambda_q1[l].astype(jnp.float32) * lambda_k1[l].astype(jnp.float32)))
               - jnp.exp(jnp.sum(lambda_q2[l].astype(jnp.float32) * lambda_k2[l].astype(jnp.float32)))
               + lambda_init)
        q = q.reshape(b, lp, N_HEADS, 2, HEAD_DIM)
        k = k.reshape(b, lp, N_HEADS, 2, HEAD_DIM)
        v = v.reshape(b, lp, N_HEADS, V_HEAD_DIM)
        o = _diff_attention(q, k, v, lam, cid)
        y_att = (_rms(o, subln_w[l]) * (1.0 - lambda_init)).reshape(b, lp, ATT_V).astype(h.dtype)
        u = glu[..., :CONV_CH] * jax.nn.sigmoid(glu[..., CONV_CH:])
        u = _causal_dwconv(u, conv_dw_w[l], conv_dw_b[l])
        u = jax.nn.silu(_layernorm(u, conv_ln_w[l], conv_ln_b[l]))
        y_conv = u @ w_conv_out[l]
        m = jax.nn.sigmoid(g_att) * y_att + jax.nn.sigmoid(g_conv) * y_conv
        h = h + m @ w_out[l]
        n2 = _rms(h, norm_ffn_w[l])
        up = _causal_dwconv(n2 @ w_up[l], ffn_dw_w[l], ffn_dw_b[l])
        gate, val = jnp.split(up, 2, axis=-1)
        h = h + (jax.nn.silu(gate) * val) @ w_down[l]

    out = _rms(h, norm_final_w)
    return out[:, N_META:N_META + s]
```

```python
import numpy as np
import ml_dtypes
import concourse.bass as bass
import concourse.mybir as mybir
from concourse.bass_utils import run_bass_kernel_spmd

F32 = mybir.dt.float32
BF16 = mybir.dt.bfloat16
ALU = mybir.AluOpType
AF = mybir.ActivationFunctionType
AX = mybir.AxisListType

D = 1024
KC = 8
NT_ALL = 65
NKEY = NT_ALL * 128
GQ = [512, 512, 512, 512, 34]
GOFF = [0, 512, 1024, 1536, 2048]
NQ = 2082
GROW = [0, 544, 1088, 1632, 2176]
NROWS = 2242
NOUT = 2072
AR = 106400
DEBUG = False


class T:
    __slots__ = ("name", "w", "r", "dkey")

    def __init__(self, name):
        self.name = name
        self.w = {}
        self.r = {}
        self.dkey = None


class Sched:
    CE = ("pe", "act", "dve", "pool")

    def __init__(self, nc):
        self.nc = nc
        self.q = {e: [] for e in ("pe", "act", "dve", "pool", "sp")}
        self.cnt = {}
        self.known = {e: {} for e in self.q}
        self.sems = {}
        self.ntile = 0
        for e in self.CE:
            self.cnt[e] = 0
            self.sems[e] = nc.alloc_semaphore(name="s_" + e)

    def tile(self, name=None):
        self.ntile += 1
        return T((name or "t") + "_%d" % self.ntile)

    def tiles(self, n, name=None):
        return [self.tile(name) for _ in range(n)]

    def _needs(self, e, reads, writes):
        needs = {}
        for t in reads:
            for k, v in t.w.items():
                if needs.get(k, 0) < v:
                    needs[k] = v
        for t in writes:
            for d in (t.w, t.r):
                for k, v in d.items():
                    if needs.get(k, 0) < v:
                        needs[k] = v
        kn = self.known[e]
        for k, v in needs.items():
            if k == "pe" and e == "pe":
                continue
            if kn.get(k, 0) < v:
                self.q[e].append(("w", k, v))
                kn[k] = v

    def _mark(self, k, v, reads, writes, wadd):
        for t in reads:
            if t.r.get(k, 0) < v:
                t.r[k] = v
        for t in writes:
            if wadd:
                t.w[k] = v
            else:
                t.w = {k: v}
                t.r = {}

    def op(self, e, fn, reads=(), writes=(), wadd=False):
        self._needs(e, reads, writes)
        self.cnt[e] += 1
        v = self.cnt[e]
        self.q[e].append(("o", fn, e, 1))
        self._mark(e, v, reads, writes, wadd)

    def dma(self, semt, out_ap, in_ap, reads=(), writes=(), wadd=False, q="sp"):
        if semt.dkey is None:
            semt.dkey = "d_" + semt.name
            self.cnt[semt.dkey] = 0
            self.sems[semt.dkey] = self.nc.alloc_semaphore(name=semt.dkey)
        k = semt.dkey
        self._needs(q, reads, writes)
        self.cnt[k] += 16
        v = self.cnt[k]
        self.q[q].append(("o", lambda eng: eng.dma_start(out=out_ap, in_=in_ap), k, 16))
        self._mark(k, v, reads, writes, wadd)

    def ccop(self, semt, fn, reads=(), writes=()):
        if semt.dkey is None:
            semt.dkey = "c_" + semt.name
            self.cnt[semt.dkey] = 0
            self.sems[semt.dkey] = self.nc.alloc_semaphore(name=semt.dkey)
        k = semt.dkey
        self._needs("pool", reads, writes)
        self.cnt[k] += 1
        v = self.cnt[k]
        self.q["pool"].append(("o", fn, k, 1))
        self._mark(k, v, reads, writes, False)

    def barrier(self, final=False):
        for e in self.q:
            kn = self.known[e]
            for k, v in self.cnt.items():
                if k == e or v == 0 or (k.startswith("c_") and not final):
                    continue
                if kn.get(k, 0) < v:
                    self.q[e].append(("w", k, v))
                    kn[k] = v

    def emit(self):
        nc = self.nc
        self.stats = {e: (len(v), sum(1 for it in v if it[0]=='w')) for e, v in self.q.items()}

        def run(e, eng):
            for it in self.q[e]:
                if it[0] == "w":
                    eng.wait_ge(self.sems[it[1]], it[2])
                else:
                    it[1](eng).then_inc(self.sems[it[2]], it[3])

        with nc.Block() as block:
            @block.tensor
            def _(eng):
                run("pe", eng)

            @block.scalar
            def _(eng):
                run("act", eng)

            @block.vector
            def _(eng):
                run("dve", eng)

            @block.gpsimd
            def _(eng):
                run("pool", eng)

            @block.sync
            def _(eng):
                run("sp", eng)


class Rot:
    def __init__(self, items):
        self.items = items
        self.i = 0

    def next(self):
        it = self.items[self.i % len(self.items)]
        self.i += 1
        return it


PC_NMIX, PC_NFFN, PC_CDW, PC_CDB, PC_LNW, PC_LNB, PC_FDW, PC_FDB, PC_SUB, NPC = 0, 8, 16, 264, 272, 280, 288, 420, 464, 466
PR_NF, PR_SUB, PR_LQ1, PR_LK1, PR_LQ2, PR_LK2, PR_CK, NPR = 0, 1024, 1152, 1216, 1280, 1344, 1408, 1472


def build_program():
    nc = bass.Bass("TRN2", target_bir_lowering=False)

    def din(name, shape, dt=F32):
        return nc.dram_tensor(name, shape, dt, kind="ExternalInput").ap()

    x_all = din("x_all", [17 * 128, D])
    x_own = din("x_own", [NROWS, D])
    w_in = din("w_in", [D, 7168])
    w_co = din("w_co", [D, D])
    w_o = din("w_o", [D, D])
    w_up = din("w_up", [D, 5632])
    w_dn = din("w_dn", [2816, D])
    pcol_d = din("pcol", [128, NPC])
    prow_d = din("prow", [128, NPR])
    cq_d = din("cq", [128, 2048], BF16)
    ident_d = din("ident", [128, 128], BF16)
    out_d = nc.dram_tensor("out_own", [NOUT, D], F32, kind="ExternalOutput").ap()
    kt_part = [nc.dram_tensor("kt_part%d" % c, [256, 2048], BF16).ap() for c in range(4)]
    kt_all = [nc.dram_tensor("kt_all%d" % c, [1024, 2048], BF16).ap() for c in range(4)]
    kt_meta = nc.dram_tensor("kt_meta", [8, 128, 128], BF16).ap()
    v_part = [nc.dram_tensor("v_part%d" % c, [256, 8 * 129], BF16).ap() for c in range(8)]
    v_all = [nc.dram_tensor("v_all%d" % c, [1024, 8 * 129], BF16).ap() for c in range(8)]
    v_meta = nc.dram_tensor("v_meta", [128, 8 * 129], BF16).ap()
    NPE_G = 20
    dg_scr = nc.dram_tensor("dg_scr", [8, 128, NPE_G * 128], BF16).ap()
    if DEBUG:
        dbg_mc1 = nc.dram_tensor("dbg_mc1", [128, 8 * NQ], BF16, kind="ExternalOutput").ap()
        dbg_mc2 = nc.dram_tensor("dbg_mc2", [128, 8 * NQ], BF16, kind="ExternalOutput").ap()
        dbg_qt = nc.dram_tensor("dbg_qt", [128, 8 * NQ], BF16, kind="ExternalOutput").ap()
        dbg_gate = nc.dram_tensor("dbg_gate", [128, 8 * NQ], BF16, kind="ExternalOutput").ap()
        dbg_acc = nc.dram_tensor("dbg_acc", [128, 17 * 1024], F32, kind="ExternalOutput").ap()
        dbg_n2t = nc.dram_tensor("dbg_n2t", [128, 8 * NQ], BF16, kind="ExternalOutput").ap()
        dbg_small = nc.dram_tensor("dbg_small", [128, 64], F32, kind="ExternalOutput").ap()

    S = Sched(nc)
    from contextlib import ExitStack
    with ExitStack() as es:
        arena = es.enter_context(nc.sbuf_tensor("arena", [128, AR], BF16))
        PS = [es.enter_context(nc.psum_tensor("ps%d" % i, [128, 512], F32)) for i in range(8)]
        PST = S.tiles(8, "ps")

        def bf(off, n):
            return arena[:, off:off + n]

        def f32(off, n):
            return arena[:, off:off + 2 * n].bitcast(F32)

        o = 0
        ident = bf(o, 128); o += 128
        ones = bf(o, 128); o += 128
        pcol = f32(o, NPC); o += 2 * NPC
        prow = f32(o, NPR); o += 2 * NPR
        small = f32(o, 64); o += 128
        STG = [(f32(o, 1024), S.tile("stg")), (f32(o + 2048, 1024), S.tile("stg"))]; o += 4096
        assert o <= 9216, o
        MOFF = 9216
        ROFF = MOFF + 17408
        t_const = S.tile("const")
        t_small = S.tile("small")
        stg = Rot(STG)
        mT = bf(MOFF, 8 * NQ).rearrange("p (k t) -> p k t", t=NQ)
        t_mT = S.tiles(8, "mT")

        S.dma(t_const, ident, ident_d, writes=[t_const], wadd=True)
        S.dma(t_const, pcol, pcol_d, writes=[t_const], wadd=True)
        S.dma(t_const, prow, prow_d, writes=[t_const], wadd=True)
        S.op("pool", lambda e: e.memset(ones, 1.0), writes=[t_const], wadd=True)
        tmpl = f32(ROFF, 64)
        t_tmpl = S.tile()
        for i, (a, b) in enumerate(((PR_LQ1, PR_LK1), (PR_LQ2, PR_LK2))):
            S.op("dve", lambda e, a=a, b=b: e.tensor_tensor(out=tmpl, in0=prow[:, a:a + 64], in1=prow[:, b:b + 64], op=ALU.mult),
                 reads=[t_const], writes=[t_tmpl])
            S.op("dve", lambda e, i=i: e.reduce_sum(out=small[:, i:i + 1], in_=tmpl, axis=AX.X), reads=[t_tmpl], writes=[t_small], wadd=True)
        S.op("act", lambda e: e.activation(out=small[:, 2:4], in_=small[:, 0:2], func=AF.Exp), reads=[t_small], writes=[t_small], wadd=True)
        S.op("dve", lambda e: e.tensor_tensor(out=small[:, 4:5], in0=small[:, 3:4], in1=small[:, 2:3], op=ALU.subtract), reads=[t_small], writes=[t_small], wadd=True)
        S.op("dve", lambda e: e.tensor_scalar(out=small[:, 4:5], in0=small[:, 4:5], scalar1=-0.2, scalar2=None, op0=ALU.add), reads=[t_small], writes=[t_small], wadd=True)
        nlam = small[:, 4:5]

        lw_i = [0]

        def load_w(dst, t_dst, src, ncols, scale):
            for c0 in range(0, ncols, 1024):
                n = min(1024, ncols - c0)
                st, t_st = stg.next()
                S.dma(t_st, st[:, :n], src[:, c0:c0 + n], writes=[t_st])
                lw_i[0] += 1
                if lw_i[0] % 2 == 0:
                    if scale is None:
                        S.op("dve", lambda e, st=st, n=n, c0=c0: e.tensor_copy(out=dst[:, c0:c0 + n], in_=st[:, :n]),
                             reads=[t_st], writes=[t_dst], wadd=True)
                    else:
                        S.op("dve", lambda e, st=st, n=n, c0=c0: e.tensor_scalar(out=dst[:, c0:c0 + n], in0=st[:, :n], scalar1=scale, scalar2=None, op0=ALU.mult),
                             reads=[t_st, t_const], writes=[t_dst], wadd=True)
                else:
                    if scale is None:
                        S.op("act", lambda e, st=st, n=n, c0=c0: e.copy(out=dst[:, c0:c0 + n], in_=st[:, :n]),
                             reads=[t_st], writes=[t_dst], wadd=True)
                    else:
                        S.op("act", lambda e, st=st, n=n, c0=c0: e.activation(out=dst[:, c0:c0 + n], in_=st[:, :n], func=AF.Identity, scale=scale),
                             reads=[t_st, t_const], writes=[t_dst], wadd=True)

        psrot = [0]
        pslist = [list(range(8))]

        def nextps():
            l = pslist[0]
            i = l[psrot[0] % len(l)]
            psrot[0] += 1
            return PS[i], PST[i]

        def norm_rows(xt, t_x, n, sqj, t_sq, ss, t_ss, xn, t_xn, eps_scaled=1024 * 1e-6, mul=32.0):
            S.op("act", lambda e: e.activation(out=sqj[:n, :], in_=xt[:n, :], func=AF.Square, accum_out=ss[:n, :]),
                 reads=[t_x], writes=[t_sq, t_ss])
            S.op("act", lambda e: e.activation(out=ss[:n, :], in_=ss[:n, :], func=AF.Sqrt, bias=eps_scaled, scale=1.0), reads=[t_ss], writes=[t_ss])
            S.op("dve", lambda e: e.reciprocal(out=ss[:n, :], in_=ss[:n, :]), reads=[t_ss], writes=[t_ss])
            S.op("dve", lambda e: e.tensor_scalar(out=xn[:n, :], in0=xt[:n, :], scalar1=ss[:n, 0:1], scalar2=mul, op0=ALU.mult, op1=ALU.mult),
                 reads=[t_x, t_ss], writes=[t_xn])

        def transpose_rows(src, t_src, n, dst3, t_dst, coff, evac="act"):
            ps, t_ps = nextps()
            psb = ps[:].bitcast(BF16)
            for kc in range(KC):
                S.op("pe", lambda e, kc=kc: e.transpose(out=psb[:, kc * 128:kc * 128 + n], in_=src[:n, kc * 128:(kc + 1) * 128], identity=ident[:n, :n]),
                     reads=[t_src, t_const], writes=[t_ps], wadd=(kc > 0))
            pv = psb.rearrange("p (k t) -> p k t", t=128)[:, :, :n]
            if evac == "act":
                S.op("act", lambda e: e.copy(out=dst3[:, :, coff:coff + n], in_=pv), reads=[t_ps], writes=[t_dst], wadd=True)
            else:
                S.op("dve", lambda e: e.tensor_copy(out=dst3[:, :, coff:coff + n], in_=pv), reads=[t_ps], writes=[t_dst], wadd=True)

        def mm_acc(ps_ap, t_ps, pairs, reads):
            n = len(pairs)
            for i, (l, r) in enumerate(pairs):
                S.op("pe", lambda e, l=l, r=r, i=i: e.matmul(ps_ap, lhsT=l, rhs=r, start=(i == 0), stop=(i == n - 1)),
                     reads=reads, writes=[t_ps], wadd=(i > 0))

        Wkv = bf(MOFF, 8 * 2048).rearrange("p (k c) -> p k c", c=2048)
        t_Wkv = S.tile("Wkv")
        PW = ROFF + 8704
        Wglu = bf(PW, 16384).rearrange("p (k c) -> p k c", c=2048)
        Wgc = bf(PW + 16384, 8192).rearrange("p (k c) -> p k c", c=1024)
        Wco = bf(PW + 24576, 8192).rearrange("p (k c) -> p k c", c=1024)
        t_Wglu, t_Wgc, t_Wco = S.tile("Wglu"), S.tile("Wgc"), S.tile("Wco")
        o = PW + 32768
        XB = []
        for i in range(4):
            XB.append((f32(o, 1024), S.tile("xb"))); o += 2048
        sqj = bf(o, 1024); o += 1024
        t_sq = S.tile("sq")
        SSB = []
        for i in range(8):
            SSB.append((f32(o, 1), S.tile("ss"))); o += 2
        XN = []
        for i in range(3):
            XN.append((bf(o, 1024), S.tile("xn"))); o += 1024
        XT = []
        for i in range(2):
            XT.append((bf(o, 4096).rearrange("p (k t) -> p k t", t=512), S.tile("xT"))); o += 4096
        KST = []
        for i in range(2):
            KST.append((bf(o, 4096).rearrange("p (h t) -> p h t", t=512), S.tile("kst"))); o += 4096
        VST = []
        for i in range(4):
            VST.append((bf(o, 1032).rearrange("p (h t) -> p h t", t=129), S.tile("vst"))); o += 1032
        DGT = []
        for i in range(1):
            DGT.append((bf(o, NPE_G * 128).rearrange("p (k c) -> p k c", c=128), S.tile("dgt"))); o += NPE_G * 128
        stg_k = (f32(o, 1024), S.tile("stg")); o += 2048
        assert o <= AR, o
        stg.items = STG + [stg_k]
        xb, ssb, xnb, xtb, kstb, vstb = Rot(XB), Rot(SSB), Rot(XN), Rot(XT), Rot(KST), Rot(VST)
        dgtb = Rot(DGT)
        t_dgscr = S.tile("dgscr")

        def build_diag(oc):
            dgt, t_dgt = dgtb.next()
            wb = PC_CDW + oc * 31
            for k in range(NPE_G):
                S.op("pool", lambda e, k=k: e.tensor_scalar(out=dgt[:, k, :], in0=ident, scalar1=pcol[:, wb + k:wb + k + 1], scalar2=None, op0=ALU.mult),
                     reads=[t_const], writes=[t_dgt], wadd=(k > 0))
            S.dma(t_dgt, dg_scr[oc], dgt.rearrange("p k c -> p (k c)"), reads=[t_dgt], writes=[t_dgscr], wadd=True, q="pool")
        t_ktscr = S.tile("ktscr")
        t_vscr = S.tile("vscr")
        t_Wv = S.tile("Wv")
        for kc in range(KC):
            load_w(Wkv[:, kc, 1024:2048], t_Wv, w_in[kc * 128:(kc + 1) * 128, 2048:3072], 1024, pcol[:, PC_NMIX + kc:PC_NMIX + kc + 1])
        for kc in range(KC):
            load_w(Wkv[:, kc, 0:1024], t_Wkv, w_in[kc * 128:(kc + 1) * 128, 1024:2048], 1024, pcol[:, PC_NMIX + kc:PC_NMIX + kc + 1])
        for vi in range(4):
            S.op("pool", lambda e, vi=vi: e.memset(VST[vi][0][:, :, 128:129], 1.0), writes=[VST[vi][1]])
        blocks = [(0, 1)] + [(1 + 4 * i, 4) for i in range(4)]
        def kv_prep(t0, ntl):
            xT, t_xT = xtb.next()
            for tl in range(ntl):
                row0 = (t0 + tl) * 128
                xt, t_x = xb.next()
                S.dma(t_x, xt, x_all[row0:row0 + 128, :], writes=[t_x])
                ss, t_ss = ssb.next()
                xn, t_xn = xnb.next()
                norm_rows(xt, t_x, 128, sqj, t_sq, ss, t_ss, xn, t_xn)
                transpose_rows(xn, t_xn, 128, xT, t_xT, tl * 128, evac="act")
            return xT, t_xT

        def kv_compute(t0, ntl, xT, t_xT):
            for tl in range(ntl):
                vst, t_vst = vstb.next()
                for half in range(2):
                    ps, t_ps = nextps()
                    mm_acc(ps[:, :], t_ps, [(xT[:, kc, tl * 128:(tl + 1) * 128], Wkv[:, kc, 1024 + half * 512:1024 + (half + 1) * 512]) for kc in range(KC)], [t_xT, t_Wv])
                    S.op("dve" if half == 0 else "act",
                         (lambda e, ps=ps, vst=vst, half=half: e.tensor_copy(out=vst[:, 4 * half:4 * half + 4, 0:128], in_=ps[:, :].rearrange("p (h d) -> p h d", d=128))) if half == 0 else
                         (lambda e, ps=ps, vst=vst, half=half: e.copy(out=vst[:, 4 * half:4 * half + 4, 0:128], in_=ps[:, :].rearrange("p (h d) -> p h d", d=128))),
                         reads=[t_ps], writes=[t_vst], wadd=True)
                tq = t0 + tl - 1
                vdst = v_meta if t0 == 0 else v_part[tq // 2][(tq % 2) * 128:(tq % 2 + 1) * 128, :]
                S.dma(t_vst, vdst, vst.rearrange("p h t -> p (h t)"), reads=[t_vst], writes=[t_vscr], wadd=True)
            nt = ntl * 128
            kst, t_kst = kstb.next()
            for h in range(8):
                ps, t_ps = nextps()
                mm_acc(ps[:, :nt], t_ps, [(Wkv[:, kc, h * 128:(h + 1) * 128], xT[:, kc, :nt]) for kc in range(KC)], [t_xT, t_Wkv])
                if h % 2 == 0:
                    S.op("dve", lambda e, ps=ps, h=h, kst=kst: e.tensor_copy(out=kst[:, h, :nt], in_=ps[:, :nt]), reads=[t_ps], writes=[t_kst], wadd=True)
                else:
                    S.op("act", lambda e, ps=ps, h=h, kst=kst: e.copy(out=kst[:, h, :nt], in_=ps[:, :nt]), reads=[t_ps], writes=[t_kst], wadd=True)
            if t0 == 0:
                S.dma(t_kst, kt_meta.rearrange("h p t -> p h t"), kst[:, :, :nt], reads=[t_kst], writes=[t_ktscr], wadd=True)
            else:
                for c in range(4):
                    kdst = kt_part[c].rearrange("(h p) t -> p h t", p=128)[:, :, (t0 - 1) * 128:(t0 - 1) * 128 + nt]
                    S.dma(t_kst, kdst, kst[:, 2 * c:2 * c + 2, :nt], reads=[t_kst], writes=[t_ktscr], wadd=True)

        pre = []
        for kc in range(KC):
            nm = pcol[:, PC_NMIX + kc:PC_NMIX + kc + 1]
            pre.append((Wglu[:, kc, :], t_Wglu, w_in[kc * 128:(kc + 1) * 128, 3072:5120], 2048, nm))
        for kc in range(KC):
            nm = pcol[:, PC_NMIX + kc:PC_NMIX + kc + 1]
            pre.append((Wgc[:, kc, :], t_Wgc, w_in[kc * 128:(kc + 1) * 128, 6144:7168], 1024, nm))
        for kc in range(KC):
            pre.append((Wco[:, kc, :], t_Wco, w_co[kc * 128:(kc + 1) * 128, :], 1024, None))
        dg_todo = list(range(8))
        nxt = kv_prep(*blocks[0])
        for bi_, (t0, ntl) in enumerate(blocks):
            cur = nxt
            if bi_ + 1 < len(blocks):
                nxt = kv_prep(*blocks[bi_ + 1])
            for _ in range(5):
                if pre:
                    load_w(*pre.pop(0))
            for _ in range(2):
                if dg_todo:
                    build_diag(dg_todo.pop(0))
            kv_compute(t0, ntl, *cur)
        while pre:
            load_w(*pre.pop(0))
        while dg_todo:
            build_diag(dg_todo.pop(0))
        XNT_pre = (bf(ROFF, 8 * 544).rearrange("p (k t) -> p k t", t=544), S.tile("XNT"))
        r_ = 0
        while r_ < 32 + GQ[0]:
            n_ = min(128, 32 + GQ[0] - r_)
            xt_, t_x_ = xb.next()
            S.dma(t_x_, xt_[:n_, :], x_own[GROW[0] + r_:GROW[0] + r_ + n_, :], writes=[t_x_])
            ss_, t_ss_ = ssb.next()
            xn_, t_xn_ = xnb.next()
            norm_rows(xt_, t_x_, n_, sqj, t_sq, ss_, t_ss_, xn_, t_xn_)
            transpose_rows(xn_, t_xn_, n_, XNT_pre[0], XNT_pre[1], r_, evac="act")
            r_ += n_
        t_ktall, t_vall, t_cc = S.tile("ktall"), S.tile("vall"), S.tile("cc")
        RG = [[0, 1, 2, 3], [4, 5, 6, 7]]
        t_ktall = S.tiles(4, "ktall")
        t_vall = S.tiles(8, "vall")
        for c in range(4):
            S.ccop(t_cc, lambda e, c=c: e.collective_compute("AllGather", ALU.bypass, replica_groups=RG, ins=[kt_part[c]], outs=[kt_all[c]]), reads=[t_ktscr], writes=[t_ktall[c]])
            for c2 in (2 * c, 2 * c + 1):
                S.ccop(t_cc, lambda e, c2=c2: e.collective_compute("AllGather", ALU.bypass, replica_groups=RG, ins=[v_part[c2]], outs=[v_all[c2]]), reads=[t_vscr], writes=[t_vall[c2]])
        S.barrier()

        xnt_pending = []

        def own_xnt_parts(g, xntb, xb, ssb, xnb, sqj, t_sq):
            XNT, t_XNT = xntb.next()
            nrows = 32 + GQ[g]
            r = 0
            tiles = []
            while r < nrows:
                tiles.append((r, min(128, nrows - r)))
                r += tiles[-1][1]
            held = {}

            def part_a(r, n):
                xt, t_x = xb.next()
                S.dma(t_x, xt[:n, :], x_own[GROW[g] + r:GROW[g] + r + n, :], writes=[t_x])
                ss, t_ss = ssb.next()
                xn, t_xn = xnb.next()
                norm_rows(xt, t_x, n, sqj, t_sq, ss, t_ss, xn, t_xn)
                held[r] = (xn, t_xn)

            def part_b(r, n):
                xn, t_xn = held.pop(r)
                transpose_rows(xn, t_xn, n, XNT, t_XNT, r, evac="act")

            order = []
            for i, (r, n) in enumerate(tiles):
                order.append(lambda r=r, n=n: part_a(r, n))
                if i >= 1:
                    pr, pn = tiles[i - 1]
                    order.append(lambda pr=pr, pn=pn: part_b(pr, pn))
            lr, ln_ = tiles[-1]
            order.append(lambda: part_b(lr, ln_))
            xnt_pending.extend(order)
            return XNT, t_XNT

        def xnt_drain(k=99):
            for _ in range(k):
                if xnt_pending:
                    xnt_pending.pop(0)()

        def own_xnt(g, xntb, xb, ssb, xnb, sqj, t_sq):
            res = own_xnt_parts(g, xntb, xb, ssb, xnb, sqj, t_sq)
            xnt_drain()
            return res

        stg.items = STG
        o = ROFF
        XNTB = [XNT_pre]
        o += 4352
        for i in range(1):
            XNTB.append((bf(o, 8 * 544).rearrange("p (k t) -> p k t", t=544), S.tile("XNT"))); o += 4352
        assert o == PW
        o += 32768
        XB = []
        for i in range(2):
            XB.append((f32(o, 1024), S.tile("xb"))); o += 2048
        sqj = bf(o, 1024); o += 1024
        t_sq = S.tile("sq")
        SSB = []
        for i in range(4):
            SSB.append((f32(o, 1), S.tile("ss"))); o += 2
        XN = []
        for i in range(2):
            XN.append((bf(o, 1024), S.tile("xn"))); o += 1024
        UB = []
        for i in range(3):
            UB.append((bf(o, 544), S.tile("u"))); o += 544
        DGB = []
        for i in range(2):
            DGB.append((bf(o, 20 * 128).rearrange("p (k c) -> p k c", c=128), S.tile("dg"))); o += 2560
        SG = []
        for i in range(2):
            SG.append((f32(o, 544), S.tile("sg"))); o += 1088
        CA = []
        for i in range(2):
            CA.append((f32(o, 512), S.tile("ca"))); o += 1024
        cb = bf(o, 4096).rearrange("p (k t) -> p k t", t=512); o += 4096
        t_cb = S.tiles(8, "cb")
        CSQ = []
        for i in range(2):
            CSQ.append((bf(o, 512), S.tile("csq"))); o += 512
        mean = f32(o, 512); o += 1024
        rstd = f32(o, 512); o += 1024
        msq = f32(o, 512); o += 1024
        t_mean, t_rstd, t_msq = S.tile("mean"), S.tile("rstd"), S.tile("msq")
        XH = []
        for i in range(2):
            XH.append((f32(o, 512), S.tile("xh"))); o += 1024
        uu = bf(o, 4096).rearrange("p (k t) -> p k t", t=512); o += 4096
        t_uu = S.tiles(8, "uu")
        GCB = []
        gcs = bf(o, 4096).rearrange("p (k t) -> p k t", t=512); o += 4096
        t_gcs = S.tiles(8, "gcs")
        assert o <= AR, o
        xntb, xb, ssb, xnb, ub, sgb, cab, csqb, xhb, gcb, dgb = Rot(XNTB), Rot(XB), Rot(SSB), Rot(XN), Rot(UB), Rot(SG), Rot(CA), Rot(CSQ), Rot(XH), Rot(GCB), Rot(DGB)

        pslist[0] = list(range(6))

        def conv_group(g, XNT, t_XNT):
            nq = GQ[g]
            nr = 32 + nq
            psum1, t_psum1 = PS[6], PST[6]
            psum2, t_psum2 = PS[7], PST[7]
            NPE = 20

            def stage1(oc):
                u, t_u = ub.next()
                sg, t_sg = sgb.next()
                dg, t_dg = dgb.next()
                wb = PC_CDW + oc * 31
                S.dma(t_dg, dg.rearrange("p k c -> p (k c)"), dg_scr[oc], reads=[t_dgscr], writes=[t_dg])
                for (c0, cn) in ((0, 32), (32, nq)):
                    psa, t_psa = nextps()
                    mm_acc(psa[:, :cn], t_psa, [(Wglu[:, kc, oc * 128:(oc + 1) * 128], XNT[:, kc, c0:c0 + cn]) for kc in range(KC)], [t_XNT, t_Wglu])
                    psg, t_psg = nextps()
                    mm_acc(psg[:, :cn], t_psg, [(Wglu[:, kc, 1024 + oc * 128:1024 + (oc + 1) * 128], XNT[:, kc, c0:c0 + cn]) for kc in range(KC)], [t_XNT, t_Wglu])
                    S.op("act", lambda e, psg=psg, sg=sg, c0=c0, cn=cn: e.activation(out=sg[:, c0:c0 + cn], in_=psg[:, :cn], func=AF.Sigmoid),
                         reads=[t_psg], writes=[t_sg], wadd=True)
                    S.op("dve", lambda e, psa=psa, sg=sg, u=u, c0=c0, cn=cn: e.tensor_tensor(out=u[:, c0:c0 + cn], in0=psa[:, :cn], in1=sg[:, c0:c0 + cn], op=ALU.mult),
                         reads=[t_psa, t_sg], writes=[t_u], wadd=True)
                return u, t_u, dg, t_dg

            def stage2(oc, u, t_u, dg, t_dg):
                wb = PC_CDW + oc * 31
                psc, t_psc = nextps()
                for k in range(NPE):
                    S.op("pe", lambda e, k=k: e.matmul(psc[:, :nq], lhsT=dg[:, k, :], rhs=u[:, 2 + k:2 + k + nq], start=(k == 0), stop=(k == NPE - 1)),
                         reads=[t_u, t_dg], writes=[t_psc], wadd=(k > 0))
                ca, t_ca = cab.next()
                S.op("dve", lambda e: e.tensor_scalar(out=ca[:, :nq], in0=u[:, 2 + NPE:2 + NPE + nq], scalar1=pcol[:, wb + NPE:wb + NPE + 1], scalar2=pcol[:, PC_CDB + oc:PC_CDB + oc + 1], op0=ALU.mult, op1=ALU.add),
                     reads=[t_u, t_const], writes=[t_ca])
                for k in range(NPE + 1, 31):
                    S.op("dve", lambda e, k=k: e.scalar_tensor_tensor(out=ca[:, :nq], in0=u[:, 2 + k:2 + k + nq], scalar=pcol[:, wb + k:wb + k + 1], in1=ca[:, :nq], op0=ALU.mult, op1=ALU.add),
                         reads=[t_u, t_ca, t_const], writes=[t_ca])
                S.op("dve", lambda e: e.tensor_tensor(out=ca[:, :nq], in0=psc[:, :nq], in1=ca[:, :nq], op=ALU.add), reads=[t_psc, t_ca], writes=[t_ca])
                S.op("act", lambda e: e.copy(out=cb[:, oc, :nq], in_=ca[:, :nq]), reads=[t_ca], writes=[t_cb[oc]])
                csq, t_csq = csqb.next()
                S.op("act", lambda e: e.activation(out=csq[:, :nq], in_=ca[:, :nq], func=AF.Square), reads=[t_ca], writes=[t_csq])
                def stats():
                    S.op("pe", lambda e: e.matmul(psum1[:, :nq], lhsT=ones, rhs=cb[:, oc, :nq], start=(oc == 0), stop=(oc == 7)),
                         reads=[t_cb[oc], t_const], writes=[t_psum1], wadd=(oc > 0))
                    S.op("pe", lambda e: e.matmul(psum2[:, :nq], lhsT=ones, rhs=csq[:, :nq], start=(oc == 0), stop=(oc == 7)),
                         reads=[t_csq, t_const], writes=[t_psum2], wadd=(oc > 0))
                return stats

            nx = stage1(0)
            pend_stats = None
            for oc in range(8):
                cu = nx
                if oc < 7:
                    nx = stage1(oc + 1)
                st = stage2(oc, *cu)
                if pend_stats is not None:
                    pend_stats()
                pend_stats = st
                if oc >= 1:
                    xnt_drain(2)
            pend_stats()
            for oc2 in range(8):
                psg, t_psg = nextps()
                mm_acc(psg[:, :nq], t_psg, [(Wgc[:, kc, oc2 * 128:(oc2 + 1) * 128], XNT[:, kc, 32:32 + nq]) for kc in range(KC)], [t_XNT, t_Wgc])
                S.op("act", lambda e, psg=psg, oc2=oc2: e.activation(out=gcs[:, oc2, :nq], in_=psg[:, :nq], func=AF.Sigmoid), reads=[t_psg], writes=[t_gcs[oc2]])
            S.op("act", lambda e: e.activation(out=mean[:, :nq], in_=psum1[:, :nq], func=AF.Identity, scale=1.0 / 1024), reads=[t_psum1], writes=[t_mean])
            S.op("dve", lambda e: e.tensor_tensor(out=msq[:, :nq], in0=mean[:, :nq], in1=mean[:, :nq], op=ALU.mult), reads=[t_mean], writes=[t_msq])
            S.op("dve", lambda e: e.scalar_tensor_tensor(out=rstd[:, :nq], in0=psum2[:, :nq], scalar=1.0 / 1024, in1=msq[:, :nq], op0=ALU.mult, op1=ALU.subtract),
                 reads=[t_psum2, t_msq], writes=[t_rstd])
            S.op("act", lambda e: e.activation(out=rstd[:, :nq], in_=rstd[:, :nq], func=AF.Ln, bias=1e-5, scale=1.0), reads=[t_rstd], writes=[t_rstd])
            S.op("act", lambda e: e.activation(out=rstd[:, :nq], in_=rstd[:, :nq], func=AF.Exp, scale=-0.5), reads=[t_rstd], writes=[t_rstd])
            for oc in range(8):
                xh, t_xh = xhb.next()
                S.op("dve", lambda e, xh=xh, oc=oc: e.tensor_tensor(out=xh[:, :nq], in0=cb[:, oc, :nq], in1=mean[:, :nq], op=ALU.subtract), reads=[t_cb[oc], t_mean], writes=[t_xh])
                S.op("dve", lambda e, xh=xh: e.tensor_tensor(out=xh[:, :nq], in0=xh[:, :nq], in1=rstd[:, :nq], op=ALU.mult), reads=[t_xh, t_rstd], writes=[t_xh])
                S.op("act", lambda e, xh=xh, oc=oc: e.activation(out=uu[:, oc, :nq], in_=xh[:, :nq], func=AF.Silu, scale=pcol[:, PC_LNW + oc:PC_LNW + oc + 1], bias=pcol[:, PC_LNB + oc:PC_LNB + oc + 1]),
                     reads=[t_xh, t_const], writes=[t_uu[oc]])
            for oc2 in range(8):
                psy, t_psy = nextps()
                mm_acc(psy[:, :nq], t_psy, [(Wco[:, oc, oc2 * 128:(oc2 + 1) * 128], uu[:, oc, :nq]) for oc in range(8)], t_uu + [t_Wco])
                S.op("dve", lambda e, psy=psy, oc2=oc2: e.tensor_tensor(out=mT[:, oc2, GOFF[g]:GOFF[g] + nq], in0=psy[:, :nq], in1=gcs[:, oc2, :nq], op=ALU.mult),
                     reads=[t_psy, t_gcs[oc2]], writes=[t_mT[oc2]], wadd=True)

        nxt = XNT_pre
        xntb.i = 1
        for g in range(5):
            cur = nxt
            if g < 4:
                nxt = own_xnt_parts(g + 1, xntb, xb, ssb, xnb, sqj, t_sq)
            conv_group(g, *cur)
            xnt_drain()
        pslist[0] = list(range(8))
        S.barrier()
        t_dbg = S.tile("dbg")
        if DEBUG:
            S.dma(t_dbg, dbg_mc1, bf(MOFF, 8 * NQ), writes=[t_dbg], wadd=True)
            S.dma(t_dbg, dbg_small, small, writes=[t_dbg], wadd=True)
            S.barrier()

        o = ROFF
        QT = bf(o, 8 * NQ).rearrange("p (h t) -> p h t", t=NQ); o += 8 * NQ
        gateT = bf(o, 8 * NQ).rearrange("p (h t) -> p h t", t=NQ); o += 8 * NQ
        t_QT = S.tile("QT")
        t_gateT = S.tile("gateT")
        PERSIST = o
        V0 = (bf(o, NT_ALL * 129).rearrange("p (t d) -> p t d", d=129), S.tile("V")); o += NT_ALL * 129 + 1

        def load_v(h, V, t_V):
            S.dma(t_V, V[:, 0, :], v_meta[:, h * 129:(h + 1) * 129], reads=[t_vscr], writes=[t_V])
            Vv = V[:, 1:65, :].rearrange("p (r c t) d -> p r c t d", r=4, c=8, t=2)
            for c in range(8):
                for r in range(4):
                    S.dma(t_V, Vv[:, r, c, :, :], v_all[c][r * 256:(r + 1) * 256, h * 129:(h + 1) * 129].rearrange("(t p) d -> p t d", p=128), reads=[t_vall[c]], writes=[t_V], wadd=True)

        XNTB = []
        for i in range(2):
            XNTB.append((bf(o, 8 * 544).rearrange("p (k t) -> p k t", t=544), S.tile("XNT"))); o += 4352
        Wq = bf(o, 8192).rearrange("p (k c) -> p k c", c=1024); o += 8192
        Wga = bf(o, 8192).rearrange("p (k c) -> p k c", c=1024); o += 8192
        t_Wq, t_Wga = S.tile("Wq"), S.tile("Wga")
        XB = []
        for i in range(2):
            XB.append((f32(o, 1024), S.tile("xb"))); o += 2048
        sqj = bf(o, 1024); o += 1024
        t_sq = S.tile("sq")
        SSB = []
        for i in range(4):
            SSB.append((f32(o, 1), S.tile("ss"))); o += 2
        XN = []
        for i in range(2):
            XN.append((bf(o, 1024), S.tile("xn"))); o += 1024
        stg_x = []
        for i in range(2):
            stg_x.append((f32(o, 1024), S.tile("stg"))); o += 2048
        assert o <= AR, o
        stg.items = STG + stg_x
        xntb, xb, ssb, xnb = Rot(XNTB), Rot(XB), Rot(SSB), Rot(XN)
        for kc in range(KC):
            nm = pcol[:, PC_NMIX + kc:PC_NMIX + kc + 1]
            load_w(Wq[:, kc, :], t_Wq, w_in[kc * 128:(kc + 1) * 128, 0:1024], 1024, nm)
            load_w(Wga[:, kc, :], t_Wga, w_in[kc * 128:(kc + 1) * 128, 5120:6144], 1024, nm)
        def qg_group(g, XNT, t_XNT, hook=None):
            nq = GQ[g]
            for h in range(8):
                ps, t_ps = nextps()
                mm_acc(ps[:, :nq], t_ps, [(Wq[:, kc, h * 128:(h + 1) * 128], XNT[:, kc, 32:32 + nq]) for kc in range(KC)], [t_XNT, t_Wq])
                if h % 2 == 0:
                    S.op("dve", lambda e, ps=ps, h=h, g=g, nq=nq: e.tensor_copy(out=QT[:, h, GOFF[g]:GOFF[g] + nq], in_=ps[:, :nq]), reads=[t_ps], writes=[t_QT], wadd=True)
                else:
                    S.op("act", lambda e, ps=ps, h=h, g=g, nq=nq: e.copy(out=QT[:, h, GOFF[g]:GOFF[g] + nq], in_=ps[:, :nq]), reads=[t_ps], writes=[t_QT], wadd=True)
                if h >= 1:
                    xnt_drain(2)
            if hook is not None:
                xnt_drain()
                hook()
            for h in range(8):
                ps, t_ps = nextps()
                mm_acc(ps[:, :nq], t_ps, [(Wga[:, kc, h * 128:(h + 1) * 128], XNT[:, kc, 32:32 + nq]) for kc in range(KC)], [t_XNT, t_Wga])
                S.op("act", lambda e, ps=ps, h=h: e.activation(out=gateT[:, h, GOFF[g]:GOFF[g] + nq], in_=ps[:, :nq], func=AF.Sigmoid),
                     reads=[t_ps], writes=[t_gateT], wadd=True)

        nxt = own_xnt(0, xntb, xb, ssb, xnb, sqj, t_sq)
        for g in range(5):
            cur = nxt
            if g < 4:
                nxt = own_xnt_parts(g + 1, xntb, xb, ssb, xnb, sqj, t_sq)
            qg_group(g, *cur, hook=((lambda: load_v(0, *V0)) if g == 3 else None))
            xnt_drain()
        S.barrier()
        if DEBUG:
            S.dma(t_dbg, dbg_qt, bf(ROFF, 8 * NQ), writes=[t_dbg], wadd=True)
            S.dma(t_dbg, dbg_gate, bf(ROFF + 8 * NQ, 8 * NQ), writes=[t_dbg], wadd=True)
            S.barrier()

        stg.items = STG
        o = PERSIST + NT_ALL * 129 + 1
        KTB, VB = [], [V0]
        for i in range(2):
            KTB.append((bf(o, NKEY), S.tile("KT"))); o += NKEY
        for i in range(1):
            VB.append((bf(o, NT_ALL * 129).rearrange("p (t d) -> p t d", d=129), S.tile("V"))); o += NT_ALL * 129 + 1
        PTB = []
        for i in range(6):
            PTB.append((bf(o, 512), S.tile("PT"))); o += 512
        cq = bf(o, 2048); o += 2048
        t_cq = S.tile("cq")
        RLB, OAB = [], []
        for i in range(2):
            RLB.append((f32(o, 512), S.tile("rl"))); o += 1024
        for i in range(2):
            OAB.append((f32(o, 512), S.tile("oa"))); o += 1024
        tt = f32(o, 512); o += 1024
        t_tt = S.tile("tt")
        o2r = f32(o, 512); o += 1024
        t_o2r = S.tile("o2r")
        sqb = bf(o, 512); o += 512
        t_sqb = S.tile("sqb")
        rsb = f32(o, 512); o += 1024
        t_rsb = S.tile("rsb")
        assert o <= AR, o
        ptb, rlb, oab = Rot(PTB), Rot(RLB), Rot(OAB)
        S.dma(t_cq, cq, cq_d, writes=[t_cq])
        ck = prow[:, PR_CK:PR_CK + 64]
        subw = pcol[:, PC_SUB:PC_SUB + 1]
        srot = [0]
        SK = 3

        def att_head(h, KT, t_KT, V, t_V):
            units = []
            for g in range(5):
                kbl = [(0, 16, 0, False, -1)]
                if g < 4:
                    for m in range(16 * g):
                        kbl.append(((m + 1) * 128, 128, m + 1, False, m))
                    for m in range(16 * g, 16 * g + 16):
                        kbl.append(((m + 1) * 128, 128, m + 1, True, m))
                else:
                    for m in range(64):
                        kbl.append(((m + 1) * 128, 128, m + 1, False, m))
                for bi, kb in enumerate(kbl):
                    units.append((g, bi == 0, bi == len(kbl) - 1) + kb)
            pts = {}
            deferred = []
            state = {}

            def emit_scores(idx):
                g, first, last, kc0, nk, vt, masked, m = units[idx]
                nq = GQ[g]
                banks = []
                for mp in range(2):
                    sb = srot[0] % 4
                    srot[0] += 1
                    pss, t_pss = PS[sb], PST[sb]
                    banks.append((pss, t_pss))
                    S.op("pe", lambda e, mp=mp, pss=pss: e.matmul(pss[:nk, :nq], lhsT=KT[64 * mp:64 * mp + 64, kc0:kc0 + nk], rhs=QT[64 * mp:64 * mp + 64, h, GOFF[g]:GOFF[g] + nq], start=True, stop=True),
                         reads=[t_KT, t_QT], writes=[t_pss])
                res = []
                for mp in range(2):
                    pss, t_pss = banks[mp]
                    pt, t_pt = ptb.next()
                    S.op("act", lambda e, pss=pss, pt=pt: e.activation(out=pt[:nk, :nq], in_=pss[:nk, :nq], func=AF.Exp, scale=0.125), reads=[t_pss], writes=[t_pt])
                    if masked:
                        S.op("dve", lambda e, pt=pt: e.scalar_tensor_tensor(out=pt[:, :nq], in0=cq[:, g * 512:g * 512 + nq], scalar=ck[:, m:m + 1], in1=pt[:, :nq], op0=ALU.is_ge, op1=ALU.mult),
                             reads=[t_pt, t_cq, t_const], writes=[t_pt])
                    res.append((pt, t_pt))
                pts[idx] = res

            def evac0(g):
                nq = GQ[g]
                rl, t_rl = rlb.next()
                oa, t_oa = oab.next()
                state[g] = (oa, t_oa)
                S.op("act", lambda e: e.copy(out=oa[:, :nq], in_=PS[4][:, :nq]), reads=[PST[4]], writes=[t_oa])
                S.op("act", lambda e: e.activation(out=rl[:, :nq], in_=PS[5][:, :nq], func=AF.Ln), reads=[PST[5]], writes=[t_rl])
                S.op("act", lambda e: e.activation(out=rl[:, :nq], in_=rl[:, :nq], func=AF.Exp, scale=-1.0), reads=[t_rl], writes=[t_rl])
                S.op("dve", lambda e: e.tensor_tensor(out=oa[:, :nq], in0=oa[:, :nq], in1=rl[:, :nq], op=ALU.mult), reads=[t_oa, t_rl], writes=[t_oa])

            def evac1(g):
                nq = GQ[g]
                rl, t_rl = rlb.next()
                oa, t_oa = state[g]
                S.op("act", lambda e: e.copy(out=o2r[:, :nq], in_=PS[6][:, :nq]), reads=[PST[6]], writes=[t_o2r])
                S.op("act", lambda e: e.activation(out=rl[:, :nq], in_=PS[7][:, :nq], func=AF.Ln), reads=[PST[7]], writes=[t_rl])
                S.op("act", lambda e: e.activation(out=rl[:, :nq], in_=rl[:, :nq], func=AF.Exp, scale=-1.0), reads=[t_rl], writes=[t_rl])
                S.op("dve", lambda e: e.tensor_tensor(out=tt[:, :nq], in0=o2r[:, :nq], in1=rl[:, :nq], op=ALU.mult), reads=[t_o2r, t_rl], writes=[t_tt])
                S.op("dve", lambda e: e.scalar_tensor_tensor(out=oa[:, :nq], in0=tt[:, :nq], scalar=nlam, in1=oa[:, :nq], op0=ALU.mult, op1=ALU.add),
                     reads=[t_tt, t_oa, t_small], writes=[t_oa])
                S.op("act", lambda e: e.activation(out=sqb[:, :nq], in_=oa[:, :nq], func=AF.Square), reads=[t_oa], writes=[t_sqb])

            def evac2(g):
                nq = GQ[g]
                oa, t_oa = state[g]
                sb = srot[0] % 4
                srot[0] += 1
                pss, t_pss = PS[sb], PST[sb]
                S.op("pe", lambda e: e.matmul(pss[:, :nq], lhsT=ones, rhs=sqb[:, :nq], start=True, stop=True), reads=[t_sqb, t_const], writes=[t_pss])
                S.op("act", lambda e: e.activation(out=rsb[:, :nq], in_=pss[:, :nq], func=AF.Ln, bias=1e-6 / 0.64, scale=1.0 / (128 * 0.64)), reads=[t_pss], writes=[t_rsb])
                S.op("act", lambda e: e.activation(out=rsb[:, :nq], in_=rsb[:, :nq], func=AF.Exp, scale=-0.5), reads=[t_rsb], writes=[t_rsb])
                S.op("dve", lambda e: e.tensor_tensor(out=tt[:, :nq], in0=oa[:, :nq], in1=rsb[:, :nq], op=ALU.mult), reads=[t_oa, t_rsb], writes=[t_tt])
                S.op("dve", lambda e: e.scalar_tensor_tensor(out=tt[:, :nq], in0=tt[:, :nq], scalar=subw, in1=gateT[:, h, GOFF[g]:GOFF[g] + nq], op0=ALU.mult, op1=ALU.mult),
                     reads=[t_tt, t_gateT, t_const], writes=[t_tt])
                S.op("dve", lambda e: e.tensor_tensor(out=mT[:, h, GOFF[g]:GOFF[g] + nq], in0=tt[:, :nq], in1=mT[:, h, GOFF[g]:GOFF[g] + nq], op=ALU.add),
                     reads=[t_tt, t_mT[h]], writes=[t_mT[h]], wadd=True)

            def emit_pv(idx):
                g, first, last, kc0, nk, vt, masked, m = units[idx]
                nq = GQ[g]
                res = pts.pop(idx)
                for mp in range(2):
                    pt, t_pt = res[mp]
                    bo, bl = (4, 5) if mp == 0 else (6, 7)
                    S.op("pe", lambda e, pt=pt, bo=bo: e.matmul(PS[bo][:, :nq], lhsT=V[:nk, vt, 0:128], rhs=pt[:nk, :nq], start=first, stop=last),
                         reads=[t_pt, t_V], writes=[PST[bo]], wadd=(not first))
                    S.op("pe", lambda e, pt=pt, bl=bl: e.matmul(PS[bl][:, :nq], lhsT=ones[:nk, :], rhs=pt[:nk, :nq], start=first, stop=last),
                         reads=[t_pt, t_const], writes=[PST[bl]], wadd=(not first))
                if last:
                    evac0(g)
                    evac1(g)
                    deferred.append((idx + 3, g))

            n = len(units)
            SKL = 1
            idx = 0
            while idx < n + SKL or deferred:
                if idx < n:
                    emit_scores(idx)
                if 0 <= idx - SKL < n:
                    emit_pv(idx - SKL)
                while deferred and deferred[0][0] <= idx:
                    evac2(deferred.pop(0)[1])
                idx += 1

        for h in range(8):
            KT, t_KT = KTB[h % 2]
            V, t_V = VB[h % 2]
            S.dma(t_KT, KT[:, 0:128], kt_meta[h], reads=[t_ktscr], writes=[t_KT])
            for r in range(4):
                S.dma(t_KT, KT[:, 128 + r * 2048:128 + (r + 1) * 2048], kt_all[h // 2][r * 256 + (h % 2) * 128:r * 256 + (h % 2 + 1) * 128, :], reads=[t_ktall[h // 2]], writes=[t_KT], wadd=True)
            if h > 0:
                load_v(h, V, t_V)
            att_head(h, KT, t_KT, V, t_V)
        S.barrier()
        if DEBUG:
            S.dma(t_dbg, dbg_mc2, bf(MOFF, 8 * NQ), writes=[t_dbg], wadd=True)
            S.barrier()

        o = ROFF
        acc_all = f32(o, 17 * 1024).rearrange("p (t c) -> p t c", c=1024); o += 17 * 2048
        t_accs = S.tiles(17, "acc")
        n2T = bf(o, 8 * NQ).rearrange("p (k t) -> p k t", t=NQ); o += 8 * NQ
        t_n2T = S.tile("n2T")
        F2FREE = o
        Wo = bf(o, 8192).rearrange("p (k c) -> p k c", c=1024); o += 8192
        t_Wo = S.tile("Wo")
        XB = []
        for i in range(2):
            XB.append((f32(o, 1024), S.tile("xr"))); o += 2048
        XN = []
        for i in range(2):
            XN.append((bf(o, 1024), S.tile("n2"))); o += 1024
        sqj = bf(o, 1024); o += 1024
        t_sq = S.tile("sq")
        hs = f32(o, 1024); o += 2048
        t_hs = S.tile("hs")
        SSB = []
        for i in range(4):
            SSB.append((f32(o, 1), S.tile("ss"))); o += 2
        assert o <= AR, o
        xb, xnb, ssb = Rot(XB), Rot(XN), Rot(SSB)
        for kc in range(KC):
            load_w(Wo[:, kc, :], t_Wo, w_o[kc * 128:(kc + 1) * 128, :], 1024, None)

        def out_tiles(g):
            if g < 4:
                return [(0, 2, None), (2, 128, 4 * g), (130, 128, 4 * g + 1), (258, 128, 4 * g + 2), (386, 126, 4 * g + 3)]
            return [(0, 2, None), (2, 32, 16)]

        def f1_a(g, q0, n, ai):
            xr, t_xr = xb.next()
            S.dma(t_xr, xr[:n, :], x_own[GROW[g] + 32 + q0:GROW[g] + 32 + q0 + n, :], writes=[t_xr])
            if ai is None:
                hm, t_hm = hs, t_hs
            else:
                hm, t_hm = acc_all[:, ai, :], t_accs[ai]
            for half in range(2):
                ps, t_ps = nextps()
                mm_acc(ps[:n, :], t_ps, [(mT[:, c, GOFF[g] + q0:GOFF[g] + q0 + n], Wo[:, c, half * 512:(half + 1) * 512]) for c in range(8)], t_mT + [t_Wo])
                S.op("dve", lambda e, ps=ps, hm=hm, xr=xr, n=n, half=half: e.tensor_tensor(out=hm[:n, half * 512:(half + 1) * 512], in0=ps[:n, :], in1=xr[:n, half * 512:(half + 1) * 512], op=ALU.add),
                     reads=[t_ps, t_xr], writes=[t_hm], wadd=True)
            ss, t_ss = ssb.next()
            n2, t_n2 = xnb.next()
            norm_rows(hm, t_hm, n, sqj, t_sq, ss, t_ss, n2, t_n2)
            return (g, q0, n, n2, t_n2)

        def f1_b(g, q0, n, n2, t_n2):
            transpose_rows(n2, t_n2, n, n2T, t_n2T, GOFF[g] + q0, evac="act")

        prev = None
        for g in range(5):
            for (q0, n, ai) in out_tiles(g):
                cur = f1_a(g, q0, n, ai)
                if prev is not None:
                    f1_b(*prev)
                prev = cur
        f1_b(*prev)
        S.barrier()
        if DEBUG:
            S.dma(t_dbg, dbg_acc, f32(ROFF, 17 * 1024), writes=[t_dbg], wadd=True)
            S.dma(t_dbg, dbg_n2t, bf(ROFF + 17 * 2048, 8 * NQ), writes=[t_dbg], wadd=True)
            S.barrier()

        o = F2FREE
        WB = []
        for i, base in enumerate((MOFF, o)):
            WB.append(dict(Wup=bf(base, 8192).rearrange("p (k c) -> p k c", c=1024), Wdn=bf(base + 8192, 4096).rearrange("p (k c) -> p k c", c=1024),
                           t_Wup=S.tile("Wup"), t_Wdn=S.tile("Wdn")))
        o += 12288
        HH = []
        for i in range(2):
            HH.append((bf(o, 2048).rearrange("p (k t) -> p k t", t=512), S.tiles(4, "hh"))); o += 2048
        TG = []
        for i in range(2):
            TG.append((f32(o, 512), S.tile("tg"))); o += 1024
        TV = []
        for i in range(2):
            TV.append((f32(o, 512), S.tile("tv"))); o += 1024
        nfw = prow[:, PR_NF:PR_NF + 1024]
        OB = []
        for i in range(2):
            OB.append((f32(o, 1024), S.tile("ob"))); o += 2048
        sqj = bf(o, 1024); o += 1024
        t_sq = S.tile("sq")
        SSB = []
        for i in range(4):
            SSB.append((f32(o, 1), S.tile("ss"))); o += 2
        stg3 = (f32(o, 1024), S.tile("stg")); o += 2048
        assert o <= AR, o
        hhb, tgb, tvb, obb, ssb = Rot(HH), Rot(TG), Rot(TV), Rot(OB), Rot(SSB)
        stg.items = STG + [stg3]
        t_out = S.tile("outd")
        def f2_group(pi, f0, nf, g, wb):
            Wup, Wdn, t_Wup, t_Wdn = wb["Wup"], wb["Wdn"], wb["t_Wup"], wb["t_Wdn"]
            nq = GQ[g]
            nn = nq - 2
            hh, t_hh = hhb.next()
            for fl in range(nf):
                fc = f0 + fl
                res = []
                for which in range(2):
                    ps, t_ps = nextps()
                    mm_acc(ps[:, :nq], t_ps, [(Wup[:, kc, which * 512 + fl * 128:which * 512 + (fl + 1) * 128], n2T[:, kc, GOFF[g]:GOFF[g] + nq]) for kc in range(KC)], [t_n2T, t_Wup])
                    ch = which * 22 + fc
                    tb, t_tb = (tgb if which == 0 else tvb).next()
                    wcol = PC_FDW + ch * 3
                    S.op("act", lambda e, ps=ps, tb=tb, wcol=wcol, ch=ch, nn=nn: e.activation(out=tb[:, :nn], in_=ps[:, 0:nn], func=AF.Identity, scale=pcol[:, wcol:wcol + 1], bias=pcol[:, PC_FDB + ch:PC_FDB + ch + 1]),
                         reads=[t_ps, t_const], writes=[t_tb])
                    for k in (1, 2):
                        S.op("dve", lambda e, ps=ps, tb=tb, wcol=wcol, k=k, nn=nn: e.scalar_tensor_tensor(out=tb[:, :nn], in0=ps[:, k:k + nn], scalar=pcol[:, wcol + k:wcol + k + 1], in1=tb[:, :nn], op0=ALU.mult, op1=ALU.add),
                             reads=[t_ps, t_tb, t_const], writes=[t_tb])
                    res.append((tb, t_tb))
                (tg, t_tg), (tv, t_tv) = res
                S.op("act", lambda e, tg=tg, nn=nn: e.activation(out=tg[:, :nn], in_=tg[:, :nn], func=AF.Silu), reads=[t_tg], writes=[t_tg])
                S.op("dve", lambda e, tg=tg, tv=tv, hh=hh, fl=fl, nn=nn: e.tensor_tensor(out=hh[:, fl, :nn], in0=tg[:, :nn], in1=tv[:, :nn], op=ALU.mult),
                     reads=[t_tg, t_tv], writes=[t_hh[fl]])
            return hh, t_hh

        def f2_down(pi, f0, nf, g, wb, hh, t_hh):
            Wup, Wdn, t_Wup, t_Wdn = wb["Wup"], wb["Wdn"], wb["t_Wup"], wb["t_Wdn"]
            nq = GQ[g]
            for (q0, n, ai) in out_tiles(g):
                if ai is None:
                    continue
                for half in range(2):
                    ps, t_ps = nextps()
                    mm_acc(ps[:n, :], t_ps, [(hh[:, fl, q0 - 2:q0 - 2 + n], Wdn[:, fl, half * 512:(half + 1) * 512]) for fl in range(nf)], t_hh[:nf] + [t_Wdn])
                    S.op("dve", lambda e, ps=ps, ai=ai, n=n, half=half: e.tensor_tensor(out=acc_all[:n, ai, half * 512:(half + 1) * 512], in0=ps[:n, :], in1=acc_all[:n, ai, half * 512:(half + 1) * 512], op=ALU.add),
                         reads=[t_ps, t_accs[ai]], writes=[t_accs[ai]], wadd=True)
                if pi == NPASS - 1:
                    ss, t_ss = ssb.next()
                    ob, t_ob = obb.next()
                    S.op("act", lambda e, ai=ai, n=n, ss=ss: e.activation(out=sqj[:n, :], in_=acc_all[:n, ai, :], func=AF.Square, accum_out=ss[:n, :]),
                         reads=[t_accs[ai]], writes=[t_sq, t_ss])
                    S.op("act", lambda e, ss=ss, n=n: e.activation(out=ss[:n, :], in_=ss[:n, :], func=AF.Sqrt, bias=1e-6, scale=1.0 / 1024), reads=[t_ss], writes=[t_ss])
                    S.op("dve", lambda e, ss=ss, n=n: e.reciprocal(out=ss[:n, :], in_=ss[:n, :]), reads=[t_ss], writes=[t_ss])
                    S.op("dve", lambda e, ai=ai, n=n, ss=ss, ob=ob: e.scalar_tensor_tensor(out=ob[:n, :], in0=acc_all[:n, ai, :], scalar=ss[:n, 0:1], in1=nfw[:n, :], op0=ALU.mult, op1=ALU.mult),
                         reads=[t_accs[ai], t_ss, t_const], writes=[t_ob])
                    orow = 510 * g + (q0 - 2)
                    S.dma(t_ob, out_d[orow:orow + n, :], ob[:n, :], reads=[t_ob], writes=[t_out], wadd=True)

        passes = [(0, 4), (4, 4), (8, 4), (12, 4), (16, 4), (20, 2)]
        NPASS = len(passes)

        def pass_loads(pi):
            f0, nf = passes[pi]
            wb = WB[pi % 2]
            res = []
            for kc in range(KC):
                nm = pcol[:, PC_NFFN + kc:PC_NFFN + kc + 1]
                res.append((wb["Wup"][:, kc, 0:nf * 128], wb["t_Wup"], w_up[kc * 128:(kc + 1) * 128, f0 * 128:(f0 + nf) * 128], nf * 128, nm))
                res.append((wb["Wup"][:, kc, 512:512 + nf * 128], wb["t_Wup"], w_up[kc * 128:(kc + 1) * 128, 2816 + f0 * 128:2816 + (f0 + nf) * 128], nf * 128, nm))
            for fl in range(nf):
                res.append((wb["Wdn"][:, fl, :], wb["t_Wdn"], w_dn[(f0 + fl) * 128:(f0 + fl + 1) * 128, :], 1024, None))
            return res

        for a in pass_loads(0):
            load_w(*a)
        for pi, (f0, nf) in enumerate(passes):
            pend = pass_loads(pi + 1) if pi + 1 < NPASS else []
            per = (len(pend) + 3) // 4
            prev = None
            for g in range(5):
                cur = (g,) + f2_group(pi, f0, nf, g, WB[pi % 2])
                if prev is not None:
                    f2_down(pi, f0, nf, prev[0], WB[pi % 2], prev[1], prev[2])
                prev = cur
                for _ in range(per):
                    if pend:
                        load_w(*pend.pop(0))
            f2_down(pi, f0, nf, prev[0], WB[pi % 2], prev[1], prev[2])
            while pend:
                load_w(*pend.pop(0))
        S.barrier(final=True)
        S.emit()
    nc._sched_stats = S.stats
    return nc


def _host_inputs(inputs):
    f = np.float32
    x = np.asarray(inputs["x"], f)
    meta = np.asarray(inputs["meta_tokens"], f)
    B = x.shape[0]
    pcol = np.zeros((128, NPC), f)

    def colmaj(v, nchunk):
        return np.ascontiguousarray(np.asarray(v, f).reshape(nchunk, 128).T)

    pcol[:, PC_NMIX:PC_NMIX + 8] = colmaj(inputs["norm_mix_w"][0], 8)
    pcol[:, PC_NFFN:PC_NFFN + 8] = colmaj(inputs["norm_ffn_w"][0], 8)
    cdw = np.asarray(inputs["conv_dw_w"][0], f)
    pcol[:, PC_CDW:PC_CDW + 248] = cdw.T.reshape(8, 128, 31).transpose(1, 0, 2).reshape(128, 248)
    pcol[:, PC_CDB:PC_CDB + 8] = colmaj(inputs["conv_dw_b"][0], 8)
    pcol[:, PC_LNW:PC_LNW + 8] = colmaj(inputs["conv_ln_w"][0], 8)
    pcol[:, PC_LNB:PC_LNB + 8] = colmaj(inputs["conv_ln_b"][0], 8)
    fdw = np.asarray(inputs["ffn_dw_w"][0], f)
    pcol[:, PC_FDW:PC_FDW + 132] = fdw.T.reshape(44, 128, 3).transpose(1, 0, 2).reshape(128, 132)
    pcol[:, PC_FDB:PC_FDB + 44] = colmaj(inputs["ffn_dw_b"][0], 44)
    pcol[:, PC_SUB] = np.asarray(inputs["subln_w"][0], f)
    prow = np.zeros((128, NPR), f)
    prow[:, PR_NF:PR_NF + 1024] = np.asarray(inputs["norm_final_w"], f)[None, :]
    prow[:, PR_SUB:PR_SUB + 128] = np.asarray(inputs["subln_w"][0], f)[None, :]
    prow[:, PR_LQ1:PR_LQ1 + 64] = np.asarray(inputs["lambda_q1"][0], f)[None, :]
    prow[:, PR_LK1:PR_LK1 + 64] = np.asarray(inputs["lambda_k1"][0], f)[None, :]
    prow[:, PR_LQ2:PR_LQ2 + 64] = np.asarray(inputs["lambda_q2"][0], f)[None, :]
    prow[:, PR_LK2:PR_LK2 + 64] = np.asarray(inputs["lambda_k2"][0], f)[None, :]
    kk = np.arange(128)[:, None]
    mm = np.arange(64)[None, :]
    prow[:, PR_CK:PR_CK + 64] = (2 * mm + (kk >= 64)).astype(f)
    ident = np.eye(128).astype(ml_dtypes.bfloat16)
    shared = dict(
        w_in=np.ascontiguousarray(np.asarray(inputs["w_in"][0], f)),
        w_co=np.ascontiguousarray(np.asarray(inputs["w_conv_out"][0], f)),
        w_o=np.ascontiguousarray(np.asarray(inputs["w_out"][0], f)),
        w_up=np.ascontiguousarray(np.asarray(inputs["w_up"][0], f)),
        w_dn=np.ascontiguousarray(np.asarray(inputs["w_down"][0], f)),
        pcol=pcol, prow=prow, ident=ident)
    in_maps = []
    for c in range(8):
        b, j = c // 4, c % 4
        xa = np.zeros((17 * 128, D), f)
        xa[0:16] = meta
        xa[128:] = x[b, 2048 * j:2048 * (j + 1)]
        seq = np.concatenate([meta, x[b]], axis=0)
        xo = np.zeros((NROWS, D), f)
        cq = np.zeros((128, 2048), ml_dtypes.bfloat16)
        for g in range(5):
            if g < 4:
                G = 4 * g + j
                p0 = 510 * G - 18
            else:
                p0 = 8142
            nr = 32 + GQ[g]
            ps = np.arange(p0, p0 + nr)
            valid = ps >= 0
            xo[GROW[g] + np.nonzero(valid)[0]] = seq[ps[valid]]
            if g < 4:
                tq = ps[32:] - 16
                cqv = np.where(tq >= 0, tq // 64, -1).astype(ml_dtypes.bfloat16)
                cq[:, g * 512:(g + 1) * 512] = cqv[None, :]
        d = dict(shared)
        d.update(x_all=xa, x_own=xo, cq=cq)
        in_maps.append(d)
    return in_maps


_NC = None


def kernel(**inputs):
    global _NC
    in_maps = _host_inputs(inputs)
    if _NC is None:
        _NC = build_program()
    res = run_bass_kernel_spmd(_NC, in_maps, core_ids=list(range(8)))
    B = 2
    out = np.zeros((B, 8192, D), np.float32)
    for c in range(8):
        b, j = c // 4, c % 4
        oo = res.results[c]["out_own"]
        for g in range(4):
            G = 4 * g + j
            out[b, 510 * G:510 * G + 510] = oo[510 * g:510 * g + 510]
        if j == 0:
            out[b, 8160:8192] = oo[2040:2072]
    return out
```

```python
import numpy as np
import ml_dtypes
import concourse.bass as bass
import concourse.mybir as mybir
from concourse.bass_utils import run_bass_kernel_spmd

F32 = mybir.dt.float32
BF16 = mybir.dt.bfloat16
ALU = mybir.AluOpType
AF = mybir.ActivationFunctionType
AX = mybir.AxisListType

D = 1024
KC = 8
NT_ALL = 65
NKEY = NT_ALL * 128
GQ = [512, 512, 512, 512, 34]
GOFF = [0, 512, 1024, 1536, 2048]
NQ = 2082
GROW = [0, 544, 1088, 1632, 2176]
NROWS = 2242
NOUT = 2072
AR = 106400
DEBUG = False


class T:
    __slots__ = ("name", "w", "r", "dkey")

    def __init__(self, name):
        self.name = name
        self.w = {}
        self.r = {}
        self.dkey = None


class Sched:
    CE = ("pe", "act", "dve", "pool")

    def __init__(self, nc):
        self.nc = nc
        self.q = {e: [] for e in ("pe", "act", "dve", "pool", "sp")}
        self.cnt = {}
        self.known = {e: {} for e in self.q}
        self.sems = {}
        self.ntile = 0
        for e in self.CE:
            self.cnt[e] = 0
            self.sems[e] = nc.alloc_semaphore(name="s_" + e)

    def tile(self, name=None):
        self.ntile += 1
        return T((name or "t") + "_%d" % self.ntile)

    def tiles(self, n, name=None):
        return [self.tile(name) for _ in range(n)]

    def _needs(self, e, reads, writes):
        needs = {}
        for t in reads:
            for k, v in t.w.items():
                if needs.get(k, 0) < v:
                    needs[k] = v
        for t in writes:
            for d in (t.w, t.r):
                for k, v in d.items():
                    if needs.get(k, 0) < v:
                        needs[k] = v
        kn = self.known[e]
        for k, v in needs.items():
            if k == "pe" and e == "pe":
                continue
            if kn.get(k, 0) < v:
                self.q[e].append(("w", k, v))
                kn[k] = v

    def _mark(self, k, v, reads, writes, wadd):
        for t in reads:
            if t.r.get(k, 0) < v:
                t.r[k] = v
        for t in writes:
            if wadd:
                t.w[k] = v
            else:
                t.w = {k: v}
                t.r = {}

    def op(self, e, fn, reads=(), writes=(), wadd=False):
        self._needs(e, reads, writes)
        self.cnt[e] += 1
        v = self.cnt[e]
        self.q[e].append(("o", fn, e, 1))
        self._mark(e, v, reads, writes, wadd)

    def dma(self, semt, out_ap, in_ap, reads=(), writes=(), wadd=False, q="sp"):
        if semt.dkey is None:
            semt.dkey = "d_" + semt.name
            self.cnt[semt.dkey] = 0
            self.sems[semt.dkey] = self.nc.alloc_semaphore(name=semt.dkey)
        k = semt.dkey
        self._needs(q, reads, writes)
        self.cnt[k] += 16
        v = self.cnt[k]
        self.q[q].append(("o", lambda eng: eng.dma_start(out=out_ap, in_=in_ap), k, 16))
        self._mark(k, v, reads, writes, wadd)

    def ccop(self, semt, fn, reads=(), writes=()):
        if semt.dkey is None:
            semt.dkey = "c_" + semt.name
            self.cnt[semt.dkey] = 0
            self.sems[semt.dkey] = self.nc.alloc_semaphore(name=semt.dkey)
        k = semt.dkey
        self._needs("pool", reads, writes)
        self.cnt[k] += 1
        v = self.cnt[k]
        self.q["pool"].append(("o", fn, k, 1))
        self._mark(k, v, reads, writes, False)

    def barrier(self, final=False):
        for e in self.q:
            kn = self.known[e]
            for k, v in self.cnt.items():
                if k == e or v == 0 or (k.startswith("c_") and not final):
                    continue
                if kn.get(k, 0) < v:
                    self.q[e].append(("w", k, v))
                    kn[k] = v

    def emit(self):
        nc = self.nc
        self.stats = {e: (len(v), sum(1 for it in v if it[0]=='w')) for e, v in self.q.items()}

        def run(e, eng):
            for it in self.q[e]:
                if it[0] == "w":
                    eng.wait_ge(self.sems[it[1]], it[2])
                else:
                    it[1](eng).then_inc(self.sems[it[2]], it[3])

        with nc.Block() as block:
            @block.tensor
            def _(eng):
                run("pe", eng)

            @block.scalar
            def _(eng):
                run("act", eng)

            @block.vector
            def _(eng):
                run("dve", eng)

            @block.gpsimd
            def _(eng):
                run("pool", eng)

            @block.sync
            def _(eng):
                run("sp", eng)


class Rot:
    def __init__(self, items):
        self.items = items
        self.i = 0

    def next(self):
        it = self.items[self.i % len(self.items)]
        self.i += 1
        return it


PC_NMIX, PC_NFFN, PC_CDW, PC_CDB, PC_LNW, PC_LNB, PC_FDW, PC_FDB, PC_SUB, NPC = 0, 8, 16, 264, 272, 280, 288, 420, 464, 466
PR_NF, PR_SUB, PR_LQ1, PR_LK1, PR_LQ2, PR_LK2, PR_CK, NPR = 0, 1024, 1152, 1216, 1280, 1344, 1408, 1472


def build_program():
    nc = bass.Bass("TRN2", target_bir_lowering=False)

    def din(name, shape, dt=F32):
        return nc.dram_tensor(name, shape, dt, kind="ExternalInput").ap()

    x_all = din("x_all", [17 * 128, D])
    x_own = din("x_own", [NROWS, D])
    w_in = din("w_in", [D, 7168])
    w_co = din("w_co", [D, D])
    w_o = din("w_o", [D, D])
    w_up = din("w_up", [D, 5632])
    w_dn = din("w_dn", [2816, D])
    pcol_d = din("pcol", [128, NPC])
    prow_d = din("prow", [128, NPR])
    cq_d = din("cq", [128, 2048], BF16)
    ident_d = din("ident", [128, 128], BF16)
    out_d = nc.dram_tensor("out_own", [NOUT, D], F32, kind="ExternalOutput").ap()
    kt_part = [nc.dram_tensor("kt_part%d" % c, [256, 2048], BF16).ap() for c in range(4)]
    kt_all = [nc.dram_tensor("kt_all%d" % c, [1024, 2048], BF16).ap() for c in range(4)]
    kt_meta = nc.dram_tensor("kt_meta", [8, 128, 128], BF16).ap()
    v_part = [nc.dram_tensor("v_part%d" % c, [256, 8 * 129], BF16).ap() for c in range(8)]
    v_all = [nc.dram_tensor("v_all%d" % c, [1024, 8 * 129], BF16).ap() for c in range(8)]
    v_meta = nc.dram_tensor("v_meta", [128, 8 * 129], BF16).ap()
    NPE_G = 20
    dg_scr = nc.dram_tensor("dg_scr", [8, 128, NPE_G * 128], BF16).ap()
    if DEBUG:
        dbg_mc1 = nc.dram_tensor("dbg_mc1", [128, 8 * NQ], BF16, kind="ExternalOutput").ap()
        dbg_mc2 = nc.dram_tensor("dbg_mc2", [128, 8 * NQ], BF16, kind="ExternalOutput").ap()
        dbg_qt = nc.dram_tensor("dbg_qt", [128, 8 * NQ], BF16, kind="ExternalOutput").ap()
        dbg_gate = nc.dram_tensor("dbg_gate", [128, 8 * NQ], BF16, kind="ExternalOutput").ap()
        dbg_acc = nc.dram_tensor("dbg_acc", [128, 17 * 1024], F32, kind="ExternalOutput").ap()
        dbg_n2t = nc.dram_tensor("dbg_n2t", [128, 8 * NQ], BF16, kind="ExternalOutput").ap()
        dbg_small = nc.dram_tensor("dbg_small", [128, 64], F32, kind="ExternalOutput").ap()

    S = Sched(nc)
    from contextlib import ExitStack
    with ExitStack() as es:
        arena = es.enter_context(nc.sbuf_tensor("arena", [128, AR], BF16))
        PS = [es.enter_context(nc.psum_tensor("ps%d" % i, [128, 512], F32)) for i in range(8)]
        PST = S.tiles(8, "ps")

        def bf(off, n):
            return arena[:, off:off + n]

        def f32(off, n):
            return arena[:, off:off + 2 * n].bitcast(F32)

        o = 0
        ident = bf(o, 128); o += 128
        ones = bf(o, 128); o += 128
        pcol = f32(o, NPC); o += 2 * NPC
        prow = f32(o, NPR); o += 2 * NPR
        small = f32(o, 64); o += 128
        STG = [(f32(o, 1024), S.tile("stg")), (f32(o + 2048, 1024), S.tile("stg"))]; o += 4096
        assert o <= 9216, o
        MOFF = 9216
        ROFF = MOFF + 17408
        t_const = S.tile("const")
        t_small = S.tile("small")
        stg = Rot(STG)
        mT = bf(MOFF, 8 * NQ).rearrange("p (k t) -> p k t", t=NQ)
        t_mT = S.tiles(8, "mT")

        S.dma(t_const, ident, ident_d, writes=[t_const], wadd=True)
        S.dma(t_const, pcol, pcol_d, writes=[t_const], wadd=True)
        S.dma(t_const, prow, prow_d, writes=[t_const], wadd=True)
        S.op("pool", lambda e: e.memset(ones, 1.0), writes=[t_const], wadd=True)
        tmpl = f32(ROFF, 64)
        t_tmpl = S.tile()
        for i, (a, b) in enumerate(((PR_LQ1, PR_LK1), (PR_LQ2, PR_LK2))):
            S.op("dve", lambda e, a=a, b=b: e.tensor_tensor(out=tmpl, in0=prow[:, a:a + 64], in1=prow[:, b:b + 64], op=ALU.mult),
                 reads=[t_const], writes=[t_tmpl])
            S.op("dve", lambda e, i=i: e.reduce_sum(out=small[:, i:i + 1], in_=tmpl, axis=AX.X), reads=[t_tmpl], writes=[t_small], wadd=True)
        S.op("act", lambda e: e.activation(out=small[:, 2:4], in_=small[:, 0:2], func=AF.Exp), reads=[t_small], writes=[t_small], wadd=True)
        S.op("dve", lambda e: e.tensor_tensor(out=small[:, 4:5], in0=small[:, 3:4], in1=small[:, 2:3], op=ALU.subtract), reads=[t_small], writes=[t_small], wadd=True)
        S.op("dve", lambda e: e.tensor_scalar(out=small[:, 4:5], in0=small[:, 4:5], scalar1=-0.2, scalar2=None, op0=ALU.add), reads=[t_small], writes=[t_small], wadd=True)
        nlam = small[:, 4:5]

        lw_i = [0]

        def load_w(dst, t_dst, src, ncols, scale):
            for c0 in range(0, ncols, 1024):
                n = min(1024, ncols - c0)
                st, t_st = stg.next()
                S.dma(t_st, st[:, :n], src[:, c0:c0 + n], writes=[t_st])
                lw_i[0] += 1
                if lw_i[0] % 2 == 0:
                    if scale is None:
                        S.op("dve", lambda e, st=st, n=n, c0=c0: e.tensor_copy(out=dst[:, c0:c0 + n], in_=st[:, :n]),
                             reads=[t_st], writes=[t_dst], wadd=True)
                    else:
                        S.op("dve", lambda e, st=st, n=n, c0=c0: e.tensor_scalar(out=dst[:, c0:c0 + n], in0=st[:, :n], scalar1=scale, scalar2=None, op0=ALU.mult),
                             reads=[t_st, t_const], writes=[t_dst], wadd=True)
                else:
                    if scale is None:
                        S.op("act", lambda e, st=st, n=n, c0=c0: e.copy(out=dst[:, c0:c0 + n], in_=st[:, :n]),
                             reads=[t_st], writes=[t_dst], wadd=True)
                    else:
                        S.op("act", lambda e, st=st, n=n, c0=c0: e.activation(out=dst[:, c0:c0 + n], in_=st[:, :n], func=AF.Identity, scale=scale),
                             reads=[t_st, t_const], writes=[t_dst], wadd=True)

        psrot = [0]
        pslist = [list(range(8))]

        def nextps():
            l = pslist[0]
            i = l[psrot[0] % len(l)]
            psrot[0] += 1
            return PS[i], PST[i]

        def norm_rows(xt, t_x, n, sqj, t_sq, ss, t_ss, xn, t_xn, eps_scaled=1024 * 1e-6, mul=32.0):
            S.op("act", lambda e: e.activation(out=sqj[:n, :], in_=xt[:n, :], func=AF.Square, accum_out=ss[:n, :]),
                 reads=[t_x], writes=[t_sq, t_ss])
            S.op("act", lambda e: e.activation(out=ss[:n, :], in_=ss[:n, :], func=AF.Sqrt, bias=eps_scaled, scale=1.0), reads=[t_ss], writes=[t_ss])
            S.op("dve", lambda e: e.reciprocal(out=ss[:n, :], in_=ss[:n, :]), reads=[t_ss], writes=[t_ss])
            S.op("dve", lambda e: e.tensor_scalar(out=xn[:n, :], in0=xt[:n, :], scalar1=ss[:n, 0:1], scalar2=mul, op0=ALU.mult, op1=ALU.mult),
                 reads=[t_x, t_ss], writes=[t_xn])

        def transpose_rows(src, t_src, n, dst3, t_dst, coff, evac="act"):
            ps, t_ps = nextps()
            psb = ps[:].bitcast(BF16)
            for kc in range(KC):
                S.op("pe", lambda e, kc=kc: e.transpose(out=psb[:, kc * 128:kc * 128 + n], in_=src[:n, kc * 128:(kc + 1) * 128], identity=ident[:n, :n]),
                     reads=[t_src, t_const], writes=[t_ps], wadd=(kc > 0))
            pv = psb.rearrange("p (k t) -> p k t", t=128)[:, :, :n]
            if evac == "act":
                S.op("act", lambda e: e.copy(out=dst3[:, :, coff:coff + n], in_=pv), reads=[t_ps], writes=[t_dst], wadd=True)
            else:
                S.op("dve", lambda e: e.tensor_copy(out=dst3[:, :, coff:coff + n], in_=pv), reads=[t_ps], writes=[t_dst], wadd=True)

        def mm_acc(ps_ap, t_ps, pairs, reads):
            n = len(pairs)
            for i, (l, r) in enumerate(pairs):
                S.op("pe", lambda e, l=l, r=r, i=i: e.matmul(ps_ap, lhsT=l, rhs=r, start=(i == 0), stop=(i == n - 1)),
                     reads=reads, writes=[t_ps], wadd=(i > 0))

        Wkv = bf(MOFF, 8 * 2048).rearrange("p (k c) -> p k c", c=2048)
        t_Wkv = S.tile("Wkv")
        PW = ROFF + 8704
        Wglu = bf(PW, 16384).rearrange("p (k c) -> p k c", c=2048)
        Wgc = bf(PW + 16384, 8192).rearrange("p (k c) -> p k c", c=1024)
        Wco = bf(PW + 24576, 8192).rearrange("p (k c) -> p k c", c=1024)
        t_Wglu, t_Wgc, t_Wco = S.tile("Wglu"), S.tile("Wgc"), S.tile("Wco")
        o = PW + 32768
        XB = []
        for i in range(4):
            XB.append((f32(o, 1024), S.tile("xb"))); o += 2048
        sqj = bf(o, 1024); o += 1024
        t_sq = S.tile("sq")
        SSB = []
        for i in range(8):
            SSB.append((f32(o, 1), S.tile("ss"))); o += 2
        XN = []
        for i in range(3):
            XN.append((bf(o, 1024), S.tile("xn"))); o += 1024
        XT = []
        for i in range(2):
            XT.append((bf(o, 4096).rearrange("p (k t) -> p k t", t=512), S.tile("xT"))); o += 4096
        KST = []
        for i in range(2):
            KST.append((bf(o, 4096).rearrange("p (h t) -> p h t", t=512), S.tile("kst"))); o += 4096
        VST = []
        for i in range(4):
            VST.append((bf(o, 1032).rearrange("p (h t) -> p h t", t=129), S.tile("vst"))); o += 1032
        DGT = []
        for i in range(1):
            DGT.append((bf(o, NPE_G * 128).rearrange("p (k c) -> p k c", c=128), S.tile("dgt"))); o += NPE_G * 128
        stg_k = (f32(o, 1024), S.tile("stg")); o += 2048
        assert o <= AR, o
        stg.items = STG + [stg_k]
        xb, ssb, xnb, xtb, kstb, vstb = Rot(XB), Rot(SSB), Rot(XN), Rot(XT), Rot(KST), Rot(VST)
        dgtb = Rot(DGT)
        t_dgscr = S.tile("dgscr")

        def build_diag(oc):
            dgt, t_dgt = dgtb.next()
            wb = PC_CDW + oc * 31
            for k in range(NPE_G):
                S.op("pool", lambda e, k=k: e.tensor_scalar(out=dgt[:, k, :], in0=ident, scalar1=pcol[:, wb + k:wb + k + 1], scalar2=None, op0=ALU.mult),
                     reads=[t_const], writes=[t_dgt], wadd=(k > 0))
            S.dma(t_dgt, dg_scr[oc], dgt.rearrange("p k c -> p (k c)"), reads=[t_dgt], writes=[t_dgscr], wadd=True, q="pool")
        t_ktscr = S.tile("ktscr")
        t_vscr = S.tile("vscr")
        t_Wv = S.tile("Wv")
        for kc in range(KC):
            load_w(Wkv[:, kc, 1024:2048], t_Wv, w_in[kc * 128:(kc + 1) * 128, 2048:3072], 1024, pcol[:, PC_NMIX + kc:PC_NMIX + kc + 1])
        for kc in range(KC):
            load_w(Wkv[:, kc, 0:1024], t_Wkv, w_in[kc * 128:(kc + 1) * 128, 1024:2048], 1024, pcol[:, PC_NMIX + kc:PC_NMIX + kc + 1])
        for vi in range(4):
            S.op("pool", lambda e, vi=vi: e.memset(VST[vi][0][:, :, 128:129], 1.0), writes=[VST[vi][1]])
        blocks = [(0, 1)] + [(1 + 4 * i, 4) for i in range(4)]
        def kv_prep(t0, ntl):
            xT, t_xT = xtb.next()
            for tl in range(ntl):
                row0 = (t0 + tl) * 128
                xt, t_x = xb.next()
                S.dma(t_x, xt, x_all[row0:row0 + 128, :], writes=[t_x])
                ss, t_ss = ssb.next()
                xn, t_xn = xnb.next()
                norm_rows(xt, t_x, 128, sqj, t_sq, ss, t_ss, xn, t_xn)
                transpose_rows(xn, t_xn, 128, xT, t_xT, tl * 128, evac="act")
            return xT, t_xT

        def kv_compute(t0, ntl, xT, t_xT):
            for tl in range(ntl):
                vst, t_vst = vstb.next()
                for half in range(2):
                    ps, t_ps = nextps()
                    mm_acc(ps[:, :], t_ps, [(xT[:, kc, tl * 128:(tl + 1) * 128], Wkv[:, kc, 1024 + half * 512:1024 + (half + 1) * 512]) for kc in range(KC)], [t_xT, t_Wv])
                    S.op("dve" if half == 0 else "act",
                         (lambda e, ps=ps, vst=vst, half=half: e.tensor_copy(out=vst[:, 4 * half:4 * half + 4, 0:128], in_=ps[:, :].rearrange("p (h d) -> p h d", d=128))) if half == 0 else
                         (lambda e, ps=ps, vst=vst, half=half: e.copy(out=vst[:, 4 * half:4 * half + 4, 0:128], in_=ps[:, :].rearrange("p (h d) -> p h d", d=128))),
                         reads=[t_ps], writes=[t_vst], wadd=True)
                tq = t0 + tl - 1
                vdst = v_meta if t0 == 0 else v_part[tq // 2][(tq % 2) * 128:(tq % 2 + 1) * 128, :]
                S.dma(t_vst, vdst, vst.rearrange("p h t -> p (h t)"), reads=[t_vst], writes=[t_vscr], wadd=True)
            nt = ntl * 128
            kst, t_kst = kstb.next()
            for h in range(8):
                ps, t_ps = nextps()
                mm_acc(ps[:, :nt], t_ps, [(Wkv[:, kc, h * 128:(h + 1) * 128], xT[:, kc, :nt]) for kc in range(KC)], [t_xT, t_Wkv])
                if h % 2 == 0:
                    S.op("dve", lambda e, ps=ps, h=h, kst=kst: e.tensor_copy(out=kst[:, h, :nt], in_=ps[:, :nt]), reads=[t_ps], writes=[t_kst], wadd=True)
                else:
                    S.op("act", lambda e, ps=ps, h=h, kst=kst: e.copy(out=kst[:, h, :nt], in_=ps[:, :nt]), reads=[t_ps], writes=[t_kst], wadd=True)
            if t0 == 0:
                S.dma(t_kst, kt_meta.rearrange("h p t -> p h t"), kst[:, :, :nt], reads=[t_kst], writes=[t_ktscr], wadd=True)
            else:
                for c in range(4):
                    kdst = kt_part[c].rearrange("(h p) t -> p h t", p=128)[:, :, (t0 - 1) * 128:(t0 - 1) * 128 + nt]
                    S.dma(t_kst, kdst, kst[:, 2 * c:2 * c + 2, :nt], reads=[t_kst], writes=[t_ktscr], wadd=True)

        pre = []
        for kc in range(KC):
            nm = pcol[:, PC_NMIX + kc:PC_NMIX + kc + 1]
            pre.append((Wglu[:, kc, :], t_Wglu, w_in[kc * 128:(kc + 1) * 128, 3072:5120], 2048, nm))
        for kc in range(KC):
            nm = pcol[:, PC_NMIX + kc:PC_NMIX + kc + 1]
            pre.append((Wgc[:, kc, :], t_Wgc, w_in[kc * 128:(kc + 1) * 128, 6144:7168], 1024, nm))
        for kc in range(KC):
            pre.append((Wco[:, kc, :], t_Wco, w_co[kc * 128:(kc + 1) * 128, :], 1024, None))
        dg_todo = list(range(8))
        nxt = kv_prep(*blocks[0])
        for bi_, (t0, ntl) in enumerate(blocks):
            cur = nxt
            if bi_ + 1 < len(blocks):
                nxt = kv_prep(*blocks[bi_ + 1])
            for _ in range(5):
                if pre:
                    load_w(*pre.pop(0))
            for _ in range(2):
                if dg_todo:
                    build_diag(dg_todo.pop(0))
            kv_compute(t0, ntl, *cur)
        while pre:
            load_w(*pre.pop(0))
        while dg_todo:
            build_diag(dg_todo.pop(0))
        XNT_pre = (bf(ROFF, 8 * 544).rearrange("p (k t) -> p k t", t=544), S.tile("XNT"))
        r_ = 0
        while r_ < 32 + GQ[0]:
            n_ = min(128, 32 + GQ[0] - r_)
            xt_, t_x_ = xb.next()
            S.dma(t_x_, xt_[:n_, :], x_own[GROW[0] + r_:GROW[0] + r_ + n_, :], writes=[t_x_])
            ss_, t_ss_ = ssb.next()
            xn_, t_xn_ = xnb.next()
            norm_rows(xt_, t_x_, n_, sqj, t_sq, ss_, t_ss_, xn_, t_xn_)
            transpose_rows(xn_, t_xn_, n_, XNT_pre[0], XNT_pre[1], r_, evac="act")
            r_ += n_
        t_ktall, t_vall, t_cc = S.tile("ktall"), S.tile("vall"), S.tile("cc")
        RG = [[0, 1, 2, 3], [4, 5, 6, 7]]
        t_ktall = S.tiles(4, "ktall")
        t_vall = S.tiles(8, "vall")
        for c in range(4):
            S.ccop(t_cc, lambda e, c=c: e.collective_compute("AllGather", ALU.bypass, replica_groups=RG, ins=[kt_part[c]], outs=[kt_all[c]]), reads=[t_ktscr], writes=[t_ktall[c]])
            for c2 in (2 * c, 2 * c + 1):
                S.ccop(t_cc, lambda e, c2=c2: e.collective_compute("AllGather", ALU.bypass, replica_groups=RG, ins=[v_part[c2]], outs=[v_all[c2]]), reads=[t_vscr], writes=[t_vall[c2]])
        S.barrier()

        xnt_pending = []

        def own_xnt_parts(g, xntb, xb, ssb, xnb, sqj, t_sq):
            XNT, t_XNT = xntb.next()
            nrows = 32 + GQ[g]
            r = 0
            tiles = []
            while r < nrows:
                tiles.append((r, min(128, nrows - r)))
                r += tiles[-1][1]
            held = {}

            def part_a(r, n):
                xt, t_x = xb.next()
                S.dma(t_x, xt[:n, :], x_own[GROW[g] + r:GROW[g] + r + n, :], writes=[t_x])
                ss, t_ss = ssb.next()
                xn, t_xn = xnb.next()
                norm_rows(xt, t_x, n, sqj, t_sq, ss, t_ss, xn, t_xn)
                held[r] = (xn, t_xn)

            def part_b(r, n):
                xn, t_xn = held.pop(r)
                transpose_rows(xn, t_xn, n, XNT, t_XNT, r, evac="act")

            order = []
            for i, (r, n) in enumerate(tiles):
                order.append(lambda r=r, n=n: part_a(r, n))
                if i >= 1:
                    pr, pn = tiles[i - 1]
                    order.append(lambda pr=pr, pn=pn: part_b(pr, pn))
            lr, ln_ = tiles[-1]
            order.append(lambda: part_b(lr, ln_))
            xnt_pending.extend(order)
            return XNT, t_XNT

        def xnt_drain(k=99):
            for _ in range(k):
                if xnt_pending:
                    xnt_pending.pop(0)()

        def own_xnt(g, xntb, xb, ssb, xnb, sqj, t_sq):
            res = own_xnt_parts(g, xntb, xb, ssb, xnb, sqj, t_sq)
            xnt_drain()
            return res

        stg.items = STG
        o = ROFF
        XNTB = [XNT_pre]
        o += 4352
        for i in range(1):
            XNTB.append((bf(o, 8 * 544).rearrange("p (k t) -> p k t", t=544), S.tile("XNT"))); o += 4352
        assert o == PW
        o += 32768
        XB = []
        for i in range(2):
            XB.append((f32(o, 1024), S.tile("xb"))); o += 2048
        sqj = bf(o, 1024); o += 1024
        t_sq = S.tile("sq")
        SSB = []
        for i in range(4):
            SSB.append((f32(o, 1), S.tile("ss"))); o += 2
        XN = []
        for i in range(2):
            XN.append((bf(o, 1024), S.tile("xn"))); o += 1024
        UB = []
        for i in range(3):
            UB.append((bf(o, 544), S.tile("u"))); o += 544
        DGB = []
        for i in range(2):
            DGB.append((bf(o, 20 * 128).rearrange("p (k c) -> p k c", c=128), S.tile("dg"))); o += 2560
        SG = []
        for i in range(2):
            SG.append((f32(o, 544), S.tile("sg"))); o += 1088
        CA = []
        for i in range(2):
            CA.append((f32(o, 512), S.tile("ca"))); o += 1024
        cb = bf(o, 4096).rearrange("p (k t) -> p k t", t=512); o += 4096
        t_cb = S.tiles(8, "cb")
        CSQ = []
        for i in range(2):
            CSQ.append((bf(o, 512), S.tile("csq"))); o += 512
        mean = f32(o, 512); o += 1024
        rstd = f32(o, 512); o += 1024
        msq = f32(o, 512); o += 1024
        t_mean, t_rstd, t_msq = S.tile("mean"), S.tile("rstd"), S.tile("msq")
        XH = []
        for i in range(2):
            XH.append((f32(o, 512), S.tile("xh"))); o += 1024
        uu = bf(o, 4096).rearrange("p (k t) -> p k t", t=512); o += 4096
        t_uu = S.tiles(8, "uu")
        GCB = []
        gcs = bf(o, 4096).rearrange("p (k t) -> p k t", t=512); o += 4096
        t_gcs = S.tiles(8, "gcs")
        assert o <= AR, o
        xntb, xb, ssb, xnb, ub, sgb, cab, csqb, xhb, gcb, dgb = Rot(XNTB), Rot(XB), Rot(SSB), Rot(XN), Rot(UB), Rot(SG), Rot(CA), Rot(CSQ), Rot(XH), Rot(GCB), Rot(DGB)

        pslist[0] = list(range(6))

        def conv_group(g, XNT, t_XNT):
            nq = GQ[g]
            nr = 32 + nq
            psum1, t_psum1 = PS[6], PST[6]
            psum2, t_psum2 = PS[7], PST[7]
            NPE = 20

            def stage1(oc):
                u, t_u = ub.next()
                sg, t_sg = sgb.next()
                dg, t_dg = dgb.next()
                wb = PC_CDW + oc * 31
                S.dma(t_dg, dg.rearrange("p k c -> p (k c)"), dg_scr[oc], reads=[t_dgscr], writes=[t_dg])
                for (c0, cn) in ((0, 32), (32, nq)):
                    psa, t_psa = nextps()
                    mm_acc(psa[:, :cn], t_psa, [(Wglu[:, kc, oc * 128:(oc + 1) * 128], XNT[:, kc, c0:c0 + cn]) for kc in range(KC)], [t_XNT, t_Wglu])
                    psg, t_psg = nextps()
                    mm_acc(psg[:, :cn], t_psg, [(Wglu[:, kc, 1024 + oc * 128:1024 + (oc + 1) * 128], XNT[:, kc, c0:c0 + cn]) for kc in range(KC)], [t_XNT, t_Wglu])
                    S.op("act", lambda e, psg=psg, sg=sg, c0=c0, cn=cn: e.activation(out=sg[:, c0:c0 + cn], in_=psg[:, :cn], func=AF.Sigmoid),
                         reads=[t_psg], writes=[t_sg], wadd=True)
                    S.op("dve", lambda e, psa=psa, sg=sg, u=u, c0=c0, cn=cn: e.tensor_tensor(out=u[:, c0:c0 + cn], in0=psa[:, :cn], in1=sg[:, c0:c0 + cn], op=ALU.mult),
                         reads=[t_psa, t_sg], writes=[t_u], wadd=True)
                return u, t_u, dg, t_dg

            def stage2(oc, u, t_u, dg, t_dg):
                wb = PC_CDW + oc * 31
                psc, t_psc = nextps()
                for k in range(NPE):
                    S.op("pe", lambda e, k=k: e.matmul(psc[:, :nq], lhsT=dg[:, k, :], rhs=u[:, 2 + k:2 + k + nq], start=(k == 0), stop=(k == NPE - 1)),
                         reads=[t_u, t_dg], writes=[t_psc], wadd=(k > 0))
                ca, t_ca = cab.next()
                S.op("dve", lambda e: e.tensor_scalar(out=ca[:, :nq], in0=u[:, 2 + NPE:2 + NPE + nq], scalar1=pcol[:, wb + NPE:wb + NPE + 1], scalar2=pcol[:, PC_CDB + oc:PC_CDB + oc + 1], op0=ALU.mult, op1=ALU.add),
                     reads=[t_u, t_const], writes=[t_ca])
                for k in range(NPE + 1, 31):
                    S.op("dve", lambda e, k=k: e.scalar_tensor_tensor(out=ca[:, :nq], in0=u[:, 2 + k:2 + k + nq], scalar=pcol[:, wb + k:wb + k + 1], in1=ca[:, :nq], op0=ALU.mult, op1=ALU.add),
                         reads=[t_u, t_ca, t_const], writes=[t_ca])
                S.op("dve", lambda e: e.tensor_tensor(out=ca[:, :nq], in0=psc[:, :nq], in1=ca[:, :nq], op=ALU.add), reads=[t_psc, t_ca], writes=[t_ca])
                S.op("act", lambda e: e.copy(out=cb[:, oc, :nq], in_=ca[:, :nq]), reads=[t_ca], writes=[t_cb[oc]])
                csq, t_csq = csqb.next()
                S.op("act", lambda e: e.activation(out=csq[:, :nq], in_=ca[:, :nq], func=AF.Square), reads=[t_ca], writes=[t_csq])
                def stats():
                    S.op("pe", lambda e: e.matmul(psum1[:, :nq], lhsT=ones, rhs=cb[:, oc, :nq], start=(oc == 0), stop=(oc == 7)),
                         reads=[t_cb[oc], t_const], writes=[t_psum1], wadd=(oc > 0))
                    S.op("pe", lambda e: e.matmul(psum2[:, :nq], lhsT=ones, rhs=csq[:, :nq], start=(oc == 0), stop=(oc == 7)),
                         reads=[t_csq, t_const], writes=[t_psum2], wadd=(oc > 0))
                return stats

            nx = stage1(0)
            pend_stats = None
            for oc in range(8):
                cu = nx
                if oc < 7:
                    nx = stage1(oc + 1)
                st = stage2(oc, *cu)
                if pend_stats is not None:
                    pend_stats()
                pend_stats = st
                if oc >= 1:
                    xnt_drain(2)
            pend_stats()
            for oc2 in range(8):
                psg, t_psg = nextps()
                mm_acc(psg[:, :nq], t_psg, [(Wgc[:, kc, oc2 * 128:(oc2 + 1) * 128], XNT[:, kc, 32:32 + nq]) for kc in range(KC)], [t_XNT, t_Wgc])
                S.op("act", lambda e, psg=psg, oc2=oc2: e.activation(out=gcs[:, oc2, :nq], in_=psg[:, :nq], func=AF.Sigmoid), reads=[t_psg], writes=[t_gcs[oc2]])
            S.op("act", lambda e: e.activation(out=mean[:, :nq], in_=psum1[:, :nq], func=AF.Identity, scale=1.0 / 1024), reads=[t_psum1], writes=[t_mean])
            S.op("dve", lambda e: e.tensor_tensor(out=msq[:, :nq], in0=mean[:, :nq], in1=mean[:, :nq], op=ALU.mult), reads=[t_mean], writes=[t_msq])
            S.op("dve", lambda e: e.scalar_tensor_tensor(out=rstd[:, :nq], in0=psum2[:, :nq], scalar=1.0 / 1024, in1=msq[:, :nq], op0=ALU.mult, op1=ALU.subtract),
                 reads=[t_psum2, t_msq], writes=[t_rstd])
            S.op("act", lambda e: e.activation(out=rstd[:, :nq], in_=rstd[:, :nq], func=AF.Ln, bias=1e-5, scale=1.0), reads=[t_rstd], writes=[t_rstd])
            S.op("act", lambda e: e.activation(out=rstd[:, :nq], in_=rstd[:, :nq], func=AF.Exp, scale=-0.5), reads=[t_rstd], writes=[t_rstd])
            for oc in range(8):
                xh, t_xh = xhb.next()
                S.op("dve", lambda e, xh=xh, oc=oc: e.tensor_tensor(out=xh[:, :nq], in0=cb[:, oc, :nq], in1=mean[:, :nq], op=ALU.subtract), reads=[t_cb[oc], t_mean], writes=[t_xh])
                S.op("dve", lambda e, xh=xh: e.tensor_tensor(out=xh[:, :nq], in0=xh[:, :nq], in1=rstd[:, :nq], op=ALU.mult), reads=[t_xh, t_rstd], writes=[t_xh])
                S.op("act", lambda e, xh=xh, oc=oc: e.activation(out=uu[:, oc, :nq], in_=xh[:, :nq], func=AF.Silu, scale=pcol[:, PC_LNW + oc:PC_LNW + oc + 1], bias=pcol[:, PC_LNB + oc:PC_LNB + oc + 1]),
                     reads=[t_xh, t_const], writes=[t_uu[oc]])
            for oc2 in range(8):
                psy, t_psy = nextps()
                mm_acc(psy[:, :nq], t_psy, [(Wco[:, oc, oc2 * 128:(oc2 + 1) * 128], uu[:, oc, :nq]) for oc in range(8)], t_uu + [t_Wco])
                S.op("dve", lambda e, psy=psy, oc2=oc2: e.tensor_tensor(out=mT[:, oc2, GOFF[g]:GOFF[g] + nq], in0=psy[:, :nq], in1=gcs[:, oc2, :nq], op=ALU.mult),
                     reads=[t_psy, t_gcs[oc2]], writes=[t_mT[oc2]], wadd=True)

        nxt = XNT_pre
        xntb.i = 1
        for g in range(5):
            cur = nxt
            if g < 4:
                nxt = own_xnt_parts(g + 1, xntb, xb, ssb, xnb, sqj, t_sq)
            conv_group(g, *cur)
            xnt_drain()
        pslist[0] = list(range(8))
        S.barrier()
        t_dbg = S.tile("dbg")
        if DEBUG:
            S.dma(t_dbg, dbg_mc1, bf(MOFF, 8 * NQ), writes=[t_dbg], wadd=True)
            S.dma(t_dbg, dbg_small, small, writes=[t_dbg], wadd=True)
            S.barrier()

        o = ROFF
        QT = bf(o, 8 * NQ).rearrange("p (h t) -> p h t", t=NQ); o += 8 * NQ
        gateT = bf(o, 8 * NQ).rearrange("p (h t) -> p h t", t=NQ); o += 8 * NQ
        t_QT = S.tile("QT")
        t_gateT = S.tile("gateT")
        PERSIST = o
        V0 = (bf(o, NT_ALL * 129).rearrange("p (t d) -> p t d", d=129), S.tile("V")); o += NT_ALL * 129 + 1

        def load_v(h, V, t_V):
            S.dma(t_V, V[:, 0, :], v_meta[:, h * 129:(h + 1) * 129], reads=[t_vscr], writes=[t_V])
            Vv = V[:, 1:65, :].rearrange("p (r c t) d -> p r c t d", r=4, c=8, t=2)
            for c in range(8):
                for r in range(4):
                    S.dma(t_V, Vv[:, r, c, :, :], v_all[c][r * 256:(r + 1) * 256, h * 129:(h + 1) * 129].rearrange("(t p) d -> p t d", p=128), reads=[t_vall[c]], writes=[t_V], wadd=True)

        XNTB = []
        for i in range(2):
            XNTB.append((bf(o, 8 * 544).rearrange("p (k t) -> p k t", t=544), S.tile("XNT"))); o += 4352
        Wq = bf(o, 8192).rearrange("p (k c) -> p k c", c=1024); o += 8192
        Wga = bf(o, 8192).rearrange("p (k c) -> p k c", c=1024); o += 8192
        t_Wq, t_Wga = S.tile("Wq"), S.tile("Wga")
        XB = []
        for i in range(2):
            XB.append((f32(o, 1024), S.tile("xb"))); o += 2048
        sqj = bf(o, 1024); o += 1024
        t_sq = S.tile("sq")
        SSB = []
        for i in range(4):
            SSB.append((f32(o, 1), S.tile("ss"))); o += 2
        XN = []
        for i in range(2):
            XN.append((bf(o, 1024), S.tile("xn"))); o += 1024
        stg_x = []
        for i in range(2):
            stg_x.append((f32(o, 1024), S.tile("stg"))); o += 2048
        assert o <= AR, o
        stg.items = STG + stg_x
        xntb, xb, ssb, xnb = Rot(XNTB), Rot(XB), Rot(SSB), Rot(XN)
        for kc in range(KC):
            nm = pcol[:, PC_NMIX + kc:PC_NMIX + kc + 1]
            load_w(Wq[:, kc, :], t_Wq, w_in[kc * 128:(kc + 1) * 128, 0:1024], 1024, nm)
            load_w(Wga[:, kc, :], t_Wga, w_in[kc * 128:(kc + 1) * 128, 5120:6144], 1024, nm)
        def qg_group(g, XNT, t_XNT, hook=None):
            nq = GQ[g]
            for h in range(8):
                ps, t_ps = nextps()
                mm_acc(ps[:, :nq], t_ps, [(Wq[:, kc, h * 128:(h + 1) * 128], XNT[:, kc, 32:32 + nq]) for kc in range(KC)], [t_XNT, t_Wq])
                if h % 2 == 0:
                    S.op("dve", lambda e, ps=ps, h=h, g=g, nq=nq: e.tensor_copy(out=QT[:, h, GOFF[g]:GOFF[g] + nq], in_=ps[:, :nq]), reads=[t_ps], writes=[t_QT], wadd=True)
                else:
                    S.op("act", lambda e, ps=ps, h=h, g=g, nq=nq: e.copy(out=QT[:, h, GOFF[g]:GOFF[g] + nq], in_=ps[:, :nq]), reads=[t_ps], writes=[t_QT], wadd=True)
                if h >= 1:
                    xnt_drain(2)
            if hook is not None:
                xnt_drain()
                hook()
            for h in range(8):
                ps, t_ps = nextps()
                mm_acc(ps[:, :nq], t_ps, [(Wga[:, kc, h * 128:(h + 1) * 128], XNT[:, kc, 32:32 + nq]) for kc in range(KC)], [t_XNT, t_Wga])
                S.op("act", lambda e, ps=ps, h=h: e.activation(out=gateT[:, h, GOFF[g]:GOFF[g] + nq], in_=ps[:, :nq], func=AF.Sigmoid),
                     reads=[t_ps], writes=[t_gateT], wadd=True)

        nxt = own_xnt(0, xntb, xb, ssb, xnb, sqj, t_sq)
        for g in range(5):
            cur = nxt
            if g < 4:
                nxt = own_xnt_parts(g + 1, xntb, xb, ssb, xnb, sqj, t_sq)
            qg_group(g, *cur, hook=((lambda: load_v(0, *V0)) if g == 3 else None))
            xnt_drain()
        S.barrier()
        if DEBUG:
            S.dma(t_dbg, dbg_qt, bf(ROFF, 8 * NQ), writes=[t_dbg], wadd=True)
            S.dma(t_dbg, dbg_gate, bf(ROFF + 8 * NQ, 8 * NQ), writes=[t_dbg], wadd=True)
            S.barrier()

        stg.items = STG
        o = PERSIST + NT_ALL * 129 + 1
        KTB, VB = [], [V0]
        for i in range(2):
            KTB.append((bf(o, NKEY), S.tile("KT"))); o += NKEY
        for i in range(1):
            VB.append((bf(o, NT_ALL * 129).rearrange("p (t d) -> p t d", d=129), S.tile("V"))); o += NT_ALL * 129 + 1
        PTB = []
        for i in range(6):
            PTB.append((bf(o, 512), S.tile("PT"))); o += 512
        cq = bf(o, 2048); o += 2048
        t_cq = S.tile("cq")
        RLB, OAB = [], []
        for i in range(2):
            RLB.append((f32(o, 512), S.tile("rl"))); o += 1024
        for i in range(2):
            OAB.append((f32(o, 512), S.tile("oa"))); o += 1024
        tt = f32(o, 512); o += 1024
        t_tt = S.tile("tt")
        o2r = f32(o, 512); o += 1024
        t_o2r = S.tile("o2r")
        sqb = bf(o, 512); o += 512
        t_sqb = S.tile("sqb")
        rsb = f32(o, 512); o += 1024
        t_rsb = S.tile("rsb")
        assert o <= AR, o
        ptb, rlb, oab = Rot(PTB), Rot(RLB), Rot(OAB)
        S.dma(t_cq, cq, cq_d, writes=[t_cq])
        ck = prow[:, PR_CK:PR_CK + 64]
        subw = pcol[:, PC_SUB:PC_SUB + 1]
        srot = [0]
        SK = 3

        def att_head(h, KT, t_KT, V, t_V):
            units = []
            for g in range(5):
                kbl = [(0, 16, 0, False, -1)]
                if g < 4:
                    for m in range(16 * g):
                        kbl.append(((m + 1) * 128, 128, m + 1, False, m))
                    for m in range(16 * g, 16 * g + 16):
                        kbl.append(((m + 1) * 128, 128, m + 1, True, m))
                else:
                    for m in range(64):
                        kbl.append(((m + 1) * 128, 128, m + 1, False, m))
                for bi, kb in enumerate(kbl):
                    units.append((g, bi == 0, bi == len(kbl) - 1) + kb)
            pts = {}
            deferred = []
            state = {}

            def emit_scores(idx):
                g, first, last, kc0, nk, vt, masked, m = units[idx]
                nq = GQ[g]
                banks = []
                for mp in range(2):
                    sb = srot[0] % 4
                    srot[0] += 1
                    pss, t_pss = PS[sb], PST[sb]
                    banks.append((pss, t_pss))
                    S.op("pe", lambda e, mp=mp, pss=pss: e.matmul(pss[:nk, :nq], lhsT=KT[64 * mp:64 * mp + 64, kc0:kc0 + nk], rhs=QT[64 * mp:64 * mp + 64, h, GOFF[g]:GOFF[g] + nq], start=True, stop=True),
                         reads=[t_KT, t_QT], writes=[t_pss])
                res = []
                for mp in range(2):
                    pss, t_pss = banks[mp]
                    pt, t_pt = ptb.next()
                    S.op("act", lambda e, pss=pss, pt=pt: e.activation(out=pt[:nk, :nq], in_=pss[:nk, :nq], func=AF.Exp, scale=0.125), reads=[t_pss], writes=[t_pt])
                    if masked:
                        S.op("dve", lambda e, pt=pt: e.scalar_tensor_tensor(out=pt[:, :nq], in0=cq[:, g * 512:g * 512 + nq], scalar=ck[:, m:m + 1], in1=pt[:, :nq], op0=ALU.is_ge, op1=ALU.mult),
                             reads=[t_pt, t_cq, t_const], writes=[t_pt])
                    res.append((pt, t_pt))
                pts[idx] = res

            def evac0(g):
                nq = GQ[g]
                rl, t_rl = rlb.next()
                oa, t_oa = oab.next()
                state[g] = (oa, t_oa)
                S.op("act", lambda e: e.copy(out=oa[:, :nq], in_=PS[4][:, :nq]), reads=[PST[4]], writes=[t_oa])
                S.op("act", lambda e: e.activation(out=rl[:, :nq], in_=PS[5][:, :nq], func=AF.Ln), reads=[PST[5]], writes=[t_rl])
                S.op("act", lambda e: e.activation(out=rl[:, :nq], in_=rl[:, :nq], func=AF.Exp, scale=-1.0), reads=[t_rl], writes=[t_rl])
                S.op("dve", lambda e: e.tensor_tensor(out=oa[:, :nq], in0=oa[:, :nq], in1=rl[:, :nq], op=ALU.mult), reads=[t_oa, t_rl], writes=[t_oa])

            def evac1(g):
                nq = GQ[g]
                rl, t_rl = rlb.next()
                oa, t_oa = state[g]
                S.op("act", lambda e: e.copy(out=o2r[:, :nq], in_=PS[6][:, :nq]), reads=[PST[6]], writes=[t_o2r])
                S.op("act", lambda e: e.activation(out=rl[:, :nq], in_=PS[7][:, :nq], func=AF.Ln), reads=[PST[7]], writes=[t_rl])
                S.op("act", lambda e: e.activation(out=rl[:, :nq], in_=rl[:, :nq], func=AF.Exp, scale=-1.0), reads=[t_rl], writes=[t_rl])
                S.op("dve", lambda e: e.tensor_tensor(out=tt[:, :nq], in0=o2r[:, :nq], in1=rl[:, :nq], op=ALU.mult), reads=[t_o2r, t_rl], writes=[t_tt])
                S.op("dve", lambda e: e.scalar_tensor_tensor(out=oa[:, :nq], in0=tt[:, :nq], scalar=nlam, in1=oa[:, :nq], op0=ALU.mult, op1=ALU.add),
                     reads=[t_tt, t_oa, t_small], writes=[t_oa])
                S.op("act", lambda e: e.activation(out=sqb[:, :nq], in_=oa[:, :nq], func=AF.Square), reads=[t_oa], writes=[t_sqb])

            def evac2(g):
                nq = GQ[g]
                oa, t_oa = state[g]
                sb = srot[0] % 4
                srot[0] += 1
                pss, t_pss = PS[sb], PST[sb]
                S.op("pe", lambda e: e.matmul(pss[:, :nq], lhsT=ones, rhs=sqb[:, :nq], start=True, stop=True), reads=[t_sqb, t_const], writes=[t_pss])
                S.op("act", lambda e: e.activation(out=rsb[:, :nq], in_=pss[:, :nq], func=AF.Ln, bias=1e-6 / 0.64, scale=1.0 / (128 * 0.64)), reads=[t_pss], writes=[t_rsb])
                S.op("act", lambda e: e.activation(out=rsb[:, :nq], in_=rsb[:, :nq], func=AF.Exp, scale=-0.5), reads=[t_rsb], writes=[t_rsb])
                S.op("dve", lambda e: e.tensor_tensor(out=tt[:, :nq], in0=oa[:, :nq], in1=rsb[:, :nq], op=ALU.mult), reads=[t_oa, t_rsb], writes=[t_tt])
                S.op("dve", lambda e: e.scalar_tensor_tensor(out=tt[:, :nq], in0=tt[:, :nq], scalar=subw, in1=gateT[:, h, GOFF[g]:GOFF[g] + nq], op0=ALU.mult, op1=ALU.mult),
                     reads=[t_tt, t_gateT, t_const], writes=[t_tt])
                S.op("dve", lambda e: e.tensor_tensor(out=mT[:, h, GOFF[g]:GOFF[g] + nq], in0=tt[:, :nq], in1=mT[:, h, GOFF[g]:GOFF[g] + nq], op=ALU.add),
                     reads=[t_tt, t_mT[h]], writes=[t_mT[h]], wadd=True)

            def emit_pv(idx):
                g, first, last, kc0, nk, vt, masked, m = units[idx]
                nq = GQ[g]
                res = pts.pop(idx)
                for mp in range(2):
                    pt, t_pt = res[mp]
                    bo, bl = (4, 5) if mp == 0 else (6, 7)
                    S.op("pe", lambda e, pt=pt, bo=bo: e.matmul(PS[bo][:, :nq], lhsT=V[:nk, vt, 0:128], rhs=pt[:nk, :nq], start=first, stop=last),
                         reads=[t_pt, t_V], writes=[PST[bo]], wadd=(not first))
                    S.op("pe", lambda e, pt=pt, bl=bl: e.matmul(PS[bl][:, :nq], lhsT=ones[:nk, :], rhs=pt[:nk, :nq], start=first, stop=last),
                         reads=[t_pt, t_const], writes=[PST[bl]], wadd=(not first))
                if last:
                    evac0(g)
                    evac1(g)
                    deferred.append((idx + 3, g))

            n = len(units)
            SKL = 1
            idx = 0
            while idx < n + SKL or deferred:
                if idx < n:
                    emit_scores(idx)
                if 0 <= idx - SKL < n:
                    emit_pv(idx - SKL)
                while deferred and deferred[0][0] <= idx:
                    evac2(deferred.pop(0)[1])
                idx += 1

        for h in range(8):
            KT, t_KT = KTB[h % 2]
            V, t_V = VB[h % 2]
            S.dma(t_KT, KT[:, 0:128], kt_meta[h], reads=[t_ktscr], writes=[t_KT])
            for r in range(4):
                S.dma(t_KT, KT[:, 128 + r * 2048:128 + (r + 1) * 2048], kt_all[h // 2][r * 256 + (h % 2) * 128:r * 256 + (h % 2 + 1) * 128, :], reads=[t_ktall[h // 2]], writes=[t_KT], wadd=True)
            if h > 0:
                load_v(h, V, t_V)
            att_head(h, KT, t_KT, V, t_V)
        S.barrier()
        if DEBUG:
            S.dma(t_dbg, dbg_mc2, bf(MOFF, 8 * NQ), writes=[t_dbg], wadd=True)
            S.barrier()

        o = ROFF
        acc_all = f32(o, 17 * 1024).rearrange("p (t c) -> p t c", c=1024); o += 17 * 2048
        t_accs = S.tiles(17, "acc")
        n2T = bf(o, 8 * NQ).rearrange("p (k t) -> p k t", t=NQ); o += 8 * NQ
        t_n2T = S.tile("n2T")
        F2FREE = o
        Wo = bf(o, 8192).rearrange("p (k c) -> p k c", c=1024); o += 8192
        t_Wo = S.tile("Wo")
        XB = []
        for i in range(2):
            XB.append((f32(o, 1024), S.tile("xr"))); o += 2048
        XN = []
        for i in range(2):
            XN.append((bf(o, 1024), S.tile("n2"))); o += 1024
        sqj = bf(o, 1024); o += 1024
        t_sq = S.tile("sq")
        hs = f32(o, 1024); o += 2048
        t_hs = S.tile("hs")
        SSB = []
        for i in range(4):
            SSB.append((f32(o, 1), S.tile("ss"))); o += 2
        stg_y = []
        for i in range(2):
            stg_y.append((f32(o, 1024), S.tile("stg"))); o += 2048
        assert o <= AR, o
        stg.items = STG + stg_y
        xb, xnb, ssb = Rot(XB), Rot(XN), Rot(SSB)
        for kc in range(KC):
            load_w(Wo[:, kc, :], t_Wo, w_o[kc * 128:(kc + 1) * 128, :], 1024, None)

        def out_tiles(g):
            if g < 4:
                return [(0, 2, None), (2, 128, 4 * g), (130, 128, 4 * g + 1), (258, 128, 4 * g + 2), (386, 126, 4 * g + 3)]
            return [(0, 2, None), (2, 32, 16)]

        def f1_a(g, q0, n, ai):
            xr, t_xr = xb.next()
            S.dma(t_xr, xr[:n, :], x_own[GROW[g] + 32 + q0:GROW[g] + 32 + q0 + n, :], writes=[t_xr])
            if ai is None:
                hm, t_hm = hs, t_hs
            else:
                hm, t_hm = acc_all[:, ai, :], t_accs[ai]
            for half in range(2):
                ps, t_ps = nextps()
                mm_acc(ps[:n, :], t_ps, [(mT[:, c, GOFF[g] + q0:GOFF[g] + q0 + n], Wo[:, c, half * 512:(half + 1) * 512]) for c in range(8)], t_mT + [t_Wo])
                S.op("dve", lambda e, ps=ps, hm=hm, xr=xr, n=n, half=half: e.tensor_tensor(out=hm[:n, half * 512:(half + 1) * 512], in0=ps[:n, :], in1=xr[:n, half * 512:(half + 1) * 512], op=ALU.add),
                     reads=[t_ps, t_xr], writes=[t_hm], wadd=True)
            ss, t_ss = ssb.next()
            n2, t_n2 = xnb.next()
            norm_rows(hm, t_hm, n, sqj, t_sq, ss, t_ss, n2, t_n2)
            return (g, q0, n, n2, t_n2)

        def f1_b(g, q0, n, n2, t_n2):
            transpose_rows(n2, t_n2, n, n2T, t_n2T, GOFF[g] + q0, evac="act")

        prev = None
        for g in range(5):
            for (q0, n, ai) in out_tiles(g):
                cur = f1_a(g, q0, n, ai)
                if prev is not None:
                    f1_b(*prev)
                prev = cur
        f1_b(*prev)
        S.barrier()
        if DEBUG:
            S.dma(t_dbg, dbg_acc, f32(ROFF, 17 * 1024), writes=[t_dbg], wadd=True)
            S.dma(t_dbg, dbg_n2t, bf(ROFF + 17 * 2048, 8 * NQ), writes=[t_dbg], wadd=True)
            S.barrier()

        o = F2FREE
        WB = []
        for i, base in enumerate((MOFF, o)):
            WB.append(dict(Wup=bf(base, 8192).rearrange("p (k c) -> p k c", c=1024), Wdn=bf(base + 8192, 4096).rearrange("p (k c) -> p k c", c=1024),
                           t_Wup=S.tile("Wup"), t_Wdn=S.tile("Wdn")))
        o += 12288
        HH = []
        for i in range(2):
            HH.append((bf(o, 2048).rearrange("p (k t) -> p k t", t=512), S.tiles(4, "hh"))); o += 2048
        TG = []
        for i in range(2):
            TG.append((f32(o, 512), S.tile("tg"))); o += 1024
        TV = []
        for i in range(2):
            TV.append((f32(o, 512), S.tile("tv"))); o += 1024
        nfw = prow[:, PR_NF:PR_NF + 1024]
        OB = []
        for i in range(2):
            OB.append((f32(o, 1024), S.tile("ob"))); o += 2048
        sqj = bf(o, 1024); o += 1024
        t_sq = S.tile("sq")
        SSB = []
        for i in range(4):
            SSB.append((f32(o, 1), S.tile("ss"))); o += 2
        stg3 = (f32(o, 1024), S.tile("stg")); o += 2048
        assert o <= AR, o
        hhb, tgb, tvb, obb, ssb = Rot(HH), Rot(TG), Rot(TV), Rot(OB), Rot(SSB)
        stg.items = STG + [stg3]
        t_out = S.tile("outd")
        def f2_group(pi, f0, nf, g, wb):
            Wup, Wdn, t_Wup, t_Wdn = wb["Wup"], wb["Wdn"], wb["t_Wup"], wb["t_Wdn"]
            nq = GQ[g]
            nn = nq - 2
            hh, t_hh = hhb.next()
            for fl in range(nf):
                fc = f0 + fl
                res = []
                for which in range(2):
                    ps, t_ps = nextps()
                    mm_acc(ps[:, :nq], t_ps, [(Wup[:, kc, which * 512 + fl * 128:which * 512 + (fl + 1) * 128], n2T[:, kc, GOFF[g]:GOFF[g] + nq]) for kc in range(KC)], [t_n2T, t_Wup])
                    ch = which * 22 + fc
                    tb, t_tb = (tgb if which == 0 else tvb).next()
                    wcol = PC_FDW + ch * 3
                    S.op("act", lambda e, ps=ps, tb=tb, wcol=wcol, ch=ch, nn=nn: e.activation(out=tb[:, :nn], in_=ps[:, 0:nn], func=AF.Identity, scale=pcol[:, wcol:wcol + 1], bias=pcol[:, PC_FDB + ch:PC_FDB + ch + 1]),
                         reads=[t_ps, t_const], writes=[t_tb])
                    for k in (1, 2):
                        S.op("dve", lambda e, ps=ps, tb=tb, wcol=wcol, k=k, nn=nn: e.scalar_tensor_tensor(out=tb[:, :nn], in0=ps[:, k:k + nn], scalar=pcol[:, wcol + k:wcol + k + 1], in1=tb[:, :nn], op0=ALU.mult, op1=ALU.add),
                             reads=[t_ps, t_tb, t_const], writes=[t_tb])
                    res.append((tb, t_tb))
                (tg, t_tg), (tv, t_tv) = res
                S.op("act", lambda e, tg=tg, nn=nn: e.activation(out=tg[:, :nn], in_=tg[:, :nn], func=AF.Silu), reads=[t_tg], writes=[t_tg])
                S.op("dve", lambda e, tg=tg, tv=tv, hh=hh, fl=fl, nn=nn: e.tensor_tensor(out=hh[:, fl, :nn], in0=tg[:, :nn], in1=tv[:, :nn], op=ALU.mult),
                     reads=[t_tg, t_tv], writes=[t_hh[fl]])
            return hh, t_hh

        def f2_down(pi, f0, nf, g, wb, hh, t_hh):
            Wup, Wdn, t_Wup, t_Wdn = wb["Wup"], wb["Wdn"], wb["t_Wup"], wb["t_Wdn"]
            nq = GQ[g]
            for (q0, n, ai) in out_tiles(g):
                if ai is None:
                    continue
                for half in range(2):
                    ps, t_ps = nextps()
                    mm_acc(ps[:n, :], t_ps, [(hh[:, fl, q0 - 2:q0 - 2 + n], Wdn[:, fl, half * 512:(half + 1) * 512]) for fl in range(nf)], t_hh[:nf] + [t_Wdn])
                    S.op("dve", lambda e, ps=ps, ai=ai, n=n, half=half: e.tensor_tensor(out=acc_all[:n, ai, half * 512:(half + 1) * 512], in0=ps[:n, :], in1=acc_all[:n, ai, half * 512:(half + 1) * 512], op=ALU.add),
                         reads=[t_ps, t_accs[ai]], writes=[t_accs[ai]], wadd=True)
                if pi == NPASS - 1:
                    ss, t_ss = ssb.next()
                    ob, t_ob = obb.next()
                    S.op("act", lambda e, ai=ai, n=n, ss=ss: e.activation(out=sqj[:n, :], in_=acc_all[:n, ai, :], func=AF.Square, accum_out=ss[:n, :]),
                         reads=[t_accs[ai]], writes=[t_sq, t_ss])
                    S.op("act", lambda e, ss=ss, n=n: e.activation(out=ss[:n, :], in_=ss[:n, :], func=AF.Sqrt, bias=1e-6, scale=1.0 / 1024), reads=[t_ss], writes=[t_ss])
                    S.op("dve", lambda e, ss=ss, n=n: e.reciprocal(out=ss[:n, :], in_=ss[:n, :]), reads=[t_ss], writes=[t_ss])
                    S.op("dve", lambda e, ai=ai, n=n, ss=ss, ob=ob: e.scalar_tensor_tensor(out=ob[:n, :], in0=acc_all[:n, ai, :], scalar=ss[:n, 0:1], in1=nfw[:n, :], op0=ALU.mult, op1=ALU.mult),
                         reads=[t_accs[ai], t_ss, t_const], writes=[t_ob])
                    orow = 510 * g + (q0 - 2)
                    S.dma(t_ob, out_d[orow:orow + n, :], ob[:n, :], reads=[t_ob], writes=[t_out], wadd=True)

        passes = [(0, 4), (4, 4), (8, 4), (12, 4), (16, 4), (20, 2)]
        NPASS = len(passes)

        def pass_loads(pi):
            f0, nf = passes[pi]
            wb = WB[pi % 2]
            res = []
            for kc in range(KC):
                nm = pcol[:, PC_NFFN + kc:PC_NFFN + kc + 1]
                res.append((wb["Wup"][:, kc, 0:nf * 128], wb["t_Wup"], w_up[kc * 128:(kc + 1) * 128, f0 * 128:(f0 + nf) * 128], nf * 128, nm))
                res.append((wb["Wup"][:, kc, 512:512 + nf * 128], wb["t_Wup"], w_up[kc * 128:(kc + 1) * 128, 2816 + f0 * 128:2816 + (f0 + nf) * 128], nf * 128, nm))
            for fl in range(nf):
                res.append((wb["Wdn"][:, fl, :], wb["t_Wdn"], w_dn[(f0 + fl) * 128:(f0 + fl + 1) * 128, :], 1024, None))
            return res

        for a in pass_loads(0):
            load_w(*a)
        for pi, (f0, nf) in enumerate(passes):
            pend = pass_loads(pi + 1) if pi + 1 < NPASS else []
            per = (len(pend) + 3) // 4
            prev = None
            for g in range(5):
                cur = (g,) + f2_group(pi, f0, nf, g, WB[pi % 2])
                if prev is not None:
                    f2_down(pi, f0, nf, prev[0], WB[pi % 2], prev[1], prev[2])
                prev = cur
                for _ in range(per):
                    if pend:
                        load_w(*pend.pop(0))
            f2_down(pi, f0, nf, prev[0], WB[pi % 2], prev[1], prev[2])
            while pend:
                load_w(*pend.pop(0))
        S.barrier(final=True)
        S.emit()
    nc._sched_stats = S.stats
    return nc


def _host_inputs(inputs):
    f = np.float32
    x = np.asarray(inputs["x"], f)
    meta = np.asarray(inputs["meta_tokens"], f)
    B = x.shape[0]
    pcol = np.zeros((128, NPC), f)

    def colmaj(v, nchunk):
        return np.ascontiguousarray(np.asarray(v, f).reshape(nchunk, 128).T)

    pcol[:, PC_NMIX:PC_NMIX + 8] = colmaj(inputs["norm_mix_w"][0], 8)
    pcol[:, PC_NFFN:PC_NFFN + 8] = colmaj(inputs["norm_ffn_w"][0], 8)
    cdw = np.asarray(inputs["conv_dw_w"][0], f)
    pcol[:, PC_CDW:PC_CDW + 248] = cdw.T.reshape(8, 128, 31).transpose(1, 0, 2).reshape(128, 248)
    pcol[:, PC_CDB:PC_CDB + 8] = colmaj(inputs["conv_dw_b"][0], 8)
    pcol[:, PC_LNW:PC_LNW + 8] = colmaj(inputs["conv_ln_w"][0], 8)
    pcol[:, PC_LNB:PC_LNB + 8] = colmaj(inputs["conv_ln_b"][0], 8)
    fdw = np.asarray(inputs["ffn_dw_w"][0], f)
    pcol[:, PC_FDW:PC_FDW + 132] = fdw.T.reshape(44, 128, 3).transpose(1, 0, 2).reshape(128, 132)
    pcol[:, PC_FDB:PC_FDB + 44] = colmaj(inputs["ffn_dw_b"][0], 44)
    pcol[:, PC_SUB] = np.asarray(inputs["subln_w"][0], f)
    prow = np.zeros((128, NPR), f)
    prow[:, PR_NF:PR_NF + 1024] = np.asarray(inputs["norm_final_w"], f)[None, :]
    prow[:, PR_SUB:PR_SUB + 128] = np.asarray(inputs["subln_w"][0], f)[None, :]
    prow[:, PR_LQ1:PR_LQ1 + 64] = np.asarray(inputs["lambda_q1"][0], f)[None, :]
    prow[:, PR_LK1:PR_LK1 + 64] = np.asarray(inputs["lambda_k1"][0], f)[None, :]
    prow[:, PR_LQ2:PR_LQ2 + 64] = np.asarray(inputs["lambda_q2"][0], f)[None, :]
    prow[:, PR_LK2:PR_LK2 + 64] = np.asarray(inputs["lambda_k2"][0], f)[None, :]
    kk = np.arange(128)[:, None]
    mm = np.arange(64)[None, :]
    prow[:, PR_CK:PR_CK + 64] = (2 * mm + (kk >= 64)).astype(f)
    ident = np.eye(128).astype(ml_dtypes.bfloat16)
    shared = dict(
        w_in=np.ascontiguousarray(np.asarray(inputs["w_in"][0], f)),
        w_co=np.ascontiguousarray(np.asarray(inputs["w_conv_out"][0], f)),
        w_o=np.ascontiguousarray(np.asarray(inputs["w_out"][0], f)),
        w_up=np.ascontiguousarray(np.asarray(inputs["w_up"][0], f)),
        w_dn=np.ascontiguousarray(np.asarray(inputs["w_down"][0], f)),
        pcol=pcol, prow=prow, ident=ident)
    in_maps = []
    for c in range(8):
        b, j = c // 4, c % 4
        xa = np.zeros((17 * 128, D), f)
        xa[0:16] = meta
        xa[128:] = x[b, 2048 * j:2048 * (j + 1)]
        seq = np.concatenate([meta, x[b]], axis=0)
        xo = np.zeros((NROWS, D), f)
        cq = np.zeros((128, 2048), ml_dtypes.bfloat16)
        for g in range(5):
            if g < 4:
                G = 4 * g + j
                p0 = 510 * G - 18
            else:
                p0 = 8142
            nr = 32 + GQ[g]
            ps = np.arange(p0, p0 + nr)
            valid = ps >= 0
            xo[GROW[g] + np.nonzero(valid)[0]] = seq[ps[valid]]
            if g < 4:
                tq = ps[32:] - 16
                cqv = np.where(tq >= 0, tq // 64, -1).astype(ml_dtypes.bfloat16)
                cq[:, g * 512:(g + 1) * 512] = cqv[None, :]
        d = dict(shared)
        d.update(x_all=xa, x_own=xo, cq=cq)
        in_maps.append(d)
    return in_maps


_NC = None


def kernel(**inputs):
    global _NC
    in_maps = _host_inputs(inputs)
    if _NC is None:
        _NC = build_program()
    res = run_bass_kernel_spmd(_NC, in_maps, core_ids=list(range(8)))
    B = 2
    out = np.zeros((B, 8192, D), np.float32)
    for c in range(8):
        b, j = c // 4, c % 4
        oo = res.results[c]["out_own"]
        for g in range(4):
            G = 4 * g + j
            out[b, 510 * G:510 * G + 510] = oo[510 * g:510 * g + 510]
        if j == 0:
            out[b, 8160:8192] = oo[2040:2072]
    return out
```

```python
import numpy as np
import ml_dtypes
import concourse.bass as bass
import concourse.mybir as mybir
from concourse.bass_utils import run_bass_kernel_spmd

F32 = mybir.dt.float32
BF16 = mybir.dt.bfloat16
ALU = mybir.AluOpType
AF = mybir.ActivationFunctionType
AX = mybir.AxisListType

D = 1024
KC = 8
NT_ALL = 65
NKEY = NT_ALL * 128
GQ = [512, 512, 512, 512, 34]
GOFF = [0, 512, 1024, 1536, 2048]
NQ = 2082
GROW = [0, 544, 1088, 1632, 2176]
NROWS = 2242
NOUT = 2072
AR = 106400
DEBUG = False


class T:
    __slots__ = ("name", "w", "r", "dkey")

    def __init__(self, name):
        self.name = name
        self.w = {}
        self.r = {}
        self.dkey = None


class Sched:
    CE = ("pe", "act", "dve", "pool")

    def __init__(self, nc):
        self.nc = nc
        self.q = {e: [] for e in ("pe", "act", "dve", "pool", "sp")}
        self.cnt = {}
        self.known = {e: {} for e in self.q}
        self.sems = {}
        self.ntile = 0
        for e in self.CE:
            self.cnt[e] = 0
            self.sems[e] = nc.alloc_semaphore(name="s_" + e)

    def tile(self, name=None):
        self.ntile += 1
        return T((name or "t") + "_%d" % self.ntile)

    def tiles(self, n, name=None):
        return [self.tile(name) for _ in range(n)]

    def _needs(self, e, reads, writes):
        needs = {}
        for t in reads:
            for k, v in t.w.items():
                if needs.get(k, 0) < v:
                    needs[k] = v
        for t in writes:
            for d in (t.w, t.r):
                for k, v in d.items():
                    if needs.get(k, 0) < v:
                        needs[k] = v
        kn = self.known[e]
        for k, v in needs.items():
            if k == "pe" and e == "pe":
                continue
            if kn.get(k, 0) < v:
                self.q[e].append(("w", k, v))
                kn[k] = v

    def _mark(self, k, v, reads, writes, wadd):
        for t in reads:
            if t.r.get(k, 0) < v:
                t.r[k] = v
        for t in writes:
            if wadd:
                t.w[k] = v
            else:
                t.w = {k: v}
                t.r = {}

    def op(self, e, fn, reads=(), writes=(), wadd=False):
        self._needs(e, reads, writes)
        self.cnt[e] += 1
        v = self.cnt[e]
        self.q[e].append(("o", fn, e, 1))
        self._mark(e, v, reads, writes, wadd)

    def dma(self, semt, out_ap, in_ap, reads=(), writes=(), wadd=False, q="sp"):
        if semt.dkey is None:
            semt.dkey = "d_" + semt.name
            self.cnt[semt.dkey] = 0
            self.sems[semt.dkey] = self.nc.alloc_semaphore(name=semt.dkey)
        k = semt.dkey
        self._needs(q, reads, writes)
        self.cnt[k] += 16
        v = self.cnt[k]
        self.q[q].append(("o", lambda eng: eng.dma_start(out=out_ap, in_=in_ap), k, 16))
        self._mark(k, v, reads, writes, wadd)

    def ccop(self, semt, fn, reads=(), writes=()):
        if semt.dkey is None:
            semt.dkey = "c_" + semt.name
            self.cnt[semt.dkey] = 0
            self.sems[semt.dkey] = self.nc.alloc_semaphore(name=semt.dkey)
        k = semt.dkey
        self._needs("pool", reads, writes)
        self.cnt[k] += 1
        v = self.cnt[k]
        self.q["pool"].append(("o", fn, k, 1))
        self._mark(k, v, reads, writes, False)

    def barrier(self, final=False):
        for e in self.q:
            kn = self.known[e]
            for k, v in self.cnt.items():
                if k == e or v == 0 or (k.startswith("c_") and not final):
                    continue
                if kn.get(k, 0) < v:
                    self.q[e].append(("w", k, v))
                    kn[k] = v

    def emit(self):
        nc = self.nc
        self.stats = {e: (len(v), sum(1 for it in v if it[0]=='w')) for e, v in self.q.items()}

        def run(e, eng):
            for it in self.q[e]:
                if it[0] == "w":
                    eng.wait_ge(self.sems[it[1]], it[2])
                else:
                    it[1](eng).then_inc(self.sems[it[2]], it[3])

        with nc.Block() as block:
            @block.tensor
            def _(eng):
                run("pe", eng)

            @block.scalar
            def _(eng):
                run("act", eng)

            @block.vector
            def _(eng):
                run("dve", eng)

            @block.gpsimd
            def _(eng):
                run("pool", eng)

            @block.sync
            def _(eng):
                run("sp", eng)


class Rot:
    def __init__(self, items):
        self.items = items
        self.i = 0

    def next(self):
        it = self.items[self.i % len(self.items)]
        self.i += 1
        return it


PC_NMIX, PC_NFFN, PC_CDW, PC_CDB, PC_LNW, PC_LNB, PC_FDW, PC_FDB, PC_SUB, NPC = 0, 8, 16, 264, 272, 280, 288, 420, 464, 466
PR_NF, PR_SUB, PR_LQ1, PR_LK1, PR_LQ2, PR_LK2, PR_CK, NPR = 0, 1024, 1152, 1216, 1280, 1344, 1408, 1472


def build_program():
    nc = bass.Bass("TRN2", target_bir_lowering=False)

    def din(name, shape, dt=F32):
        return nc.dram_tensor(name, shape, dt, kind="ExternalInput").ap()

    x_all = din("x_all", [17 * 128, D])
    x_own = din("x_own", [NROWS, D])
    w_in = din("w_in", [D, 7168])
    w_co = din("w_co", [D, D])
    w_o = din("w_o", [D, D])
    w_up = din("w_up", [D, 5632])
    w_dn = din("w_dn", [2816, D])
    pcol_d = din("pcol", [128, NPC])
    prow_d = din("prow", [128, NPR])
    cq_d = din("cq", [128, 2048], BF16)
    ident_d = din("ident", [128, 128], BF16)
    out_d = nc.dram_tensor("out_own", [NOUT, D], F32, kind="ExternalOutput").ap()
    kt_part = [nc.dram_tensor("kt_part%d" % c, [256, 2048], BF16).ap() for c in range(4)]
    kt_all = [nc.dram_tensor("kt_all%d" % c, [1024, 2048], BF16).ap() for c in range(4)]
    kt_meta = nc.dram_tensor("kt_meta", [8, 128, 128], BF16).ap()
    v_part = [nc.dram_tensor("v_part%d" % c, [256, 8 * 129], BF16).ap() for c in range(8)]
    v_all = [nc.dram_tensor("v_all%d" % c, [1024, 8 * 129], BF16).ap() for c in range(8)]
    v_meta = nc.dram_tensor("v_meta", [128, 8 * 129], BF16).ap()
    NPE_G = 20
    dg_scr = nc.dram_tensor("dg_scr", [8, 128, NPE_G * 128], BF16).ap()
    if DEBUG:
        dbg_mc1 = nc.dram_tensor("dbg_mc1", [128, 8 * NQ], BF16, kind="ExternalOutput").ap()
        dbg_mc2 = nc.dram_tensor("dbg_mc2", [128, 8 * NQ], BF16, kind="ExternalOutput").ap()
        dbg_qt = nc.dram_tensor("dbg_qt", [128, 8 * NQ], BF16, kind="ExternalOutput").ap()
        dbg_gate = nc.dram_tensor("dbg_gate", [128, 8 * NQ], BF16, kind="ExternalOutput").ap()
        dbg_acc = nc.dram_tensor("dbg_acc", [128, 17 * 1024], F32, kind="ExternalOutput").ap()
        dbg_n2t = nc.dram_tensor("dbg_n2t", [128, 8 * NQ], BF16, kind="ExternalOutput").ap()
        dbg_small = nc.dram_tensor("dbg_small", [128, 64], F32, kind="ExternalOutput").ap()

    S = Sched(nc)
    from contextlib import ExitStack
    with ExitStack() as es:
        arena = es.enter_context(nc.sbuf_tensor("arena", [128, AR], BF16))
        PS = [es.enter_context(nc.psum_tensor("ps%d" % i, [128, 512], F32)) for i in range(8)]
        PST = S.tiles(8, "ps")

        def bf(off, n):
            return arena[:, off:off + n]

        def f32(off, n):
            return arena[:, off:off + 2 * n].bitcast(F32)

        o = 0
        ident = bf(o, 128); o += 128
        ones = bf(o, 128); o += 128
        pcol = f32(o, NPC); o += 2 * NPC
        prow = f32(o, NPR); o += 2 * NPR
        small = f32(o, 64); o += 128
        STG = [(f32(o, 1024), S.tile("stg")), (f32(o + 2048, 1024), S.tile("stg"))]; o += 4096
        assert o <= 9216, o
        MOFF = 9216
        ROFF = MOFF + 17408
        t_const = S.tile("const")
        t_small = S.tile("small")
        stg = Rot(STG)
        mT = bf(MOFF, 8 * NQ).rearrange("p (k t) -> p k t", t=NQ)
        t_mT = S.tiles(8, "mT")

        S.dma(t_const, ident, ident_d, writes=[t_const], wadd=True)
        S.dma(t_const, pcol, pcol_d, writes=[t_const], wadd=True)
        S.dma(t_const, prow, prow_d, writes=[t_const], wadd=True)
        S.op("pool", lambda e: e.memset(ones, 1.0), writes=[t_const], wadd=True)
        tmpl = f32(ROFF, 64)
        t_tmpl = S.tile()
        for i, (a, b) in enumerate(((PR_LQ1, PR_LK1), (PR_LQ2, PR_LK2))):
            S.op("dve", lambda e, a=a, b=b: e.tensor_tensor(out=tmpl, in0=prow[:, a:a + 64], in1=prow[:, b:b + 64], op=ALU.mult),
                 reads=[t_const], writes=[t_tmpl])
            S.op("dve", lambda e, i=i: e.reduce_sum(out=small[:, i:i + 1], in_=tmpl, axis=AX.X), reads=[t_tmpl], writes=[t_small], wadd=True)
        S.op("act", lambda e: e.activation(out=small[:, 2:4], in_=small[:, 0:2], func=AF.Exp), reads=[t_small], writes=[t_small], wadd=True)
        S.op("dve", lambda e: e.tensor_tensor(out=small[:, 4:5], in0=small[:, 3:4], in1=small[:, 2:3], op=ALU.subtract), reads=[t_small], writes=[t_small], wadd=True)
        S.op("dve", lambda e: e.tensor_scalar(out=small[:, 4:5], in0=small[:, 4:5], scalar1=-0.2, scalar2=None, op0=ALU.add), reads=[t_small], writes=[t_small], wadd=True)
        nlam = small[:, 4:5]

        lw_i = [0]

        def load_w(dst, t_dst, src, ncols, scale):
            for c0 in range(0, ncols, 1024):
                n = min(1024, ncols - c0)
                st, t_st = stg.next()
                S.dma(t_st, st[:, :n], src[:, c0:c0 + n], writes=[t_st])
                lw_i[0] += 1
                if lw_i[0] % 2 == 0:
                    if scale is None:
                        S.op("dve", lambda e, st=st, n=n, c0=c0: e.tensor_copy(out=dst[:, c0:c0 + n], in_=st[:, :n]),
                             reads=[t_st], writes=[t_dst], wadd=True)
                    else:
                        S.op("dve", lambda e, st=st, n=n, c0=c0: e.tensor_scalar(out=dst[:, c0:c0 + n], in0=st[:, :n], scalar1=scale, scalar2=None, op0=ALU.mult),
                             reads=[t_st, t_const], writes=[t_dst], wadd=True)
                else:
                    if scale is None:
                        S.op("act", lambda e, st=st, n=n, c0=c0: e.copy(out=dst[:, c0:c0 + n], in_=st[:, :n]),
                             reads=[t_st], writes=[t_dst], wadd=True)
                    else:
                        S.op("act", lambda e, st=st, n=n, c0=c0: e.activation(out=dst[:, c0:c0 + n], in_=st[:, :n], func=AF.Identity, scale=scale),
                             reads=[t_st, t_const], writes=[t_dst], wadd=True)

        psrot = [0]
        pslist = [list(range(8))]

        def nextps():
            l = pslist[0]
            i = l[psrot[0] % len(l)]
            psrot[0] += 1
            return PS[i], PST[i]

        def norm_rows(xt, t_x, n, sqj, t_sq, ss, t_ss, xn, t_xn, eps_scaled=1024 * 1e-6, mul=32.0):
            S.op("act", lambda e: e.activation(out=sqj[:n, :], in_=xt[:n, :], func=AF.Square, accum_out=ss[:n, :]),
                 reads=[t_x], writes=[t_sq, t_ss])
            S.op("act", lambda e: e.activation(out=ss[:n, :], in_=ss[:n, :], func=AF.Sqrt, bias=eps_scaled, scale=1.0), reads=[t_ss], writes=[t_ss])
            S.op("dve", lambda e: e.reciprocal(out=ss[:n, :], in_=ss[:n, :]), reads=[t_ss], writes=[t_ss])
            S.op("dve", lambda e: e.tensor_scalar(out=xn[:n, :], in0=xt[:n, :], scalar1=ss[:n, 0:1], scalar2=mul, op0=ALU.mult, op1=ALU.mult),
                 reads=[t_x, t_ss], writes=[t_xn])

        def transpose_rows(src, t_src, n, dst3, t_dst, coff, evac="act"):
            ps, t_ps = nextps()
            psb = ps[:].bitcast(BF16)
            for kc in range(KC):
                S.op("pe", lambda e, kc=kc: e.transpose(out=psb[:, kc * 128:kc * 128 + n], in_=src[:n, kc * 128:(kc + 1) * 128], identity=ident[:n, :n]),
                     reads=[t_src, t_const], writes=[t_ps], wadd=(kc > 0))
            pv = psb.rearrange("p (k t) -> p k t", t=128)[:, :, :n]
            if evac == "act":
                S.op("act", lambda e: e.copy(out=dst3[:, :, coff:coff + n], in_=pv), reads=[t_ps], writes=[t_dst], wadd=True)
            else:
                S.op("dve", lambda e: e.tensor_copy(out=dst3[:, :, coff:coff + n], in_=pv), reads=[t_ps], writes=[t_dst], wadd=True)

        def mm_acc(ps_ap, t_ps, pairs, reads):
            n = len(pairs)
            for i, (l, r) in enumerate(pairs):
                S.op("pe", lambda e, l=l, r=r, i=i: e.matmul(ps_ap, lhsT=l, rhs=r, start=(i == 0), stop=(i == n - 1)),
                     reads=reads, writes=[t_ps], wadd=(i > 0))

        Wkv = bf(MOFF, 8 * 2048).rearrange("p (k c) -> p k c", c=2048)
        t_Wkv = S.tile("Wkv")
        PW = ROFF + 8704
        Wglu = bf(PW, 16384).rearrange("p (k c) -> p k c", c=2048)
        Wgc = bf(PW + 16384, 8192).rearrange("p (k c) -> p k c", c=1024)
        Wco = bf(PW + 24576, 8192).rearrange("p (k c) -> p k c", c=1024)
        t_Wglu, t_Wgc, t_Wco = S.tile("Wglu"), S.tile("Wgc"), S.tile("Wco")
        o = PW + 32768
        XB = []
        for i in range(4):
            XB.append((f32(o, 1024), S.tile("xb"))); o += 2048
        sqj = bf(o, 1024); o += 1024
        t_sq = S.tile("sq")
        SSB = []
        for i in range(8):
            SSB.append((f32(o, 1), S.tile("ss"))); o += 2
        XN = []
        for i in range(3):
            XN.append((bf(o, 1024), S.tile("xn"))); o += 1024
        XT = []
        for i in range(2):
            XT.append((bf(o, 4096).rearrange("p (k t) -> p k t", t=512), S.tile("xT"))); o += 4096
        KST = []
        for i in range(2):
            KST.append((bf(o, 4096).rearrange("p (h t) -> p h t", t=512), S.tile("kst"))); o += 4096
        VST = []
        for i in range(4):
            VST.append((bf(o, 1032).rearrange("p (h t) -> p h t", t=129), S.tile("vst"))); o += 1032
        DGT = []
        for i in range(2):
            DGT.append((bf(o, NPE_G * 128).rearrange("p (k c) -> p k c", c=128), S.tile("dgt"))); o += NPE_G * 128
        assert o <= AR, o
        xb, ssb, xnb, xtb, kstb, vstb = Rot(XB), Rot(SSB), Rot(XN), Rot(XT), Rot(KST), Rot(VST)
        dgtb = Rot(DGT)
        t_dgscr = S.tile("dgscr")

        def build_diag(oc):
            dgt, t_dgt = dgtb.next()
            wb = PC_CDW + oc * 31
            on_pool = oc < 4
            for k in range(NPE_G):
                if on_pool:
                    S.op("pool", lambda e, k=k: e.tensor_scalar(out=dgt[:, k, :], in0=ident, scalar1=pcol[:, wb + k:wb + k + 1], scalar2=None, op0=ALU.mult),
                         reads=[t_const], writes=[t_dgt], wadd=(k > 0))
                else:
                    S.op("act", lambda e, k=k: e.activation(out=dgt[:, k, :], in_=ident, func=AF.Identity, scale=pcol[:, wb + k:wb + k + 1]),
                         reads=[t_const], writes=[t_dgt], wadd=(k > 0))
            S.dma(t_dgt, dg_scr[oc], dgt.rearrange("p k c -> p (k c)"), reads=[t_dgt], writes=[t_dgscr], wadd=True, q=("pool" if on_pool else "sp"))
        t_ktscr = S.tile("ktscr")
        t_vscr = S.tile("vscr")
        t_Wv = S.tile("Wv")
        for kc in range(KC):
            load_w(Wkv[:, kc, 1024:2048], t_Wv, w_in[kc * 128:(kc + 1) * 128, 2048:3072], 1024, pcol[:, PC_NMIX + kc:PC_NMIX + kc + 1])
        for kc in range(KC):
            load_w(Wkv[:, kc, 0:1024], t_Wkv, w_in[kc * 128:(kc + 1) * 128, 1024:2048], 1024, pcol[:, PC_NMIX + kc:PC_NMIX + kc + 1])
        for vi in range(4):
            S.op("pool", lambda e, vi=vi: e.memset(VST[vi][0][:, :, 128:129], 1.0), writes=[VST[vi][1]])
        blocks = [(0, 1)] + [(1 + 4 * i, 4) for i in range(4)]
        def kv_prep(t0, ntl):
            xT, t_xT = xtb.next()
            for tl in range(ntl):
                row0 = (t0 + tl) * 128
                xt, t_x = xb.next()
                S.dma(t_x, xt, x_all[row0:row0 + 128, :], writes=[t_x])
                ss, t_ss = ssb.next()
                xn, t_xn = xnb.next()
                norm_rows(xt, t_x, 128, sqj, t_sq, ss, t_ss, xn, t_xn)
                transpose_rows(xn, t_xn, 128, xT, t_xT, tl * 128, evac="act")
            return xT, t_xT

        def kv_compute(t0, ntl, xT, t_xT):
            for tl in range(ntl):
                vst, t_vst = vstb.next()
                for half in range(2):
                    ps, t_ps = nextps()
                    mm_acc(ps[:, :], t_ps, [(xT[:, kc, tl * 128:(tl + 1) * 128], Wkv[:, kc, 1024 + half * 512:1024 + (half + 1) * 512]) for kc in range(KC)], [t_xT, t_Wv])
                    S.op("dve" if half == 0 else "act",
                         (lambda e, ps=ps, vst=vst, half=half: e.tensor_copy(out=vst[:, 4 * half:4 * half + 4, 0:128], in_=ps[:, :].rearrange("p (h d) -> p h d", d=128))) if half == 0 else
                         (lambda e, ps=ps, vst=vst, half=half: e.copy(out=vst[:, 4 * half:4 * half + 4, 0:128], in_=ps[:, :].rearrange("p (h d) -> p h d", d=128))),
                         reads=[t_ps], writes=[t_vst], wadd=True)
                tq = t0 + tl - 1
                vdst = v_meta if t0 == 0 else v_part[tq // 2][(tq % 2) * 128:(tq % 2 + 1) * 128, :]
                S.dma(t_vst, vdst, vst.rearrange("p h t -> p (h t)"), reads=[t_vst], writes=[t_vscr], wadd=True)
            nt = ntl * 128
            kst, t_kst = kstb.next()
            for h in range(8):
                ps, t_ps = nextps()
                mm_acc(ps[:, :nt], t_ps, [(Wkv[:, kc, h * 128:(h + 1) * 128], xT[:, kc, :nt]) for kc in range(KC)], [t_xT, t_Wkv])
                if h % 2 == 0:
                    S.op("dve", lambda e, ps=ps, h=h, kst=kst: e.tensor_copy(out=kst[:, h, :nt], in_=ps[:, :nt]), reads=[t_ps], writes=[t_kst], wadd=True)
                else:
                    S.op("act", lambda e, ps=ps, h=h, kst=kst: e.copy(out=kst[:, h, :nt], in_=ps[:, :nt]), reads=[t_ps], writes=[t_kst], wadd=True)
            if t0 == 0:
                S.dma(t_kst, kt_meta.rearrange("h p t -> p h t"), kst[:, :, :nt], reads=[t_kst], writes=[t_ktscr], wadd=True)
            else:
                for c in range(4):
                    kdst = kt_part[c].rearrange("(h p) t -> p h t", p=128)[:, :, (t0 - 1) * 128:(t0 - 1) * 128 + nt]
                    S.dma(t_kst, kdst, kst[:, 2 * c:2 * c + 2, :nt], reads=[t_kst], writes=[t_ktscr], wadd=True)

        pre = []
        for kc in range(KC):
            nm = pcol[:, PC_NMIX + kc:PC_NMIX + kc + 1]
            pre.append((Wglu[:, kc, :], t_Wglu, w_in[kc * 128:(kc + 1) * 128, 3072:5120], 2048, nm))
        for kc in range(KC):
            nm = pcol[:, PC_NMIX + kc:PC_NMIX + kc + 1]
            pre.append((Wgc[:, kc, :], t_Wgc, w_in[kc * 128:(kc + 1) * 128, 6144:7168], 1024, nm))
        for kc in range(KC):
            pre.append((Wco[:, kc, :], t_Wco, w_co[kc * 128:(kc + 1) * 128, :], 1024, None))
        dg_todo = [0, 1, 2, 3, 4, 5, 6, 7]
        nxt = kv_prep(*blocks[0])
        for bi_, (t0, ntl) in enumerate(blocks):
            cur = nxt
            if bi_ + 1 < len(blocks):
                nxt = kv_prep(*blocks[bi_ + 1])
            for _ in range(5):
                if pre:
                    load_w(*pre.pop(0))
            for _ in range(2):
                if dg_todo:
                    build_diag(dg_todo.pop(0))
            kv_compute(t0, ntl, *cur)
        while pre:
            load_w(*pre.pop(0))
        while dg_todo:
            build_diag(dg_todo.pop(0))
        XNT_pre = (bf(ROFF, 8 * 544).rearrange("p (k t) -> p k t", t=544), S.tile("XNT"))
        r_ = 0
        while r_ < 32 + GQ[0]:
            n_ = min(128, 32 + GQ[0] - r_)
            xt_, t_x_ = xb.next()
            S.dma(t_x_, xt_[:n_, :], x_own[GROW[0] + r_:GROW[0] + r_ + n_, :], writes=[t_x_])
            ss_, t_ss_ = ssb.next()
            xn_, t_xn_ = xnb.next()
            norm_rows(xt_, t_x_, n_, sqj, t_sq, ss_, t_ss_, xn_, t_xn_)
            transpose_rows(xn_, t_xn_, n_, XNT_pre[0], XNT_pre[1], r_, evac="act")
            r_ += n_
        t_ktall, t_vall, t_cc = S.tile("ktall"), S.tile("vall"), S.tile("cc")
        RG = [[0, 1, 2, 3], [4, 5, 6, 7]]
        t_ktall = S.tiles(4, "ktall")
        t_vall = S.tiles(8, "vall")
        for c in range(4):
            S.ccop(t_cc, lambda e, c=c: e.collective_compute("AllGather", ALU.bypass, replica_groups=RG, ins=[kt_part[c]], outs=[kt_all[c]]), reads=[t_ktscr], writes=[t_ktall[c]])
            for c2 in (2 * c, 2 * c + 1):
                S.ccop(t_cc, lambda e, c2=c2: e.collective_compute("AllGather", ALU.bypass, replica_groups=RG, ins=[v_part[c2]], outs=[v_all[c2]]), reads=[t_vscr], writes=[t_vall[c2]])
        S.barrier()

        xnt_pending = []

        def own_xnt_parts(g, xntb, xb, ssb, xnb, sqj, t_sq):
            XNT, t_XNT = xntb.next()
            nrows = 32 + GQ[g]
            r = 0
            tiles = []
            while r < nrows:
                tiles.append((r, min(128, nrows - r)))
                r += tiles[-1][1]
            held = {}

            def part_a(r, n):
                xt, t_x = xb.next()
                S.dma(t_x, xt[:n, :], x_own[GROW[g] + r:GROW[g] + r + n, :], writes=[t_x])
                ss, t_ss = ssb.next()
                xn, t_xn = xnb.next()
                norm_rows(xt, t_x, n, sqj, t_sq, ss, t_ss, xn, t_xn)
                held[r] = (xn, t_xn)

            def part_b(r, n):
                xn, t_xn = held.pop(r)
                transpose_rows(xn, t_xn, n, XNT, t_XNT, r, evac="act")

            order = []
            for i, (r, n) in enumerate(tiles):
                order.append(lambda r=r, n=n: part_a(r, n))
                if i >= 1:
                    pr, pn = tiles[i - 1]
                    order.append(lambda pr=pr, pn=pn: part_b(pr, pn))
            lr, ln_ = tiles[-1]
            order.append(lambda: part_b(lr, ln_))
            xnt_pending.extend(order)
            return XNT, t_XNT

        def xnt_drain(k=99):
            for _ in range(k):
                if xnt_pending:
                    xnt_pending.pop(0)()

        def own_xnt(g, xntb, xb, ssb, xnb, sqj, t_sq):
            res = own_xnt_parts(g, xntb, xb, ssb, xnb, sqj, t_sq)
            xnt_drain()
            return res

        o = ROFF
        XNTB = [XNT_pre]
        o += 4352
        for i in range(1):
            XNTB.append((bf(o, 8 * 544).rearrange("p (k t) -> p k t", t=544), S.tile("XNT"))); o += 4352
        assert o == PW
        o += 32768
        XB = []
        for i in range(2):
            XB.append((f32(o, 1024), S.tile("xb"))); o += 2048
        sqj = bf(o, 1024); o += 1024
        t_sq = S.tile("sq")
        SSB = []
        for i in range(4):
            SSB.append((f32(o, 1), S.tile("ss"))); o += 2
        XN = []
        for i in range(2):
            XN.append((bf(o, 1024), S.tile("xn"))); o += 1024
        UB = []
        for i in range(3):
            UB.append((bf(o, 544), S.tile("u"))); o += 544
        DGB = []
        for i in range(2):
            DGB.append((bf(o, 20 * 128).rearrange("p (k c) -> p k c", c=128), S.tile("dg"))); o += 2560
        SG = []
        for i in range(2):
            SG.append((f32(o, 544), S.tile("sg"))); o += 1088
        CA = []
        for i in range(2):
            CA.append((f32(o, 512), S.tile("ca"))); o += 1024
        cb = bf(o, 4096).rearrange("p (k t) -> p k t", t=512); o += 4096
        t_cb = S.tiles(8, "cb")
        CSQ = []
        for i in range(2):
            CSQ.append((bf(o, 512), S.tile("csq"))); o += 512
        mean = f32(o, 512); o += 1024
        rstd = f32(o, 512); o += 1024
        msq = f32(o, 512); o += 1024
        t_mean, t_rstd, t_msq = S.tile("mean"), S.tile("rstd"), S.tile("msq")
        XH = []
        for i in range(2):
            XH.append((f32(o, 512), S.tile("xh"))); o += 1024
        uu = bf(o, 4096).rearrange("p (k t) -> p k t", t=512); o += 4096
        t_uu = S.tiles(8, "uu")
        GCB = []
        gcs = bf(o, 4096).rearrange("p (k t) -> p k t", t=512); o += 4096
        t_gcs = S.tiles(8, "gcs")
        assert o <= AR, o
        xntb, xb, ssb, xnb, ub, sgb, cab, csqb, xhb, gcb, dgb = Rot(XNTB), Rot(XB), Rot(SSB), Rot(XN), Rot(UB), Rot(SG), Rot(CA), Rot(CSQ), Rot(XH), Rot(GCB), Rot(DGB)

        pslist[0] = list(range(6))

        def conv_group(g, XNT, t_XNT):
            nq = GQ[g]
            nr = 32 + nq
            psum1, t_psum1 = PS[6], PST[6]
            psum2, t_psum2 = PS[7], PST[7]
            NPE = 20

            def stage1(oc):
                u, t_u = ub.next()
                sg, t_sg = sgb.next()
                dg, t_dg = dgb.next()
                wb = PC_CDW + oc * 31
                S.dma(t_dg, dg.rearrange("p k c -> p (k c)"), dg_scr[oc], reads=[t_dgscr], writes=[t_dg])
                for (c0, cn) in ((0, 32), (32, nq)):
                    psa, t_psa = nextps()
                    mm_acc(psa[:, :cn], t_psa, [(Wglu[:, kc, oc * 128:(oc + 1) * 128], XNT[:, kc, c0:c0 + cn]) for kc in range(KC)], [t_XNT, t_Wglu])
                    psg, t_psg = nextps()
                    mm_acc(psg[:, :cn], t_psg, [(Wglu[:, kc, 1024 + oc * 128:1024 + (oc + 1) * 128], XNT[:, kc, c0:c0 + cn]) for kc in range(KC)], [t_XNT, t_Wglu])
                    S.op("act", lambda e, psg=psg, sg=sg, c0=c0, cn=cn: e.activation(out=sg[:, c0:c0 + cn], in_=psg[:, :cn], func=AF.Sigmoid),
                         reads=[t_psg], writes=[t_sg], wadd=True)
                    S.op("dve", lambda e, psa=psa, sg=sg, u=u, c0=c0, cn=cn: e.tensor_tensor(out=u[:, c0:c0 + cn], in0=psa[:, :cn], in1=sg[:, c0:c0 + cn], op=ALU.mult),
                         reads=[t_psa, t_sg], writes=[t_u], wadd=True)
                return u, t_u, dg, t_dg

            def stage2(oc, u, t_u, dg, t_dg):
                wb = PC_CDW + oc * 31
                psc, t_psc = nextps()
                for k in range(NPE):
                    S.op("pe", lambda e, k=k: e.matmul(psc[:, :nq], lhsT=dg[:, k, :], rhs=u[:, 2 + k:2 + k + nq], start=(k == 0), stop=(k == NPE - 1)),
                         reads=[t_u, t_dg], writes=[t_psc], wadd=(k > 0))
                ca, t_ca = cab.next()
                S.op("dve", lambda e: e.tensor_scalar(out=ca[:, :nq], in0=u[:, 2 + NPE:2 + NPE + nq], scalar1=pcol[:, wb + NPE:wb + NPE + 1], scalar2=pcol[:, PC_CDB + oc:PC_CDB + oc + 1], op0=ALU.mult, op1=ALU.add),
                     reads=[t_u, t_const], writes=[t_ca])
                for k in range(NPE + 1, 31):
                    S.op("dve", lambda e, k=k: e.scalar_tensor_tensor(out=ca[:, :nq], in0=u[:, 2 + k:2 + k + nq], scalar=pcol[:, wb + k:wb + k + 1], in1=ca[:, :nq], op0=ALU.mult, op1=ALU.add),
                         reads=[t_u, t_ca, t_const], writes=[t_ca])
                S.op("dve", lambda e: e.tensor_tensor(out=ca[:, :nq], in0=psc[:, :nq], in1=ca[:, :nq], op=ALU.add), reads=[t_psc, t_ca], writes=[t_ca])
                S.op("act", lambda e: e.copy(out=cb[:, oc, :nq], in_=ca[:, :nq]), reads=[t_ca], writes=[t_cb[oc]])
                csq, t_csq = csqb.next()
                S.op("act", lambda e: e.activation(out=csq[:, :nq], in_=ca[:, :nq], func=AF.Square), reads=[t_ca], writes=[t_csq])
                def stats():
                    S.op("pe", lambda e: e.matmul(psum1[:, :nq], lhsT=ones, rhs=cb[:, oc, :nq], start=(oc == 0), stop=(oc == 7)),
                         reads=[t_cb[oc], t_const], writes=[t_psum1], wadd=(oc > 0))
                    S.op("pe", lambda e: e.matmul(psum2[:, :nq], lhsT=ones, rhs=csq[:, :nq], start=(oc == 0), stop=(oc == 7)),
                         reads=[t_csq, t_const], writes=[t_psum2], wadd=(oc > 0))
                return stats

            nx = stage1(0)
            pend_stats = None
            for oc in range(8):
                cu = nx
                if oc < 7:
                    nx = stage1(oc + 1)
                st = stage2(oc, *cu)
                if pend_stats is not None:
                    pend_stats()
                pend_stats = st
                if oc >= 1:
                    xnt_drain(2)
            pend_stats()
            for oc2 in range(8):
                psg, t_psg = nextps()
                mm_acc(psg[:, :nq], t_psg, [(Wgc[:, kc, oc2 * 128:(oc2 + 1) * 128], XNT[:, kc, 32:32 + nq]) for kc in range(KC)], [t_XNT, t_Wgc])
                S.op("act", lambda e, psg=psg, oc2=oc2: e.activation(out=gcs[:, oc2, :nq], in_=psg[:, :nq], func=AF.Sigmoid), reads=[t_psg], writes=[t_gcs[oc2]])
            S.op("act", lambda e: e.activation(out=mean[:, :nq], in_=psum1[:, :nq], func=AF.Identity, scale=1.0 / 1024), reads=[t_psum1], writes=[t_mean])
            S.op("dve", lambda e: e.tensor_tensor(out=msq[:, :nq], in0=mean[:, :nq], in1=mean[:, :nq], op=ALU.mult), reads=[t_mean], writes=[t_msq])
            S.op("dve", lambda e: e.scalar_tensor_tensor(out=rstd[:, :nq], in0=psum2[:, :nq], scalar=1.0 / 1024, in1=msq[:, :nq], op0=ALU.mult, op1=ALU.subtract),
                 reads=[t_psum2, t_msq], writes=[t_rstd])
            S.op("act", lambda e: e.activation(out=rstd[:, :nq], in_=rstd[:, :nq], func=AF.Ln, bias=1e-5, scale=1.0), reads=[t_rstd], writes=[t_rstd])
            S.op("act", lambda e: e.activation(out=rstd[:, :nq], in_=rstd[:, :nq], func=AF.Exp, scale=-0.5), reads=[t_rstd], writes=[t_rstd])
            for oc in range(8):
                xh, t_xh = xhb.next()
                S.op("dve", lambda e, xh=xh, oc=oc: e.tensor_tensor(out=xh[:, :nq], in0=cb[:, oc, :nq], in1=mean[:, :nq], op=ALU.subtract), reads=[t_cb[oc], t_mean], writes=[t_xh])
                S.op("dve", lambda e, xh=xh: e.tensor_tensor(out=xh[:, :nq], in0=xh[:, :nq], in1=rstd[:, :nq], op=ALU.mult), reads=[t_xh, t_rstd], writes=[t_xh])
                S.op("act", lambda e, xh=xh, oc=oc: e.activation(out=uu[:, oc, :nq], in_=xh[:, :nq], func=AF.Silu, scale=pcol[:, PC_LNW + oc:PC_LNW + oc + 1], bias=pcol[:, PC_LNB + oc:PC_LNB + oc + 1]),
                     reads=[t_xh, t_const], writes=[t_uu[oc]])
            for oc2 in range(8):
                psy, t_psy = nextps()
                mm_acc(psy[:, :nq], t_psy, [(Wco[:, oc, oc2 * 128:(oc2 + 1) * 128], uu[:, oc, :nq]) for oc in range(8)], t_uu + [t_Wco])
                S.op("dve", lambda e, psy=psy, oc2=oc2: e.tensor_tensor(out=mT[:, oc2, GOFF[g]:GOFF[g] + nq], in0=psy[:, :nq], in1=gcs[:, oc2, :nq], op=ALU.mult),
                     reads=[t_psy, t_gcs[oc2]], writes=[t_mT[oc2]], wadd=True)

        nxt = XNT_pre
        xntb.i = 1
        for g in range(5):
            cur = nxt
            if g < 4:
                nxt = own_xnt_parts(g + 1, xntb, xb, ssb, xnb, sqj, t_sq)
            conv_group(g, *cur)
            xnt_drain()
        pslist[0] = list(range(8))
        S.barrier()
        t_dbg = S.tile("dbg")
        if DEBUG:
            S.dma(t_dbg, dbg_mc1, bf(MOFF, 8 * NQ), writes=[t_dbg], wadd=True)
            S.dma(t_dbg, dbg_small, small, writes=[t_dbg], wadd=True)
            S.barrier()

        o = ROFF
        QT = bf(o, 8 * NQ).rearrange("p (h t) -> p h t", t=NQ); o += 8 * NQ
        gateT = bf(o, 8 * NQ).rearrange("p (h t) -> p h t", t=NQ); o += 8 * NQ
        t_QT = S.tile("QT")
        t_gateT = S.tile("gateT")
        PERSIST = o
        V0 = (bf(o, NT_ALL * 129).rearrange("p (t d) -> p t d", d=129), S.tile("V")); o += NT_ALL * 129 + 1

        def load_v(h, V, t_V):
            S.dma(t_V, V[:, 0, :], v_meta[:, h * 129:(h + 1) * 129], reads=[t_vscr], writes=[t_V])
            Vv = V[:, 1:65, :].rearrange("p (r c t) d -> p r c t d", r=4, c=8, t=2)
            for c in range(8):
                for r in range(4):
                    S.dma(t_V, Vv[:, r, c, :, :], v_all[c][r * 256:(r + 1) * 256, h * 129:(h + 1) * 129].rearrange("(t p) d -> p t d", p=128), reads=[t_vall[c]], writes=[t_V], wadd=True)

        XNTB = []
        for i in range(2):
            XNTB.append((bf(o, 8 * 544).rearrange("p (k t) -> p k t", t=544), S.tile("XNT"))); o += 4352
        Wq = bf(o, 8192).rearrange("p (k c) -> p k c", c=1024); o += 8192
        Wga = bf(o, 8192).rearrange("p (k c) -> p k c", c=1024); o += 8192
        t_Wq, t_Wga = S.tile("Wq"), S.tile("Wga")
        XB = []
        for i in range(2):
            XB.append((f32(o, 1024), S.tile("xb"))); o += 2048
        sqj = bf(o, 1024); o += 1024
        t_sq = S.tile("sq")
        SSB = []
        for i in range(4):
            SSB.append((f32(o, 1), S.tile("ss"))); o += 2
        XN = []
        for i in range(2):
            XN.append((bf(o, 1024), S.tile("xn"))); o += 1024
        stg_x = []
        for i in range(2):
            stg_x.append((f32(o, 1024), S.tile("stg"))); o += 2048
        assert o <= AR, o
        stg.items = STG + stg_x
        xntb, xb, ssb, xnb = Rot(XNTB), Rot(XB), Rot(SSB), Rot(XN)
        for kc in range(KC):
            nm = pcol[:, PC_NMIX + kc:PC_NMIX + kc + 1]
            load_w(Wq[:, kc, :], t_Wq, w_in[kc * 128:(kc + 1) * 128, 0:1024], 1024, nm)
            load_w(Wga[:, kc, :], t_Wga, w_in[kc * 128:(kc + 1) * 128, 5120:6144], 1024, nm)
        def qg_group(g, XNT, t_XNT, hook=None):
            nq = GQ[g]
            for h in range(8):
                ps, t_ps = nextps()
                mm_acc(ps[:, :nq], t_ps, [(Wq[:, kc, h * 128:(h + 1) * 128], XNT[:, kc, 32:32 + nq]) for kc in range(KC)], [t_XNT, t_Wq])
                if h % 2 == 0:
                    S.op("dve", lambda e, ps=ps, h=h, g=g, nq=nq: e.tensor_copy(out=QT[:, h, GOFF[g]:GOFF[g] + nq], in_=ps[:, :nq]), reads=[t_ps], writes=[t_QT], wadd=True)
                else:
                    S.op("act", lambda e, ps=ps, h=h, g=g, nq=nq: e.copy(out=QT[:, h, GOFF[g]:GOFF[g] + nq], in_=ps[:, :nq]), reads=[t_ps], writes=[t_QT], wadd=True)
                if h >= 1:
                    xnt_drain(2)
            if hook is not None:
                xnt_drain()
                hook()
            for h in range(8):
                ps, t_ps = nextps()
                mm_acc(ps[:, :nq], t_ps, [(Wga[:, kc, h * 128:(h + 1) * 128], XNT[:, kc, 32:32 + nq]) for kc in range(KC)], [t_XNT, t_Wga])
                S.op("act", lambda e, ps=ps, h=h: e.activation(out=gateT[:, h, GOFF[g]:GOFF[g] + nq], in_=ps[:, :nq], func=AF.Sigmoid),
                     reads=[t_ps], writes=[t_gateT], wadd=True)

        nxt = own_xnt(0, xntb, xb, ssb, xnb, sqj, t_sq)
        for g in range(5):
            cur = nxt
            if g < 4:
                nxt = own_xnt_parts(g + 1, xntb, xb, ssb, xnb, sqj, t_sq)
            qg_group(g, *cur, hook=((lambda: load_v(0, *V0)) if g == 3 else None))
            xnt_drain()
        S.barrier()
        if DEBUG:
            S.dma(t_dbg, dbg_qt, bf(ROFF, 8 * NQ), writes=[t_dbg], wadd=True)
            S.dma(t_dbg, dbg_gate, bf(ROFF + 8 * NQ, 8 * NQ), writes=[t_dbg], wadd=True)
            S.barrier()

        stg.items = STG
        o = PERSIST + NT_ALL * 129 + 1
        KTB, VB = [], [V0]
        for i in range(2):
            KTB.append((bf(o, NKEY), S.tile("KT"))); o += NKEY
        for i in range(1):
            VB.append((bf(o, NT_ALL * 129).rearrange("p (t d) -> p t d", d=129), S.tile("V"))); o += NT_ALL * 129 + 1
        PTB = []
        for i in range(6):
            PTB.append((bf(o, 512), S.tile("PT"))); o += 512
        cq = bf(o, 2048); o += 2048
        t_cq = S.tile("cq")
        RLB, OAB = [], []
        for i in range(2):
            RLB.append((f32(o, 512), S.tile("rl"))); o += 1024
        for i in range(2):
            OAB.append((f32(o, 512), S.tile("oa"))); o += 1024
        tt = f32(o, 512); o += 1024
        t_tt = S.tile("tt")
        o2r = f32(o, 512); o += 1024
        t_o2r = S.tile("o2r")
        sqb = bf(o, 512); o += 512
        t_sqb = S.tile("sqb")
        rsb = f32(o, 512); o += 1024
        t_rsb = S.tile("rsb")
        assert o <= AR, o
        ptb, rlb, oab = Rot(PTB), Rot(RLB), Rot(OAB)
        S.dma(t_cq, cq, cq_d, writes=[t_cq])
        ck = prow[:, PR_CK:PR_CK + 64]
        subw = pcol[:, PC_SUB:PC_SUB + 1]
        srot = [0]
        SK = 3

        def att_head(h, KT, t_KT, V, t_V):
            units = []
            for g in range(5):
                kbl = [(0, 16, 0, False, -1)]
                if g < 4:
                    for m in range(16 * g):
                        kbl.append(((m + 1) * 128, 128, m + 1, False, m))
                    for m in range(16 * g, 16 * g + 16):
                        kbl.append(((m + 1) * 128, 128, m + 1, True, m))
                else:
                    for m in range(64):
                        kbl.append(((m + 1) * 128, 128, m + 1, False, m))
                for bi, kb in enumerate(kbl):
                    units.append((g, bi == 0, bi == len(kbl) - 1) + kb)
            pts = {}
            deferred = []
            state = {}

            def emit_scores(idx):
                g, first, last, kc0, nk, vt, masked, m = units[idx]
                nq = GQ[g]
                banks = []
                for mp in range(2):
                    sb = srot[0] % 4
                    srot[0] += 1
                    pss, t_pss = PS[sb], PST[sb]
                    banks.append((pss, t_pss))
                    S.op("pe", lambda e, mp=mp, pss=pss: e.matmul(pss[:nk, :nq], lhsT=KT[64 * mp:64 * mp + 64, kc0:kc0 + nk], rhs=QT[64 * mp:64 * mp + 64, h, GOFF[g]:GOFF[g] + nq], start=True, stop=True),
                         reads=[t_KT, t_QT], writes=[t_pss])
                res = []
                for mp in range(2):
                    pss, t_pss = banks[mp]
                    pt, t_pt = ptb.next()
                    S.op("act", lambda e, pss=pss, pt=pt: e.activation(out=pt[:nk, :nq], in_=pss[:nk, :nq], func=AF.Exp, scale=0.125), reads=[t_pss], writes=[t_pt])
                    if masked:
                        S.op("dve", lambda e, pt=pt: e.scalar_tensor_tensor(out=pt[:, :nq], in0=cq[:, g * 512:g * 512 + nq], scalar=ck[:, m:m + 1], in1=pt[:, :nq], op0=ALU.is_ge, op1=ALU.mult),
                             reads=[t_pt, t_cq, t_const], writes=[t_pt])
                    res.append((pt, t_pt))
                pts[idx] = res

            def evac0(g):
                nq = GQ[g]
                rl, t_rl = rlb.next()
                oa, t_oa = oab.next()
                state[g] = (oa, t_oa)
                S.op("act", lambda e: e.copy(out=oa[:, :nq], in_=PS[4][:, :nq]), reads=[PST[4]], writes=[t_oa])
                S.op("act", lambda e: e.activation(out=rl[:, :nq], in_=PS[5][:, :nq], func=AF.Ln), reads=[PST[5]], writes=[t_rl])
                S.op("act", lambda e: e.activation(out=rl[:, :nq], in_=rl[:, :nq], func=AF.Exp, scale=-1.0), reads=[t_rl], writes=[t_rl])
                S.op("dve", lambda e: e.tensor_tensor(out=oa[:, :nq], in0=oa[:, :nq], in1=rl[:, :nq], op=ALU.mult), reads=[t_oa, t_rl], writes=[t_oa])

            def evac1(g):
                nq = GQ[g]
                rl, t_rl = rlb.next()
                oa, t_oa = state[g]
                S.op("act", lambda e: e.copy(out=o2r[:, :nq], in_=PS[6][:, :nq]), reads=[PST[6]], writes=[t_o2r])
                S.op("act", lambda e: e.activation(out=rl[:, :nq], in_=PS[7][:, :nq], func=AF.Ln), reads=[PST[7]], writes=[t_rl])
                S.op("act", lambda e: e.activation(out=rl[:, :nq], in_=rl[:, :nq], func=AF.Exp, scale=-1.0), reads=[t_rl], writes=[t_rl])
                S.op("dve", lambda e: e.tensor_tensor(out=tt[:, :nq], in0=o2r[:, :nq], in1=rl[:, :nq], op=ALU.mult), reads=[t_o2r, t_rl], writes=[t_tt])
                S.op("dve", lambda e: e.scalar_tensor_tensor(out=oa[:, :nq], in0=tt[:, :nq], scalar=nlam, in1=oa[:, :nq], op0=ALU.mult, op1=ALU.add),
                     reads=[t_tt, t_oa, t_small], writes=[t_oa])
                S.op("act", lambda e: e.activation(out=sqb[:, :nq], in_=oa[:, :nq], func=AF.Square), reads=[t_oa], writes=[t_sqb])

            def evac2(g):
                nq = GQ[g]
                oa, t_oa = state[g]
                sb = srot[0] % 4
                srot[0] += 1
                pss, t_pss = PS[sb], PST[sb]
                S.op("pe", lambda e: e.matmul(pss[:, :nq], lhsT=ones, rhs=sqb[:, :nq], start=True, stop=True), reads=[t_sqb, t_const], writes=[t_pss])
                S.op("act", lambda e: e.activation(out=rsb[:, :nq], in_=pss[:, :nq], func=AF.Ln, bias=1e-6 / 0.64, scale=1.0 / (128 * 0.64)), reads=[t_pss], writes=[t_rsb])
                S.op("act", lambda e: e.activation(out=rsb[:, :nq], in_=rsb[:, :nq], func=AF.Exp, scale=-0.5), reads=[t_rsb], writes=[t_rsb])
                S.op("dve", lambda e: e.tensor_tensor(out=tt[:, :nq], in0=oa[:, :nq], in1=rsb[:, :nq], op=ALU.mult), reads=[t_oa, t_rsb], writes=[t_tt])
                S.op("dve", lambda e: e.scalar_tensor_tensor(out=tt[:, :nq], in0=tt[:, :nq], scalar=subw, in1=gateT[:, h, GOFF[g]:GOFF[g] + nq], op0=ALU.mult, op1=ALU.mult),
                     reads=[t_tt, t_gateT, t_const], writes=[t_tt])
                S.op("dve", lambda e: e.tensor_tensor(out=mT[:, h, GOFF[g]:GOFF[g] + nq], in0=tt[:, :nq], in1=mT[:, h, GOFF[g]:GOFF[g] + nq], op=ALU.add),
                     reads=[t_tt, t_mT[h]], writes=[t_mT[h]], wadd=True)

            def emit_pv(idx):
                g, first, last, kc0, nk, vt, masked, m = units[idx]
                nq = GQ[g]
                res = pts.pop(idx)
                for mp in range(2):
                    pt, t_pt = res[mp]
                    bo, bl = (4, 5) if mp == 0 else (6, 7)
                    S.op("pe", lambda e, pt=pt, bo=bo: e.matmul(PS[bo][:, :nq], lhsT=V[:nk, vt, 0:128], rhs=pt[:nk, :nq], start=first, stop=last),
                         reads=[t_pt, t_V], writes=[PST[bo]], wadd=(not first))
                    S.op("pe", lambda e, pt=pt, bl=bl: e.matmul(PS[bl][:, :nq], lhsT=ones[:nk, :], rhs=pt[:nk, :nq], start=first, stop=last),
                         reads=[t_pt, t_const], writes=[PST[bl]], wadd=(not first))
                if last:
                    evac0(g)
                    evac1(g)
                    deferred.append((idx + 3, g))

            n = len(units)
            SKL = 1
            idx = 0
            while idx < n + SKL or deferred:
                if idx < n:
                    emit_scores(idx)
                if 0 <= idx - SKL < n:
                    emit_pv(idx - SKL)
                while deferred and deferred[0][0] <= idx:
                    evac2(deferred.pop(0)[1])
                idx += 1

        for h in range(8):
            KT, t_KT = KTB[h % 2]
            V, t_V = VB[h % 2]
            S.dma(t_KT, KT[:, 0:128], kt_meta[h], reads=[t_ktscr], writes=[t_KT])
            for r in range(4):
                S.dma(t_KT, KT[:, 128 + r * 2048:128 + (r + 1) * 2048], kt_all[h // 2][r * 256 + (h % 2) * 128:r * 256 + (h % 2 + 1) * 128, :], reads=[t_ktall[h // 2]], writes=[t_KT], wadd=True)
            if h > 0:
                load_v(h, V, t_V)
            att_head(h, KT, t_KT, V, t_V)
        S.barrier()
        if DEBUG:
            S.dma(t_dbg, dbg_mc2, bf(MOFF, 8 * NQ), writes=[t_dbg], wadd=True)
            S.barrier()

        o = ROFF
        acc_all = f32(o, 17 * 1024).rearrange("p (t c) -> p t c", c=1024); o += 17 * 2048
        t_accs = S.tiles(17, "acc")
        n2T = bf(o, 8 * NQ).rearrange("p (k t) -> p k t", t=NQ); o += 8 * NQ
        t_n2T = S.tile("n2T")
        F2FREE = o
        Wo = bf(o, 8192).rearrange("p (k c) -> p k c", c=1024); o += 8192
        t_Wo = S.tile("Wo")
        XB = []
        for i in range(2):
            XB.append((f32(o, 1024), S.tile("xr"))); o += 2048
        XN = []
        for i in range(2):
            XN.append((bf(o, 1024), S.tile("n2"))); o += 1024
        sqj = bf(o, 1024); o += 1024
        t_sq = S.tile("sq")
        hs = f32(o, 1024); o += 2048
        t_hs = S.tile("hs")
        SSB = []
        for i in range(4):
            SSB.append((f32(o, 1), S.tile("ss"))); o += 2
        assert o <= AR, o
        xb, xnb, ssb = Rot(XB), Rot(XN), Rot(SSB)
        for kc in range(KC):
            load_w(Wo[:, kc, :], t_Wo, w_o[kc * 128:(kc + 1) * 128, :], 1024, None)

        def out_tiles(g):
            if g < 4:
                return [(0, 2, None), (2, 128, 4 * g), (130, 128, 4 * g + 1), (258, 128, 4 * g + 2), (386, 126, 4 * g + 3)]
            return [(0, 2, None), (2, 32, 16)]

        def f1_a(g, q0, n, ai):
            xr, t_xr = xb.next()
            S.dma(t_xr, xr[:n, :], x_own[GROW[g] + 32 + q0:GROW[g] + 32 + q0 + n, :], writes=[t_xr])
            if ai is None:
                hm, t_hm = hs, t_hs
            else:
                hm, t_hm = acc_all[:, ai, :], t_accs[ai]
            for half in range(2):
                ps, t_ps = nextps()
                mm_acc(ps[:n, :], t_ps, [(mT[:, c, GOFF[g] + q0:GOFF[g] + q0 + n], Wo[:, c, half * 512:(half + 1) * 512]) for c in range(8)], t_mT + [t_Wo])
                S.op("dve", lambda e, ps=ps, hm=hm, xr=xr, n=n, half=half: e.tensor_tensor(out=hm[:n, half * 512:(half + 1) * 512], in0=ps[:n, :], in1=xr[:n, half * 512:(half + 1) * 512], op=ALU.add),
                     reads=[t_ps, t_xr], writes=[t_hm], wadd=True)
            ss, t_ss = ssb.next()
            n2, t_n2 = xnb.next()
            norm_rows(hm, t_hm, n, sqj, t_sq, ss, t_ss, n2, t_n2)
            return (g, q0, n, n2, t_n2)

        def f1_b(g, q0, n, n2, t_n2):
            transpose_rows(n2, t_n2, n, n2T, t_n2T, GOFF[g] + q0, evac="act")

        prev = None
        for g in range(5):
            for (q0, n, ai) in out_tiles(g):
                cur = f1_a(g, q0, n, ai)
                if prev is not None:
                    f1_b(*prev)
                prev = cur
        f1_b(*prev)
        S.barrier()
        if DEBUG:
            S.dma(t_dbg, dbg_acc, f32(ROFF, 17 * 1024), writes=[t_dbg], wadd=True)
            S.dma(t_dbg, dbg_n2t, bf(ROFF + 17 * 2048, 8 * NQ), writes=[t_dbg], wadd=True)
            S.barrier()

        o = F2FREE
        WB = []
        for i, base in enumerate((MOFF, o)):
            WB.append(dict(Wup=bf(base, 8192).rearrange("p (k c) -> p k c", c=1024), Wdn=bf(base + 8192, 4096).rearrange("p (k c) -> p k c", c=1024),
                           t_Wup=S.tile("Wup"), t_Wdn=S.tile("Wdn")))
        o += 12288
        HH = []
        for i in range(2):
            HH.append((bf(o, 2048).rearrange("p (k t) -> p k t", t=512), S.tiles(4, "hh"))); o += 2048
        TG = []
        for i in range(2):
            TG.append((f32(o, 512), S.tile("tg"))); o += 1024
        TV = []
        for i in range(2):
            TV.append((f32(o, 512), S.tile("tv"))); o += 1024
        nfw = prow[:, PR_NF:PR_NF + 1024]
        OB = []
        for i in range(2):
            OB.append((f32(o, 1024), S.tile("ob"))); o += 2048
        sqj = bf(o, 1024); o += 1024
        t_sq = S.tile("sq")
        SSB = []
        for i in range(4):
            SSB.append((f32(o, 1), S.tile("ss"))); o += 2
        stg3 = (f32(o, 1024), S.tile("stg")); o += 2048
        assert o <= AR, o
        hhb, tgb, tvb, obb, ssb = Rot(HH), Rot(TG), Rot(TV), Rot(OB), Rot(SSB)
        stg.items = STG + [stg3]
        t_out = S.tile("outd")
        def f2_group(pi, f0, nf, g, wb):
            Wup, Wdn, t_Wup, t_Wdn = wb["Wup"], wb["Wdn"], wb["t_Wup"], wb["t_Wdn"]
            nq = GQ[g]
            nn = nq - 2
            hh, t_hh = hhb.next()
            for fl in range(nf):
                fc = f0 + fl
                res = []
                for which in range(2):
                    ps, t_ps = nextps()
                    mm_acc(ps[:, :nq], t_ps, [(Wup[:, kc, which * 512 + fl * 128:which * 512 + (fl + 1) * 128], n2T[:, kc, GOFF[g]:GOFF[g] + nq]) for kc in range(KC)], [t_n2T, t_Wup])
                    ch = which * 22 + fc
                    tb, t_tb = (tgb if which == 0 else tvb).next()
                    wcol = PC_FDW + ch * 3
                    S.op("act", lambda e, ps=ps, tb=tb, wcol=wcol, ch=ch, nn=nn: e.activation(out=tb[:, :nn], in_=ps[:, 0:nn], func=AF.Identity, scale=pcol[:, wcol:wcol + 1], bias=pcol[:, PC_FDB + ch:PC_FDB + ch + 1]),
                         reads=[t_ps, t_const], writes=[t_tb])
                    for k in (1, 2):
                        S.op("dve", lambda e, ps=ps, tb=tb, wcol=wcol, k=k, nn=nn: e.scalar_tensor_tensor(out=tb[:, :nn], in0=ps[:, k:k + nn], scalar=pcol[:, wcol + k:wcol + k + 1], in1=tb[:, :nn], op0=ALU.mult, op1=ALU.add),
                             reads=[t_ps, t_tb, t_const], writes=[t_tb])
                    res.append((tb, t_tb))
                (tg, t_tg), (tv, t_tv) = res
                S.op("act", lambda e, tg=tg, nn=nn: e.activation(out=tg[:, :nn], in_=tg[:, :nn], func=AF.Silu), reads=[t_tg], writes=[t_tg])
                S.op("dve", lambda e, tg=tg, tv=tv, hh=hh, fl=fl, nn=nn: e.tensor_tensor(out=hh[:, fl, :nn], in0=tg[:, :nn], in1=tv[:, :nn], op=ALU.mult),
                     reads=[t_tg, t_tv], writes=[t_hh[fl]])
            return hh, t_hh

        def f2_down(pi, f0, nf, g, wb, hh, t_hh):
            Wup, Wdn, t_Wup, t_Wdn = wb["Wup"], wb["Wdn"], wb["t_Wup"], wb["t_Wdn"]
            nq = GQ[g]
            for (q0, n, ai) in out_tiles(g):
                if ai is None:
                    continue
                for half in range(2):
                    ps, t_ps = nextps()
                    mm_acc(ps[:n, :], t_ps, [(hh[:, fl, q0 - 2:q0 - 2 + n], Wdn[:, fl, half * 512:(half + 1) * 512]) for fl in range(nf)], t_hh[:nf] + [t_Wdn])
                    S.op("dve", lambda e, ps=ps, ai=ai, n=n, half=half: e.tensor_tensor(out=acc_all[:n, ai, half * 512:(half + 1) * 512], in0=ps[:n, :], in1=acc_all[:n, ai, half * 512:(half + 1) * 512], op=ALU.add),
                         reads=[t_ps, t_accs[ai]], writes=[t_accs[ai]], wadd=True)
                if pi == NPASS - 1:
                    ss, t_ss = ssb.next()
                    ob, t_ob = obb.next()
                    S.op("act", lambda e, ai=ai, n=n, ss=ss: e.activation(out=sqj[:n, :], in_=acc_all[:n, ai, :], func=AF.Square, accum_out=ss[:n, :]),
                         reads=[t_accs[ai]], writes=[t_sq, t_ss])
                    S.op("act", lambda e, ss=ss, n=n: e.activation(out=ss[:n, :], in_=ss[:n, :], func=AF.Sqrt, bias=1e-6, scale=1.0 / 1024), reads=[t_ss], writes=[t_ss])
                    S.op("dve", lambda e, ss=ss, n=n: e.reciprocal(out=ss[:n, :], in_=ss[:n, :]), reads=[t_ss], writes=[t_ss])
                    S.op("dve", lambda e, ai=ai, n=n, ss=ss, ob=ob: e.scalar_tensor_tensor(out=ob[:n, :], in0=acc_all[:n, ai, :], scalar=ss[:n, 0:1], in1=nfw[:n, :], op0=ALU.mult, op1=ALU.mult),
                         reads=[t_accs[ai], t_ss, t_const], writes=[t_ob])
                    orow = 510 * g + (q0 - 2)
                    S.dma(t_ob, out_d[orow:orow + n, :], ob[:n, :], reads=[t_ob], writes=[t_out], wadd=True)

        passes = [(0, 4), (4, 4), (8, 4), (12, 4), (16, 4), (20, 2)]
        NPASS = len(passes)

        def pass_loads(pi):
            f0, nf = passes[pi]
            wb = WB[pi % 2]
            res = []
            for kc in range(KC):
                nm = pcol[:, PC_NFFN + kc:PC_NFFN + kc + 1]
                res.append((wb["Wup"][:, kc, 0:nf * 128], wb["t_Wup"], w_up[kc * 128:(kc + 1) * 128, f0 * 128:(f0 + nf) * 128], nf * 128, nm))
                res.append((wb["Wup"][:, kc, 512:512 + nf * 128], wb["t_Wup"], w_up[kc * 128:(kc + 1) * 128, 2816 + f0 * 128:2816 + (f0 + nf) * 128], nf * 128, nm))
            for fl in range(nf):
                res.append((wb["Wdn"][:, fl, :], wb["t_Wdn"], w_dn[(f0 + fl) * 128:(f0 + fl + 1) * 128, :], 1024, None))
            return res

        for a in pass_loads(0):
            load_w(*a)
        for pi, (f0, nf) in enumerate(passes):
            pend = pass_loads(pi + 1) if pi + 1 < NPASS else []
            per = (len(pend) + 3) // 4
            prev = None
            for g in range(5):
                cur = (g,) + f2_group(pi, f0, nf, g, WB[pi % 2])
                if prev is not None:
                    f2_down(pi, f0, nf, prev[0], WB[pi % 2], prev[1], prev[2])
                prev = cur
                for _ in range(per):
                    if pend:
                        load_w(*pend.pop(0))
            f2_down(pi, f0, nf, prev[0], WB[pi % 2], prev[1], prev[2])
            while pend:
                load_w(*pend.pop(0))
        S.barrier(final=True)
        S.emit()
    nc._sched_stats = S.stats
    return nc


def _host_inputs(inputs):
    f = np.float32
    x = np.asarray(inputs["x"], f)
    meta = np.asarray(inputs["meta_tokens"], f)
    B = x.shape[0]
    pcol = np.zeros((128, NPC), f)

    def colmaj(v, nchunk):
        return np.ascontiguousarray(np.asarray(v, f).reshape(nchunk, 128).T)

    pcol[:, PC_NMIX:PC_NMIX + 8] = colmaj(inputs["norm_mix_w"][0], 8)
    pcol[:, PC_NFFN:PC_NFFN + 8] = colmaj(inputs["norm_ffn_w"][0], 8)
    cdw = np.asarray(inputs["conv_dw_w"][0], f)
    pcol[:, PC_CDW:PC_CDW + 248] = cdw.T.reshape(8, 128, 31).transpose(1, 0, 2).reshape(128, 248)
    pcol[:, PC_CDB:PC_CDB + 8] = colmaj(inputs["conv_dw_b"][0], 8)
    pcol[:, PC_LNW:PC_LNW + 8] = colmaj(inputs["conv_ln_w"][0], 8)
    pcol[:, PC_LNB:PC_LNB + 8] = colmaj(inputs["conv_ln_b"][0], 8)
    fdw = np.asarray(inputs["ffn_dw_w"][0], f)
    pcol[:, PC_FDW:PC_FDW + 132] = fdw.T.reshape(44, 128, 3).transpose(1, 0, 2).reshape(128, 132)
    pcol[:, PC_FDB:PC_FDB + 44] = colmaj(inputs["ffn_dw_b"][0], 44)
    pcol[:, PC_SUB] = np.asarray(inputs["subln_w"][0], f)
    prow = np.zeros((128, NPR), f)
    prow[:, PR_NF:PR_NF + 1024] = np.asarray(inputs["norm_final_w"], f)[None, :]
    prow[:, PR_SUB:PR_SUB + 128] = np.asarray(inputs["subln_w"][0], f)[None, :]
    prow[:, PR_LQ1:PR_LQ1 + 64] = np.asarray(inputs["lambda_q1"][0], f)[None, :]
    prow[:, PR_LK1:PR_LK1 + 64] = np.asarray(inputs["lambda_k1"][0], f)[None, :]
    prow[:, PR_LQ2:PR_LQ2 + 64] = np.asarray(inputs["lambda_q2"][0], f)[None, :]
    prow[:, PR_LK2:PR_LK2 + 64] = np.asarray(inputs["lambda_k2"][0], f)[None, :]
    kk = np.arange(128)[:, None]
    mm = np.arange(64)[None, :]
    prow[:, PR_CK:PR_CK + 64] = (2 * mm + (kk >= 64)).astype(f)
    ident = np.eye(128).astype(ml_dtypes.bfloat16)
    shared = dict(
        w_in=np.ascontiguousarray(np.asarray(inputs["w_in"][0], f)),
        w_co=np.ascontiguousarray(np.asarray(inputs["w_conv_out"][0], f)),
        w_o=np.ascontiguousarray(np.asarray(inputs["w_out"][0], f)),
        w_up=np.ascontiguousarray(np.asarray(inputs["w_up"][0], f)),
        w_dn=np.ascontiguousarray(np.asarray(inputs["w_down"][0], f)),
        pcol=pcol, prow=prow, ident=ident)
    in_maps = []
    for c in range(8):
        b, j = c // 4, c % 4
        xa = np.zeros((17 * 128, D), f)
        xa[0:16] = meta
        xa[128:] = x[b, 2048 * j:2048 * (j + 1)]
        seq = np.concatenate([meta, x[b]], axis=0)
        xo = np.zeros((NROWS, D), f)
        cq = np.zeros((128, 2048), ml_dtypes.bfloat16)
        for g in range(5):
            if g < 4:
                G = 4 * g + j
                p0 = 510 * G - 18
            else:
                p0 = 8142
            nr = 32 + GQ[g]
            ps = np.arange(p0, p0 + nr)
            valid = ps >= 0
            xo[GROW[g] + np.nonzero(valid)[0]] = seq[ps[valid]]
            if g < 4:
                tq = ps[32:] - 16
                cqv = np.where(tq >= 0, tq // 64, -1).astype(ml_dtypes.bfloat16)
                cq[:, g * 512:(g + 1) * 512] = cqv[None, :]
        d = dict(shared)
        d.update(x_all=xa, x_own=xo, cq=cq)
        in_maps.append(d)
    return in_maps


_NC = None


def kernel(**inputs):
    global _NC
    in_maps = _host_inputs(inputs)
    if _NC is None:
        _NC = build_program()
    res = run_bass_kernel_spmd(_NC, in_maps, core_ids=list(range(8)))
    B = 2
    out = np.zeros((B, 8192, D), np.float32)
    for c in range(8):
        b, j = c // 4, c % 4
        oo = res.results[c]["out_own"]
        for g in range(4):
            G = 4 * g + j
            out[b, 510 * G:510 * G + 510] = oo[510 * g:510 * g + 510]
        if j == 0:
            out[b, 8160:8192] = oo[2040:2072]
    return out
```
